# Optimizing a Trainium2 kernel written in Bass

```python
import math
import jax
import jax.numpy as jnp
from jax import lax
import numpy as np

D_MODEL = 2048
BATCH = 2
SEQ = 8192
DEPTH = 1

HEAD_DIM = 64
RWKV_WIDTH = D_MODEL // 2
RWKV_HEADS = RWKV_WIDTH // HEAD_DIM
SB_WIDTH = D_MODEL // 2
SB_HEADS = SB_WIDTH // HEAD_DIM
DECAY_LORA = 64
ICLR_LORA = 64
GATE_LORA = 160
D_FF = 4 * D_MODEL
Q_BLOCK = 128
N_MOD = 6
NORM_EPS = 1e-6
GN_EPS = 64e-5
RWKV_COLS = 3 * RWKV_WIDTH + DECAY_LORA + ICLR_LORA + GATE_LORA
SB_COLS = 3 * SB_WIDTH
GATE_COLS = 2 * D_MODEL
IN_COLS = RWKV_COLS + SB_COLS + GATE_COLS

kernel_name = "hybrid_rwkv7_stickbreaking_block"


def _rmsnorm(x, g):
    xf = x.astype(jnp.float32)
    y = xf * lax.rsqrt(jnp.mean(xf * xf, axis=-1, keepdims=True) + NORM_EPS)
    return (y * g).astype(x.dtype)


def _token_shift(u):
    return jnp.pad(u, ((0, 0), (1, 0), (0, 0)))[:, :-1]


def _l2norm_heads(t):
    tf = t.astype(jnp.float32)
    return tf * lax.rsqrt(jnp.maximum(jnp.sum(tf * tf, axis=-1, keepdims=True), 1e-24))


def _group_norm(y):
    mean = jnp.mean(y, axis=-1, keepdims=True)
    var = jnp.mean(jnp.square(y - mean), axis=-1, keepdims=True)
    return (y - mean) * lax.rsqrt(var + GN_EPS)


def _rwkv7_scan(r, w, k, v, kk, a):
    B, S, H, N = r.shape

    def step(state, inp):
        r_t, w_t, k_t, v_t, kk_t, a_t = inp
        s_kk = jnp.einsum('bhij,bhj->bhi', state, kk_t)
        state = (state * w_t[:, :, None, :]
                 - s_kk[..., None] * (kk_t * a_t)[:, :, None, :]
                 + v_t[..., None] * k_t[:, :, None, :])
        return state, jnp.einsum('bhij,bhj->bhi', state, r_t)

    xs = tuple(jnp.moveaxis(t.astype(jnp.float32), 1, 0) for t in (r, w, k, v, kk, a))
    s0 = jnp.zeros((B, H, N, N), jnp.float32)
    _, ys = lax.scan(step, s0, xs)
    return jnp.moveaxis(ys, 0, 1)


def _stick_breaking(q, k, v):
    B, H, S, N = q.shape
    nb = S // Q_BLOCK
    qb = q.reshape(B, H, nb, Q_BLOCK, N).transpose(2, 0, 1, 3, 4)
    key_pos = jnp.arange(S)
    scale = 1.0 / math.sqrt(N)

    def block(args):
        q_blk, i = args
        z = jnp.einsum('bhqd,bhkd->bhqk', q_blk, k,
                       preferred_element_type=jnp.float32) * scale
        q_pos = i * Q_BLOCK + jnp.arange(Q_BLOCK)
        causal = key_pos[None, :] < q_pos[:, None]
        log_keep = jnp.where(causal, jax.nn.log_sigmoid(-z), 0.0)
        later = lax.cumsum(log_keep, axis=3, reverse=True) - log_keep
        att = jnp.where(causal, jnp.exp(jax.nn.log_sigmoid(z) + later), 0.0)
        return jnp.einsum('bhqk,bhkd->bhqd', att.astype(v.dtype), v)

    out = lax.map(block, (qb, jnp.arange(nb)))
    return out.transpose(1, 2, 0, 3, 4).reshape(B, H, S, N)


def setup_inputs(seed: int = 0) -> dict:
    key = jax.random.key(seed)
    ks = jax.random.split(key, 22)
    L, D, C = DEPTH, D_MODEL, RWKV_WIDTH

    def nrm(k, shape, s):
        return jax.random.normal(k, shape, jnp.float32) * s

    return {
        "x": nrm(ks[0], (BATCH, SEQ, D), 1.0),
        "c": nrm(ks[1], (BATCH, D), 1.0),
        "w_ada": nrm(ks[2], (L, D, N_MOD * D), 0.5 * D ** -0.5),
        "b_ada": nrm(ks[3], (L, N_MOD * D), 0.05),
        "norm_g": 1.0 + nrm(ks[4], (L, 4, D), 0.05),
        "w_in": nrm(ks[5], (L, D, IN_COLS), D ** -0.5),
        "mu_shift": jax.random.uniform(ks[6], (L, RWKV_COLS), jnp.float32),
        "w0": -0.6 + nrm(ks[7], (L, C), 0.5),
        "w2": nrm(ks[8], (L, DECAY_LORA, C), 0.5 * DECAY_LORA ** -0.5),
        "a0": nrm(ks[9], (L, C), 0.5),
        "a2": nrm(ks[10], (L, ICLR_LORA, C), 0.5 * ICLR_LORA ** -0.5),
        "g2": nrm(ks[11], (L, GATE_LORA, C), GATE_LORA ** -0.5),
        "k_k": 0.85 + nrm(ks[12], (L, C), 0.05),
        "k_a": 1.0 + nrm(ks[13], (L, C), 0.05),
        "r_k": nrm(ks[14], (L, RWKV_HEADS, HEAD_DIM), 0.1),
        "ln_x_w": 1.0 + nrm(ks[15], (L, C), 0.05),
        "ln_x_b": nrm(ks[16], (L, C), 0.02),
        "w_up_rwkv": nrm(ks[17], (L, C, D), C ** -0.5),
        "w_up_sb": nrm(ks[18], (L, SB_WIDTH, D), SB_WIDTH ** -0.5),
        "w_out": nrm(ks[19], (L, D, D), D ** -0.5),
        "w_mlp_in": nrm(ks[20], (L, D, D_FF), D ** -0.5),
        "w_mlp_out": nrm(ks[21], (L, D_FF, D), D_FF ** -0.5),
    }


def reference(x, c, w_ada, b_ada, norm_g, w_in, mu_shift, w0, w2, a0, a2, g2, k_k, k_a,
              r_k, ln_x_w, ln_x_b, w_up_rwkv, w_up_sb, w_out, w_mlp_in, w_mlp_out):
    B, S, D = x.shape
    H, N = RWKV_HEADS, HEAD_DIM
    C = RWKV_WIDTH
    rwkv_splits = (C, 2 * C, 3 * C, 3 * C + DECAY_LORA, 3 * C + DECAY_LORA + ICLR_LORA)

    def heads(t):
        return t.reshape(B, S, H, N)

    for l in range(DEPTH):
        mod = jax.nn.silu(c) @ w_ada[l] + b_ada[l]
        shift_m, scale_m, gate_m, shift_f, scale_f, gate_f = jnp.split(mod[:, None, :], N_MOD, axis=-1)

        h = _rmsnorm(x, norm_g[l, 0]) * (1.0 + scale_m) + shift_m
        proj = h @ w_in[l]
        p_rwkv, p_sb, p_gate = jnp.split(proj, (RWKV_COLS, RWKV_COLS + SB_COLS), axis=-1)

        p_rwkv = p_rwkv + (_token_shift(p_rwkv) - p_rwkv) * mu_shift[l]
        r, k, v, dw, da, dg = jnp.split(p_rwkv, rwkv_splits, axis=-1)
        w_log = -jax.nn.softplus(-(w0[l] + jnp.tanh(dw) @ w2[l])) - 0.5
        decay = jnp.exp(-jnp.exp(w_log.astype(jnp.float32)))
        a = jax.nn.sigmoid(a0[l] + da @ a2[l])
        g = jax.nn.sigmoid(dg) @ g2[l]
        kk = _l2norm_heads(heads(k * k_k[l]))
        k = k * (1.0 + (a - 1.0) * k_a[l])
        r_h, k_h, v_h = heads(r), heads(k), heads(v)
        y = _rwkv7_scan(r_h, heads(decay), k_h, v_h, kk, heads(a))
        y = _group_norm(y).reshape(B, S, C) * ln_x_w[l] + ln_x_b[l]
        bonus = jnp.sum(r_h * k_h * r_k[l], axis=-1, keepdims=True) * v_h
        y_a = ((y + bonus.reshape(B, S, C)) * g).astype(x.dtype)

        q_s, k_s, v_s = [t.reshape(B, S, SB_HEADS, HEAD_DIM).transpose(0, 2, 1, 3)
                         for t in jnp.split(p_sb, 3, axis=-1)]
        y_b = _stick_breaking(q_s, k_s, v_s).transpose(0, 2, 1, 3).reshape(B, S, SB_WIDTH)

        gate_a, gate_b = jnp.split(p_gate, 2, axis=-1)
        merged = (jax.nn.sigmoid(gate_a) * (y_a @ w_up_rwkv[l])
                  + jax.nn.sigmoid(gate_b) * (y_b @ w_up_sb[l]))
        mix = merged @ w_out[l]
        x = x + gate_m * _rmsnorm(mix, norm_g[l, 1])

        hf = _rmsnorm(x, norm_g[l, 2]) * (1.0 + scale_f) + shift_f
        f = jnp.square(jax.nn.relu(hf @ w_mlp_in[l])) @ w_mlp_out[l]
        x = x + gate_f * _rmsnorm(f, norm_g[l, 3])
    return x
```

```python
import contextlib
import numpy as np
import ml_dtypes
import concourse.bass as bass
import concourse.mybir as mybir
from concourse.bass_utils import run_bass_kernel_spmd

F32 = mybir.dt.float32
BF16 = mybir.dt.bfloat16
AF = mybir.ActivationFunctionType
ALU = mybir.AluOpType

D = 2048
S = 8192
NB = 2
C = 1024
DFF = 8192
NCORES = 8
NORM_EPS = 1e-6
GN_EPS = 64e-5
SEM_LIMIT = 30000
STOP_AT = 0


class Dep:
    __slots__ = ("name", "lw", "rd", "dsem", "dcnt")

    def __init__(self, name=""):
        self.name = name
        self.lw = []
        self.rd = {}
        self.dsem = None
        self.dcnt = 0


class Eng:
    def __init__(self, K, name, h):
        self.K = K
        self.name = name
        self.h = h
        self.sem = None
        self.cnt = 0
        self.seen = {}
        self.nsem = 0

    def new_sem(self):
        self.sem = self.K.es.enter_context(self.K.nc.semaphore(f"e_{self.name}_{self.nsem}"))
        self.nsem += 1
        self.K.nsems += 1
        self.cnt = 0

    def wait_for(self, ev):
        sem, val, _ = ev
        key = id(sem)
        if self.seen.get(key, 0) < val:
            self.h.wait_ge(sem, val)
            self.seen[key] = val
            self.K.nwaits += 1


class Buf:
    def __init__(self, t, name, nparts=1):
        self.t = t
        self.name = name
        self.d = [Dep(f"{name}.{i}") for i in range(nparts)]

    def __getitem__(self, k):
        return self.t[k]

    def view(self, ap, name):
        b = Buf(ap, name, 1)
        return b

    @property
    def all(self):
        return list(self.d)


class Ctx:
    def __init__(self, nc):
        self.nc = nc
        self.es = contextlib.ExitStack()
        self.sem_es = self.es
        self.scopes = []
        self.nsems = 0
        self.nwaits = 0
        self.ninst = 0
        self.engs = {
            "pe": Eng(self, "pe", nc.tensor),
            "act": Eng(self, "act", nc.scalar),
            "dve": Eng(self, "dve", nc.vector),
            "pool": Eng(self, "pool", nc.gpsimd),
            "sp": Eng(self, "sp", nc.sync),
        }
        for e in self.engs.values():
            e.new_sem()
        self.out_events = []
        self.scratch_events = []

    def sbuf(self, name, shape, dt, nparts=1):
        es = self.scopes[-1][0] if self.scopes else self.es
        t = es.enter_context(self.nc.sbuf_tensor(name, list(shape), dt))
        b = Buf(t, name, nparts)
        if self.scopes:
            self.scopes[-1][1].append(b)
        return b

    @contextlib.contextmanager
    def scope(self):
        es = contextlib.ExitStack()
        self.scopes.append((es, []))
        try:
            yield
        finally:
            _, bufs = self.scopes.pop()
            deps = [d for b in bufs for d in b.d]
            self.barrier(deps)
            es.close()

    def psum(self, name, shape, dt=F32, nparts=1):
        es = self.scopes[-1][0] if self.scopes else self.es
        t = es.enter_context(self.nc.psum_tensor(name, list(shape), dt))
        b = Buf(t, name, nparts)
        if self.scopes:
            self.scopes[-1][1].append(b)
        return b

    def dram(self, name, shape, dt, kind="Internal", nparts=1):
        t = self.nc.dram_tensor(name, list(shape), dt, kind=kind)
        return Buf(t.ap(), name, nparts)

    def _deps(self, E, r, w, same_raw=True):
        for d in r:
            for ev in d.lw:
                if ev[2] == E.name and not same_raw:
                    continue
                E.wait_for(ev)
        for d in w:
            for ev in d.lw:
                if ev[2] == E.name and not same_raw:
                    continue
                E.wait_for(ev)
            for ev in d.rd.values():
                if ev[2] == E.name and not same_raw:
                    continue
                E.wait_for(ev)

    def op(self, eng, fn, r=(), w=()):
        E = self.engs[eng]
        if E.cnt >= SEM_LIMIT:
            E.new_sem()
        self._deps(E, r, w, same_raw=(eng != "pe"))
        inst = fn(E.h)
        E.cnt += 1
        inst.then_inc(E.sem, 1)
        self.ninst += 1
        ev = (E.sem, E.cnt, E.name)
        for d in r:
            d.rd[E.name] = ev
        for d in w:
            d.lw = [ev]
            d.rd = {}
        return ev

    def dma(self, queue, out, in_, r=(), w=(), sem_dep=None, is_output=False, waw=True, **kw):
        E = self.engs[queue]
        if waw:
            self._deps(E, r, w, same_raw=True)
        else:
            self._deps(E, r, (), same_raw=True)
            for d in w:
                for ev in d.rd.values():
                    E.wait_for(ev)
        d0 = sem_dep or (w[0] if len(w) else r[0])
        if d0.dsem is None or d0.dcnt >= SEM_LIMIT:
            d0.dsem = self.es.enter_context(self.nc.semaphore(f"d{self.nsems}"))
            self.nsems += 1
            d0.dcnt = 0
        inst = E.h.dma_start(out=out, in_=in_, **kw)
        d0.dcnt += 16
        inst.then_inc(d0.dsem, 16)
        self.ninst += 1
        ev = (d0.dsem, d0.dcnt, "dma" + str(id(d0)))
        for d in r:
            d.rd[ev[2]] = ev
        for d in w:
            d.lw = [ev]
            d.rd = {}
        if is_output:
            self.out_events.append(ev)
        return ev

    def barrier(self, deps=()):
        evs = []
        for E in self.engs.values():
            if E.cnt > 0:
                evs.append((E.sem, E.cnt, E.name))
        for d in deps:
            evs += d.lw
            evs += list(d.rd.values())
        for E in self.engs.values():
            for ev in evs:
                if ev[2] != E.name:
                    E.wait_for(ev)

    def finish(self):
        E = self.engs["sp"]
        for ev in self.out_events:
            E.wait_for(ev)
        for name, e2 in self.engs.items():
            if name != "sp" and e2.cnt > 0:
                E.wait_for((e2.sem, e2.cnt, name))


def mm(K, ps, ps_ap, lhsT, rhs, start, stop, r=(), w=None):
    wd = w if w is not None else ps.all
    return K.op("pe", lambda e: e.matmul(ps_ap, lhsT=lhsT, rhs=rhs, start=start, stop=stop), r=r, w=wd)


def load_w_cast(K, dst_buf, dst_ap, src_ap, w=None):
    return K.dma("pool", dst_ap, src_ap, w=(w if w is not None else dst_buf.all), waw=False)


class Common:
    pass


def setup_common(K, io, need_f=True):
    nc = K.nc
    cm = Common()
    cm.ones_bf = K.sbuf("ones_bf", [128, 128], BF16)
    K.op("dve", lambda e: e.memset(cm.ones_bf[:], 1.0), w=cm.ones_bf.all)
    cT = K.sbuf("cT_sb", [128, 16], F32)
    K.dma("sp", cT[:], io["cT"], w=cT.all)
    sc = K.sbuf("sc", [128, 16], BF16)
    K.op("act", lambda e: e.activation(out=sc[:], in_=cT[:], func=AF.Silu), r=cT.all, w=sc.all)
    bada = K.sbuf("bada", [128, 96], F32)
    K.dma("sp", bada[:], io["b_adaT"], w=bada.all)
    ng = K.sbuf("ng", [128, 64], F32)
    K.dma("sp", ng[:], io["norm_gT"], w=ng.all)
    nmod = 6 if need_f else 2
    modT = K.sbuf("modT", [128, 96], F32)
    cm.A1 = K.sbuf("A1", [128, 16], F32)
    if need_f:
        cm.Cm = K.sbuf("Cm", [128, 16], F32)
        cm.A2 = K.sbuf("A2", [128, 16], F32)
        cm.Cf = K.sbuf("Cf", [128, 16], F32)
    with K.scope():
        _setup_common_body(K, io, cm, need_f, nmod, modT, sc, bada, ng)
    cm.modT = modT
    cm.B1 = modT
    return cm


def _setup_common_body(K, io, cm, need_f, nmod, modT, sc, bada, ng):
    wst = [K.sbuf(f"wada{i}", [128, 16, 512], BF16) for i in range(2)]
    psm = K.psum("ps_mod", [128, 512], F32)
    w_ada = io["w_ada"].rearrange("(kc p) n -> p kc n", p=128)
    ngrp = nmod * 4
    for g in range(ngrp):
        wt = wst[g % 2]
        load_w_cast(K, wt, wt[:], w_ada[:, :, g * 512:(g + 1) * 512])
        for j in range(4):
            col = g * 4 + j
            for kc in range(16):
                mm(K, psm, psm[:, col:col + 1], wt[:, kc, j * 128:(j + 1) * 128], sc[:, kc:kc + 1],
                   start=(kc == 0), stop=(kc == 15), r=wt.all + sc.all)
    ncol = ngrp * 4
    K.op("dve", lambda e: e.tensor_tensor(out=modT[:, 0:ncol], in0=psm[:, 0:ncol], in1=bada[:, 0:ncol], op=ALU.add),
         r=psm.all + bada.all, w=modT.all)
    K.op("dve", lambda e: e.scalar_tensor_tensor(out=cm.A1[:], in0=modT[:, 16:32], scalar=1.0, in1=ng[:, 0:16],
                                                  op0=ALU.add, op1=ALU.mult), r=modT.all + ng.all, w=cm.A1.all)
    if need_f:
        K.op("dve", lambda e: e.tensor_tensor(out=cm.Cm[:], in0=modT[:, 32:48], in1=ng[:, 16:32], op=ALU.mult),
             r=modT.all + ng.all, w=cm.Cm.all)
        K.op("dve", lambda e: e.scalar_tensor_tensor(out=cm.A2[:], in0=modT[:, 64:80], scalar=1.0, in1=ng[:, 32:48],
                                                      op0=ALU.add, op1=ALU.mult), r=modT.all + ng.all, w=cm.A2.all)
        K.op("dve", lambda e: e.tensor_tensor(out=cm.Cf[:], in0=modT[:, 80:96], in1=ng[:, 48:64], op=ALU.mult),
             r=modT.all + ng.all, w=cm.Cf.all)


def rstd_from_ss(K, ss_ps, sq_t, rstd_t, n):
    K.op("act", lambda e: e.activation(out=sq_t[:], in_=ss_ps[:], func=AF.Sqrt, scale=1.0 / n, bias=NORM_EPS),
         r=ss_ps.all, w=sq_t.all)
    K.op("dve", lambda e: e.reciprocal(out=rstd_t[:], in_=sq_t[:]), r=sq_t.all, w=rstd_t.all)


def phase2(K, io, cm, n_pass=2):
    nc = K.nc
    T = 1024
    ones = cm.ones_bf
    R1 = K.sbuf("R1", [128, 16, T], BF16, nparts=32)
    R2 = K.sbuf("R2", [128, 16, T], BF16, nparts=32)
    Fb = K.sbuf("Fb", [128, 16, T], F32, nparts=16)
    merged_v = Fb.t[:, 0:8, :].bitcast(BF16)
    WA = [K.sbuf(f"WA{i}", [128, 16 * 512], BF16) for i in range(2)]
    WB = [K.sbuf(f"WB{i}", [128, 16 * 256], BF16) for i in range(2)]
    xs = [K.sbuf(f"xs{i}", [128, 512], F32) for i in range(3)]
    sqb = [K.sbuf(f"sqb{i}", [128, 512], BF16) for i in range(2)]
    tt = [K.sbuf(f"tt{i}", [128, 512], F32) for i in range(2)]
    sg = [K.sbuf(f"sg{i}", [128, 512], F32) for i in range(2)]
    sqrt_t = K.sbuf("sqrt_t", [128, 512], F32)
    rstd = K.sbuf("rstd", [128, 512], F32)
    PS = [K.psum(f"ps{i}", [128, 512], F32) for i in range(8)]

    def merged_ap(m, half):
        off = (m % 2) * T + half * 512
        return merged_v[:, m // 2, off:off + 512]

    def merged_dep(m):
        return [Fb.d[m // 2]]

    def mix_ap(m):
        off = (m % 2) * 512
        return Fb.t[:, 8 + m // 2, off:off + 512]

    def mix_dep(m):
        return [Fb.d[8 + m // 2]]

    wg = io["w_gate"].rearrange("(kc p) n -> p kc n", p=128)
    wua = io["w_up_rwkv"].rearrange("(kc p) n -> p kc n", p=128)
    wub = io["w_up_sb"].rearrange("(kc p) n -> p kc n", p=128)
    wo = io["w_out"].rearrange("(kc p) n -> p kc n", p=128)
    w1 = io["w_mlp_in"].rearrange("(kc p) n -> p kc n", p=128)
    w2 = io["w_mlp_out"].rearrange("(kc p) n -> p kc n", p=128)
    xT = io["x2T"].rearrange("(c p) t -> p c t", p=128)
    yT = io["yT"].rearrange("(c p) t -> p c t", p=128)
    outT = io["outT"].rearrange("(c p) t -> p c t", p=128)
    x1s = io["x1s"].rearrange("(c p) t -> p c t", p=128)
    x1dep = [Dep(f"x1s{i}") for i in range(4)]

    cnt = {"xs": 0, "sq": 0, "tt": 0, "sg": 0, "wa": 0, "wb": 0, "ps": 0}

    def nxt(key, lst):
        b = lst[cnt[key] % len(lst)]
        cnt[key] += 1
        return b

    def stats_norm(src_ap_fn, src_dep_fn, ss):
        for m in range(16):
            sq = nxt("sq", sqb)
            K.op("act", lambda e: e.activation(out=sq[:], in_=src_ap_fn(m), func=AF.Square),
                 r=src_dep_fn(m), w=sq.all)
            mm(K, ss, ss[:], ones[:], sq[:], start=(m == 0), stop=(m == 15), r=sq.all + ones.all)
        rstd_from_ss(K, ss, sqrt_t, rstd, float(D))

    for p in range(n_pass):
        t0 = p * T
        for c in range(16):
            evl = K.dma("sp", R2[:, c, :], yT[:, c, t0:t0 + T], w=[R2.d[2 * c], R2.d[2 * c + 1]], sem_dep=R2.d[0], waw=False)
        for d_ in R2.d:
            d_.lw = [evl]
        for half in range(2):
            tk = t0 + half * 512
            ss = PS[7]
            xst = Fb.t[:, 0:8, :].rearrange("p a (b t) -> p (a b) t", t=512)
            for c in range(16):
                K.dma("sp", xst[:, c, :], xT[:, c, tk:tk + 512], w=[Fb.d[c // 2]], sem_dep=Fb.d[c // 2], waw=False)
            stats_norm(lambda m: xst[:, m, :], lambda m: [Fb.d[m // 2]], ss)
            for c in range(16):
                t_ = nxt("tt", tt)
                K.op("dve", lambda e: e.tensor_tensor(out=t_[:], in0=xst[:, c, :], in1=rstd[:], op=ALU.mult),
                     r=[Fb.d[c // 2]] + rstd.all, w=t_.all)
                K.op("act", lambda e: e.activation(out=R1[:, c, half * 512:(half + 1) * 512], in_=t_[:],
                                                    func=AF.Identity, scale=cm.A1[:, c:c + 1],
                                                    bias=cm.B1[:, c:c + 1]),
                     r=t_.all + cm.A1.all + cm.B1.all, w=[R1.d[2 * c + half]])
        for jg in range(8):
            wa = nxt("wa", WA)
            wb = nxt("wb", WB)
            wav = wa.t[:, :].rearrange("p (a n) -> p a n", n=256)
            wbv = wb.t[:, :].rearrange("p (a n) -> p a n", n=256)
            c0 = jg * 256
            load_w_cast(K, wa, wav[:, 0:16, :], wg[:, :, c0:c0 + 256])
            load_w_cast(K, wa, wav[:, 16:32, :], wg[:, :, 2048 + c0:2048 + c0 + 256])
            load_w_cast(K, wb, wbv[:, 0:8, :], wua[:, :, c0:c0 + 256])
            load_w_cast(K, wb, wbv[:, 8:16, :], wub[:, :, c0:c0 + 256])
            for jj in range(2):
                j = jg * 2 + jj
                for half in range(2):
                    hs = slice(half * 512, (half + 1) * 512)
                    base = (cnt["ps"] % 2) * 4
                    cnt["ps"] += 1
                    pga, pgb, pua, pub = PS[base], PS[base + 1], PS[base + 2], PS[base + 3]
                    for kc in range(16):
                        mm(K, pga, pga[:], wav[:, kc, jj * 128:(jj + 1) * 128], R1[:, kc, hs],
                           start=(kc == 0), stop=(kc == 15), r=wa.all + [R1.d[2 * kc + half]])
                    for kc in range(16):
                        mm(K, pgb, pgb[:], wav[:, 16 + kc, jj * 128:(jj + 1) * 128], R1[:, kc, hs],
                           start=(kc == 0), stop=(kc == 15), r=wa.all + [R1.d[2 * kc + half]])
                    for kc in range(8):
                        mm(K, pua, pua[:], wbv[:, kc, jj * 128:(jj + 1) * 128], R2[:, kc, hs],
                           start=(kc == 0), stop=(kc == 7), r=wb.all + [R2.d[2 * kc + half]])
                    for kc in range(8):
                        mm(K, pub, pub[:], wbv[:, 8 + kc, jj * 128:(jj + 1) * 128], R2[:, 8 + kc, hs],
                           start=(kc == 0), stop=(kc == 7), r=wb.all + [R2.d[2 * (8 + kc) + half]])
                    sa = nxt("sg", sg)
                    sb_ = nxt("sg", sg)
                    K.op("act", lambda e: e.activation(out=sa[:], in_=pga[:], func=AF.Sigmoid), r=pga.all, w=sa.all)
                    K.op("act", lambda e: e.activation(out=sb_[:], in_=pgb[:], func=AF.Sigmoid), r=pgb.all, w=sb_.all)
                    K.op("dve", lambda e: e.tensor_tensor(out=sa[:], in0=sa[:], in1=pua[:], op=ALU.mult),
                         r=sa.all + pua.all, w=sa.all)
                    K.op("dve", lambda e: e.tensor_tensor(out=sb_[:], in0=sb_[:], in1=pub[:], op=ALU.mult),
                         r=sb_.all + pub.all, w=sb_.all)
                    K.op("dve", lambda e: e.tensor_tensor(out=merged_ap(j, half), in0=sa[:], in1=sb_[:], op=ALU.add),
                         r=sa.all + sb_.all, w=merged_dep(j))
        for half in range(2):
            tk = t0 + half * 512
            ss = PS[7]
            for mg in range(4):
                wa = nxt("wa", WA)
                wav = wa.t[:, :].rearrange("p (a n) -> p a n", n=512)
                load_w_cast(K, wa, wav[:, :, :], wo[:, :, mg * 512:(mg + 1) * 512])
                for mj in range(4):
                    m = mg * 4 + mj
                    pm = PS[cnt["ps"] % 4]
                    cnt["ps"] += 1
                    for kc in range(16):
                        mm(K, pm, pm[:], wav[:, kc, mj * 128:(mj + 1) * 128], merged_ap(kc, half),
                           start=(kc == 0), stop=(kc == 15), r=wa.all + merged_dep(kc))
                    K.op("act", lambda e: e.activation(out=mix_ap(m), in_=pm[:], func=AF.Identity),
                         r=pm.all, w=mix_dep(m))
                    sq = nxt("sq", sqb)
                    K.op("act", lambda e: e.activation(out=sq[:], in_=pm[:], func=AF.Square), r=pm.all, w=sq.all)
                    mm(K, ss, ss[:], ones[:], sq[:], start=(m == 0), stop=(m == 15), r=sq.all + ones.all)
            rstd_from_ss(K, ss, sqrt_t, rstd, float(D))
            ss2 = PS[6]
            for m in range(16):
                xb = nxt("xs", xs)
                K.dma("sp", xb[:], xT[:, m, tk:tk + 512], w=xb.all)
                t_ = nxt("tt", tt)
                K.op("dve", lambda e: e.tensor_tensor(out=t_[:], in0=mix_ap(m), in1=rstd[:], op=ALU.mult),
                     r=mix_dep(m) + rstd.all, w=t_.all)
                K.op("dve", lambda e: e.scalar_tensor_tensor(out=mix_ap(m), in0=t_[:], scalar=cm.Cm[:, m:m + 1],
                                                              in1=xb[:], op0=ALU.mult, op1=ALU.add),
                     r=t_.all + xb.all + cm.Cm.all, w=mix_dep(m))
                ev_ = K.dma("sp", x1s[:, m, tk:tk + 512], mix_ap(m), r=mix_dep(m), sem_dep=mix_dep(m)[0])
                if m == 0:
                    x1dep[p * 2 + half].lw = []
                x1dep[p * 2 + half].lw.append(ev_)
                sq = nxt("sq", sqb)
                K.op("act", lambda e: e.activation(out=sq[:], in_=mix_ap(m), func=AF.Square), r=mix_dep(m), w=sq.all)
                mm(K, ss2, ss2[:], ones[:], sq[:], start=(m == 0), stop=(m == 15), r=sq.all + ones.all)
            rstd_from_ss(K, ss2, sqrt_t, rstd, float(D))
            for m in range(16):
                t_ = nxt("tt", tt)
                K.op("dve", lambda e: e.tensor_tensor(out=t_[:], in0=mix_ap(m), in1=rstd[:], op=ALU.mult),
                     r=mix_dep(m) + rstd.all, w=t_.all)
                K.op("act", lambda e: e.activation(out=R1[:, m, half * 512:(half + 1) * 512], in_=t_[:],
                                                    func=AF.Identity, scale=cm.A2[:, m:m + 1],
                                                    bias=cm.modT[:, 48 + m:49 + m]),
                     r=t_.all + cm.A2.all + cm.modT.all, w=[R1.d[2 * m + half]])
        for G in range(4):
            for kg in range(4):
                wa = nxt("wa", WA)
                wav = wa.t[:, :].rearrange("p (a n) -> p a n", n=512)
                f0 = G * 2048 + kg * 512
                load_w_cast(K, wa, wav[:, :, :], w1[:, :, f0:f0 + 512])
                for kj in range(4):
                    k = kg * 4 + kj
                    for half in range(2):
                        hs = slice(half * 512, (half + 1) * 512)
                        pa = PS[cnt["ps"] % 4]
                        cnt["ps"] += 1
                        for kc in range(16):
                            mm(K, pa, pa[:], wav[:, kc, kj * 128:(kj + 1) * 128], R1[:, kc, hs],
                               start=(kc == 0), stop=(kc == 15), r=wa.all + [R1.d[2 * kc + half]])
                        r_ = nxt("sg", sg)
                        K.op("act", lambda e: e.activation(out=r_[:], in_=pa[:], func=AF.Relu), r=pa.all, w=r_.all)
                        K.op("dve", lambda e: e.tensor_tensor(out=R2[:, k, hs], in0=r_[:], in1=r_[:], op=ALU.mult),
                             r=r_.all, w=[R2.d[2 * k + half]])
            for mp in range(8):
                wb = nxt("wb", WB)
                wbv = wb.t[:, :].rearrange("p (a n) -> p a n", n=256)
                load_w_cast(K, wb, wbv[:, :, :], w2[:, G * 16:(G + 1) * 16, mp * 256:(mp + 1) * 256])
                for mj in range(2):
                    m = mp * 2 + mj
                    for half in range(2):
                        hs = slice(half * 512, (half + 1) * 512)
                        pf = PS[4 + cnt["ps"] % 2]
                        cnt["ps"] += 1
                        for k in range(16):
                            mm(K, pf, pf[:], wbv[:, k, mj * 128:(mj + 1) * 128], R2[:, k, hs],
                               start=(k == 0), stop=(k == 15), r=wb.all + [R2.d[2 * k + half]])
                        if G == 0:
                            K.op("act", lambda e: e.activation(out=Fb.t[:, m, hs], in_=pf[:], func=AF.Identity),
                                 r=pf.all, w=[Fb.d[m]])
                        else:
                            K.op("dve", lambda e: e.tensor_tensor(out=Fb.t[:, m, hs], in0=Fb.t[:, m, hs], in1=pf[:],
                                                                  op=ALU.add), r=pf.all + [Fb.d[m]], w=[Fb.d[m]])
        for half in range(2):
            tk = t0 + half * 512
            hs = slice(half * 512, (half + 1) * 512)
            ss = PS[7]
            stats_norm(lambda m: Fb.t[:, m, hs], lambda m: [Fb.d[m]], ss)
            for m in range(16):
                xb = nxt("xs", xs)
                K.dma("sp", xb[:], x1s[:, m, tk:tk + 512], r=[x1dep[p * 2 + half]], w=xb.all)
                t_ = nxt("tt", tt)
                K.op("dve", lambda e: e.tensor_tensor(out=t_[:], in0=Fb.t[:, m, hs], in1=rstd[:], op=ALU.mult),
                     r=[Fb.d[m]] + rstd.all, w=t_.all)
                K.op("dve", lambda e: e.scalar_tensor_tensor(out=t_[:], in0=t_[:], scalar=cm.Cf[:, m:m + 1],
                                                              in1=xb[:], op0=ALU.mult, op1=ALU.add),
                     r=t_.all + xb.all + cm.Cf.all, w=t_.all)
                K.dma("sp", outT[:, m, tk:tk + 512], t_[:], r=t_.all, sem_dep=t_.d[0], is_output=True)


def declare(nc, name, shape, dt, kind):
    return nc.dram_tensor(name, list(shape), dt, kind=kind).ap()


def build_phase2_only():
    nc = bass.Bass("TRN2", target_bir_lowering=False)
    io = {}
    io["cT"] = declare(nc, "cT", [128, 16], F32, "ExternalInput")
    io["b_adaT"] = declare(nc, "b_adaT", [128, 96], F32, "ExternalInput")
    io["norm_gT"] = declare(nc, "norm_gT", [128, 64], F32, "ExternalInput")
    io["w_ada"] = declare(nc, "w_ada", [D, 6 * D], F32, "ExternalInput")
    io["x2T"] = declare(nc, "x2T", [D, 2048], F32, "ExternalInput")
    io["yT"] = declare(nc, "yT", [D, 2048], BF16, "ExternalInput")
    io["w_gate"] = declare(nc, "w_gate", [D, 2 * D], F32, "ExternalInput")
    io["w_up_rwkv"] = declare(nc, "w_up_rwkv", [C, D], F32, "ExternalInput")
    io["w_up_sb"] = declare(nc, "w_up_sb", [C, D], F32, "ExternalInput")
    io["w_out"] = declare(nc, "w_out", [D, D], F32, "ExternalInput")
    io["w_mlp_in"] = declare(nc, "w_mlp_in", [D, DFF], F32, "ExternalInput")
    io["w_mlp_out"] = declare(nc, "w_mlp_out", [DFF, D], F32, "ExternalInput")
    io["outT"] = declare(nc, "outT", [D, 2048], F32, "ExternalOutput")
    io["x1s"] = declare(nc, "x1s", [D, 2048], F32, "Internal")
    K = Ctx(nc)
    with K.es:
        cm = setup_common(K, io, need_f=True)
        phase2(K, io, cm)
        K.finish()
    print("phase2: ninst", K.ninst, "nwaits", K.nwaits, "nsems", K.nsems)
    return nc


def cols128(v):
    return np.ascontiguousarray(v.reshape(-1, 128).T)


def common_inputs(inp, b):
    return {
        "cT": cols128(inp["c"][b]),
        "b_adaT": cols128(inp["b_ada"][0]),
        "norm_gT": cols128(inp["norm_g"][0].reshape(-1)),
        "w_ada": inp["w_ada"][0],
    }


def host_consts():
    p = np.arange(128)[:, None]
    c128_ = np.arange(128)[None, :]
    same = (p // 64) == (c128_ // 64)
    lo = (same & ((c128_ % 64) < (p % 64))).astype(np.float32)
    ups = (same & ((p % 64) < (c128_ % 64))).astype(np.float32)
    upi = (same & ((p % 64) <= (c128_ % 64))).astype(np.float32)
    cst = {}
    cst["mask1"] = np.tile(lo, (1, 4))
    cst["mask2"] = np.tile(np.concatenate([ups, upi], axis=1), (1, 2))
    cst["mask3"] = np.tile(upi, (1, 2))
    cst["id128x4"] = np.tile((p == c128_).astype(np.float32), (1, 4))
    c128 = np.arange(128)[None, :]
    cst["ident"] = (p == c128).astype(np.float32)
    cst["negtri"] = -(p >= c128).astype(np.float32)
    t512 = np.arange(512)[None, :]
    cst["maskd"] = np.concatenate([((p + 128 * d) < t512).astype(np.float32) for d in range(4)], axis=1)
    bf = {k: v.astype(ml_dtypes.bfloat16) for k, v in cst.items()}
    bf["bd1"] = ((p // 64) == (c128 // 64)).astype(np.float32)
    bf["bd64"] = bf["bd1"] / 64.0
    bf["scanmask"] = np.tile((np.arange(512)[None, :] % 64 != 0).astype(np.float32), (128, 1))
    return bf


CONST_SHAPES = {"mask1": (512, BF16), "mask2": (512, BF16), "mask3": (256, BF16), "id128x4": (512, BF16),
                "ident": (128, BF16), "negtri": (128, BF16), "maskd": (2048, BF16), "bd1": (128, F32),
                "bd64": (128, F32), "scanmask": (512, F32)}


def phase1(K, io, cm, n_tiles=16, do_rwkv=True, do_sb=True, groups=(None,), own_from=0, ysc=None):
    nc = K.nc
    ones = cm.ones_bf
    cst = {}
    for name, (n, dt) in CONST_SHAPES.items():
        cst[name] = K.sbuf("cs_" + name, [128, n], dt)
        K.dma("sp", cst[name][:], io["c_" + name], w=cst[name].all)
    prm = K.sbuf("prm_sb", [128, 32], F32)
    omk = K.sbuf("omk", [128, 2], F32)
    lw = K.sbuf("lw", [128, 4, 256], BF16)
    tokm = K.sbuf("tokm", [128, 512], BF16) if "tokmask" in io else None

    def load_group(g):
        sel = (lambda ap: ap) if g is None else (lambda ap: ap[g])
        K.dma("sp", prm[:], sel(io["prm"]), w=prm.all)
        K.op("dve", lambda e: e.tensor_scalar(out=omk[:], in0=prm[:, 16:18], scalar1=-1.0, scalar2=1.0,
                                               op0=ALU.mult, op1=ALU.add), r=prm.all, w=omk.all)
        K.dma("pool", lw[0:64, 0, :], sel(io["w2s"]), w=lw.all, waw=False)
        K.dma("pool", lw[0:64, 1, :], sel(io["a2s"]), w=lw.all, waw=False)
        K.dma("pool", lw[:, 2, :], sel(io["g2s"])[0:128, :], w=lw.all, waw=False)
        K.dma("pool", lw[0:32, 3, :], sel(io["g2s"])[128:160, :], w=lw.all, waw=False)

    KT = [K.sbuf(f"KT{i}", [128, S], BF16, nparts=16) for i in range(2)]
    Vtm = K.sbuf("Vtm", [128, 64, 256], BF16, nparts=16)
    hT = K.sbuf("hT", [128, 16, 512], BF16, nparts=16)
    raw = [K.sbuf(f"raw{i}", [128, 513], F32) for i in range(2)]
    lastcol = K.sbuf("lastcol", [128, 10], F32)
    wbuf = [K.sbuf(f"w1b{i}", [128, 16, 128], BF16) for i in range(2)]
    xs = [K.sbuf(f"p1xs{i}", [128, 512], F32) for i in range(2)]
    sqb = [K.sbuf(f"p1sq{i}", [128, 512], BF16) for i in range(2)]
    sqrt_t = K.sbuf("p1sqrt", [128, 512], F32)
    rstd = K.sbuf("p1rstd", [128, 512], F32)
    PS = [K.psum(f"p1ps{i}", [128, 512], F32) for i in range(8)]
    pr_ = [K.sbuf(f"p_r{i}", [128, 512], F32) for i in range(2)]
    pk_ = [K.sbuf(f"p_k{i}", [128, 512], F32) for i in range(2)]
    pv_ = [K.sbuf(f"p_v{i}", [128, 512], F32) for i in range(2)]
    dwt = K.sbuf("dwt", [64, 512], BF16)
    dat = K.sbuf("dat", [64, 512], BF16)
    dgs = K.sbuf("dgs", [128, 2, 512], BF16)
    q8 = [K.sbuf(f"q8_{i}", [128, 512], BF16) for i in range(2)]
    vtmp = K.sbuf("vtmp", [128, 512], BF16)
    tmpf = [K.sbuf(f"tmpf{i}", [128, 512], F32) for i in range(2)]
    cnt = {"xs": 0, "sq": 0, "w": 0, "ps": 0, "tf": 0}

    def nxt(key, lst):
        b = lst[cnt[key] % len(lst)]
        cnt[key] += 1
        return b

    xT = io["xT"].rearrange("(c p) t -> p c t", p=128)

    rw = RwkvState(K, cst, prm, omk, lw, PS) if do_rwkv else None
    for g in groups:
        load_group(g)
        K.op("dve", lambda e: e.memset(lastcol[:], 0.0), w=lastcol.all)
        if rw is not None:
            rw.reset()
        w1c = io["w1c"] if g is None else io["w1c"][g]
        if ysc is None:
            ya_dst = lambda pair, qt: io["yaT"][pair * 128:(pair + 1) * 128, qt * 512:(qt + 1) * 512]
            yb_dst = lambda hd, qt: io["ybT"][hd * 64:(hd + 1) * 64, qt * 512:(qt + 1) * 512]
            ydep = None
        else:
            gg = g
            ya_dst = lambda pair, qt: ysc[gg * 256 + pair * 128:gg * 256 + (pair + 1) * 128,
                                          (qt - own_from) * 512:(qt - own_from + 1) * 512]
            yb_dst = lambda hd, qt: ysc[1024 + gg * 256 + hd * 64:1024 + gg * 256 + (hd + 1) * 64,
                                        (qt - own_from) * 512:(qt - own_from + 1) * 512]
            ydep = ysc.d[0]
        _phase1_tiles(K, io, cm, n_tiles, do_rwkv, do_sb, own_from, w1c, ya_dst, yb_dst, ydep, rw, xT, tokm,
                      cst, ones, prm, PS, KT, Vtm, hT, raw, lastcol, wbuf, xs, sqb, sqrt_t, rstd, pr_, pk_, pv_,
                      dwt, dat, dgs, q8, vtmp, tmpf, cnt, nxt)


def _phase1_tiles(K, io, cm, n_tiles, do_rwkv, do_sb, own_from, w1c, ya_dst, yb_dst, ydep, rw, xT, tokm,
                  cst, ones, prm, PS, KT, Vtm, hT, raw, lastcol, wbuf, xs, sqb, sqrt_t, rstd, pr_, pk_, pv_,
                  dwt, dat, dgs, q8, vtmp, tmpf, cnt, nxt):
    for qt in range(n_tiles):
        t0 = qt * 512
        ss = PS[7]
        for c in range(16):
            xb = nxt("xs", xs)
            K.dma("sp", xb[:], xT[:, c, t0:t0 + 512], w=xb.all)
            sq = nxt("sq", sqb)
            K.op("act", lambda e: e.activation(out=sq[:], in_=xb[:], func=AF.Square), r=xb.all, w=sq.all)
            mm(K, ss, ss[:], ones[:], sq[:], start=(c == 0), stop=(c == 15), r=sq.all + ones.all)
        rstd_from_ss(K, ss, sqrt_t, rstd, float(D))
        for c in range(16):
            xb = nxt("xs", xs)
            K.dma("sp", xb[:], xT[:, c, t0:t0 + 512], w=xb.all)
            K.op("dve", lambda e: e.tensor_tensor(out=xb[:], in0=xb[:], in1=rstd[:], op=ALU.mult),
                 r=xb.all + rstd.all, w=xb.all)
            K.op("act", lambda e: e.activation(out=hT[:, c, :], in_=xb[:], func=AF.Identity,
                                                scale=cm.A1[:, c:c + 1], bias=cm.B1[:, c:c + 1]),
                 r=xb.all + cm.A1.all + cm.B1.all, w=[hT.d[c]])
            if tokm is not None:
                if c == 0:
                    K.dma("sp", tokm[:], io["tokmask"][:, t0:t0 + 512], w=tokm.all)
                K.op("dve", lambda e: e.tensor_tensor(out=hT[:, c, :], in0=hT[:, c, :], in1=tokm[:], op=ALU.mult),
                     r=[hT.d[c]] + tokm.all, w=[hT.d[c]])
        own = qt >= own_from
        for ch in range(16):
            if (ch < 10 and not do_rwkv) or (ch >= 10 and not do_sb):
                continue
            if ch in (10, 11) and not own:
                continue
            if ch in (0, 1, 8, 9) and qt < own_from - 1:
                continue
            wb = nxt("w", wbuf)
            load_w_cast(K, wb, wb[:], w1c[ch])
            pp = PS[cnt["ps"] % 4]
            cnt["ps"] += 1
            M = {6: 64, 7: 64, 9: 32}.get(ch, 128)
            for kc in range(16):
                mm(K, pp, pp[0:M, :], wb[:, kc, 0:M], hT[:, kc, :], start=(kc == 0), stop=(kc == 15),
                   r=wb.all + [hT.d[kc]])
            if ch < 10:
                rb = raw[ch % 2]
                K.op("dve", lambda e: e.tensor_copy(out=rb[0:M, 0:1], in_=lastcol[0:M, ch:ch + 1]),
                     r=lastcol.all, w=rb.all)
                K.op("act", lambda e: e.activation(out=rb[0:M, 1:513], in_=pp[0:M, :], func=AF.Identity),
                     r=pp.all, w=rb.all)
                df = nxt("tf", tmpf)
                K.op("dve", lambda e: e.tensor_tensor(out=df[0:M, :], in0=rb[0:M, 0:512], in1=rb[0:M, 1:513],
                                                      op=ALU.subtract), r=rb.all, w=df.all)
                if ch < 6:
                    dst = [pr_, pk_, pv_][ch // 2][ch % 2]
                    K.op("dve", lambda e: e.scalar_tensor_tensor(out=dst[:], in0=df[:], scalar=prm[:, ch:ch + 1],
                                                                  in1=rb[:, 1:513], op0=ALU.mult, op1=ALU.add),
                         r=df.all + rb.all + prm.all, w=dst.all)
                else:
                    K.op("dve", lambda e: e.scalar_tensor_tensor(out=df[0:M, :], in0=df[0:M, :],
                                                                  scalar=prm[0:M, ch:ch + 1], in1=rb[0:M, 1:513],
                                                                  op0=ALU.mult, op1=ALU.add),
                         r=df.all + rb.all + prm.all, w=df.all)
                    if ch == 6:
                        K.op("act", lambda e: e.activation(out=dwt[:], in_=df[0:64, :], func=AF.Tanh),
                             r=df.all, w=dwt.all)
                    elif ch == 7:
                        K.op("act", lambda e: e.activation(out=dat[:], in_=df[0:64, :], func=AF.Identity),
                             r=df.all, w=dat.all)
                    elif ch == 8:
                        K.op("act", lambda e: e.activation(out=dgs[:, 0, :], in_=df[:], func=AF.Sigmoid),
                             r=df.all, w=dgs.all)
                    else:
                        K.op("act", lambda e: e.activation(out=dgs[0:32, 1, :], in_=df[0:32, :], func=AF.Sigmoid),
                             r=df.all, w=dgs.all)
                K.op("dve", lambda e: e.tensor_copy(out=lastcol[0:M, ch:ch + 1], in_=rb[0:M, 512:513]),
                     r=rb.all, w=lastcol.all)
            elif ch < 12:
                K.op("act", lambda e: e.activation(out=q8[ch - 10][:], in_=pp[:], func=AF.Identity, scale=0.125),
                     r=pp.all, w=q8[ch - 10].all)
            elif ch < 14:
                K.op("act", lambda e: e.activation(out=KT[ch - 12][:, t0:t0 + 512], in_=pp[:], func=AF.Identity),
                     r=pp.all, w=[KT[ch - 12].d[qt]])
            else:
                pair = ch - 14
                K.op("act", lambda e: e.activation(out=vtmp[:], in_=pp[:], func=AF.Identity), r=pp.all, w=vtmp.all)
                pT = PS[6]
                pTb = pT.t[:].bitcast(BF16)
                for bk in range(4):
                    K.op("pe", lambda e: e.transpose(pTb[:, bk * 128:(bk + 1) * 128], vtmp[:, bk * 128:(bk + 1) * 128],
                                                     cst["ident"][:]),
                         r=vtmp.all + cst["ident"].all, w=pT.all)
                K.op("dve", lambda e: e.tensor_copy(
                    out=Vtm[:, 4 * qt:4 * qt + 4, pair * 128:(pair + 1) * 128],
                    in_=pTb[:, 0:512].rearrange("p (b c) -> p b c", c=128)), r=pT.all, w=[Vtm.d[qt]])
        if do_rwkv:
            for pair in range(2):
                rw.tile(qt, pair, pr_[pair], pk_[pair], pv_[pair], dwt, dat, dgs, ya_dst(pair, qt) if own else None, own, ydep)
        if do_sb and own:
            K.barrier()
            sb_tile(K, cst, ones, PS, qt, q8, KT, Vtm, yb_dst, ydep, tmpf, rw=rw)
            K.barrier()


_sbst = {}


def sb_tile(K, cst, ones, PS, qt, q8, KT, Vtm, yb_dst, ydep, tmpf, rw=None):
    st = _sbst.get(id(K))
    if st is None:
        st = {}
        if rw is None:
            st["e"] = [K.sbuf(f"sb_e{i}", [128, 512], F32) for i in range(2)]
            st["sp"] = [K.sbuf(f"sb_sp{i}", [128, 512], BF16) for i in range(3)]
            st["ar"] = [K.sbuf(f"sb_ar{i}", [128, 512], F32) for i in range(2)]
            st["att"] = [K.sbuf(f"sb_att{i}", [128, 512], BF16) for i in range(3)]
            st["Cc"] = K.sbuf("sb_Cc", [128, 512], F32)
        else:
            st["e"] = [rw.kk.view(rw.kk[:], "sbv_e0"), rw.cl.view(rw.cl[:], "sbv_e1")]
            st["ar"] = [rw.t1.view(rw.t1[:], "sbv_ar0"), rw.t2.view(rw.t2[:], "sbv_ar1")]
            st["Cc"] = rw.t3.view(rw.t3[:], "sbv_Cc")
            st["sp"] = [rw.AR.view(rw.AR[:, 0, :], "sbv_sp0"), rw.AR.view(rw.AR[:, 1, :], "sbv_sp1"),
                        rw.BK.view(rw.BK[:, 0, :], "sbv_sp2")]
            st["att"] = [rw.BK.view(rw.BK[:, 1, :], "sbv_at0"), rw.pre.view(rw.pre[:, 0, :], "sbv_at1"),
                         rw.pre.view(rw.pre[:, 1, :], "sbv_at2")]
        st["yo"] = [K.sbuf(f"sb_yo{i}", [64, 512], BF16) for i in range(2)]
        st["n"] = 0
        _sbst[id(K)] = st
    t0 = qt * 512
    negtri = cst["negtri"]
    maskd = cst["maskd"]
    Cc = st["Cc"]
    for hd in range(4):
        pair, bp = hd // 2, 64 * (hd % 2)
        qv = q8[pair][bp:bp + 64, :]
        qd = q8[pair].all
        jmax = 4 * qt + 3
        js = list(range(jmax, -1, -1))
        ypsum = PS[6]
        info = {}

        def stage1a(idx):
            j = js[idx]
            d = j - 4 * qt
            kT = KT[pair][bp:bp + 64, j * 128:(j + 1) * 128]
            kd = [KT[pair].d[j // 4]]
            zp = PS[idx % 2]
            mm(K, zp, zp[:], kT, qv, True, True, r=kd + qd)
            e = st["e"][idx % 2]
            sp = st["sp"][idx % 3]
            K.op("act", lambda en: en.activation(out=e[:], in_=zp[:], func=AF.Exp), r=zp.all, w=e.all)
            info[idx] = (j, d, kT, kd, sp, e)

        def stage1b(idx):
            j, d, kT, kd, sp, e = info[idx]
            K.op("act", lambda en: en.activation(out=sp[:], in_=e[:], func=AF.Ln, bias=1.0), r=e.all, w=sp.all)
            if d >= 0:
                K.op("dve", lambda en: en.tensor_tensor(out=sp[:], in0=sp[:], in1=maskd[:, d * 512:(d + 1) * 512],
                                                        op=ALU.mult), r=sp.all + maskd.all, w=sp.all)
            info[idx] = (j, d, kT, kd, sp)

        def stage2(idx):
            j, d, kT, kd, sp = info[idx]
            ap_ = PS[2 + idx % 2]
            cp = PS[4 + idx % 2]
            mm(K, ap_, ap_[:], kT, qv, True, False, r=kd + qd)
            mm(K, ap_, ap_[:], negtri[:], sp[:], False, True, r=sp.all + negtri.all)
            last = (idx == len(js) - 1)
            if not last:
                mm(K, cp, cp[:], ones[:], sp[:], True, True, r=sp.all + ones.all)
            att = st["att"][idx % 3]
            if idx == 0:
                K.op("act", lambda en: en.activation(out=att[:], in_=ap_[:], func=AF.Exp), r=ap_.all, w=att.all)
                if not last:
                    K.op("dve", lambda en: en.tensor_copy(out=Cc[:], in_=cp[:]), r=cp.all, w=Cc.all)
            else:
                ar = st["ar"][idx % 2]
                K.op("dve", lambda en: en.tensor_tensor(out=ar[:], in0=ap_[:], in1=Cc[:], op=ALU.subtract),
                     r=ap_.all + Cc.all, w=ar.all)
                K.op("act", lambda en: en.activation(out=att[:], in_=ar[:], func=AF.Exp), r=ar.all, w=att.all)
                if not last:
                    K.op("dve", lambda en: en.tensor_tensor(out=Cc[:], in0=Cc[:], in1=cp[:], op=ALU.add),
                         r=cp.all + Cc.all, w=Cc.all)
            if d >= 0:
                K.op("dve", lambda en: en.tensor_tensor(out=att[:], in0=att[:], in1=maskd[:, d * 512:(d + 1) * 512],
                                                        op=ALU.mult), r=att.all + maskd.all, w=att.all)
            info[idx] = (j, att)

        def stage3(idx):
            j, att = info[idx]
            mm(K, ypsum, ypsum[0:64, :], Vtm[:, j, hd * 64:(hd + 1) * 64], att[:], idx == 0, idx == len(js) - 1,
               r=att.all + [Vtm.d[j // 4]])

        n = len(js)
        for step in range(n + 2):
            if step < n:
                stage1a(step)
            if 0 <= step - 1 < n:
                stage2(step - 1)
            if step < n:
                stage1b(step)
            if 0 <= step - 2 < n:
                stage3(step - 2)
        yo = st["yo"][st["n"] % 2]
        st["n"] += 1
        K.op("act", lambda en: en.activation(out=yo[:], in_=ypsum[0:64, :], func=AF.Identity), r=ypsum.all, w=yo.all)
        if ydep is None:
            K.dma("sp", yb_dst(hd, qt), yo[:], r=yo.all, sem_dep=yo.d[0], is_output=True)
        else:
            K.scratch_events.append(K.dma("sp", yb_dst(hd, qt), yo[:], r=yo.all, sem_dep=yo.d[0]))


class RwkvState:
    def __init__(self, K, cst, prm, omk, lw, PS):
        self.K, self.cst, self.prm, self.omk, self.lw, self.PS = K, cst, prm, omk, lw, PS
        sb = K.sbuf
        self.St = [[sb(f"St{p}_{i}", [128, 64], BF16) for i in range(3)] for p in range(2)]
        for p in range(2):
            K.op("dve", lambda e: e.memset(self.St[p][0][:], 0.0), w=self.St[p][0].all)
        self.stn = [0, 0]
        f = lambda n: sb("rw_" + n, [128, 512], F32)
        self.ld, self.a, self.g, self.kk, self.kf, self.cl, self.W = (f(n) for n in ("ld", "a", "g", "kk", "kf", "cl", "W"))
        self.t1, self.t2, self.t3 = f("t1"), f("t2"), f("t3")
        self.AR = sb("rw_AR", [128, 2, 512], BF16)
        self.BK = sb("rw_BK", [128, 2, 512], BF16)
        self.pre = sb("rw_pre", [128, 3, 512], BF16)
        self.TT = sb("rw_TT", [128, 4, 512], BF16)
        self.X1 = sb("rw_X1", [128, 1024], BF16)
        self.X2 = sb("rw_X2", [128, 1024], BF16)
        self.X3 = sb("rw_X3", [128, 512], BF16)
        self.A = [sb(f"rw_A{i}", [128, 512], BF16) for i in range(2)]
        self.AT = [sb(f"rw_AT{i}", [128, 512], BF16) for i in range(2)]
        self.Tt = [sb(f"rw_Tt{i}", [128, 512], BF16) for i in range(2)]
        self.X1b = sb("rw_X1b", [128, 512], BF16)
        self.X2b = sb("rw_X2b", [128, 512], BF16)
        self.Rbd = sb("rw_Rbd", [128, 8, 2, 64], BF16)
        self.Pbd = sb("rw_Pbd", [128, 8, 128], BF16)
        self.Ghb = sb("rw_Ghb", [128, 8, 128], BF16)
        self.Gbd = sb("rw_Gbd", [128, 8, 128], BF16)
        self.BhB = sb("rw_BhB", [128, 8, 128], BF16)
        self.KhB = sb("rw_KhB", [128, 8, 128], BF16)
        for b_ in (self.Rbd, self.Pbd, self.BhB, self.KhB):
            K.op("dve", lambda e: e.memset(b_[:], 0.0), w=b_.all)
        self.yo = [sb(f"rw_yo{i}", [128, 512], BF16) for i in range(2)]
        self.nyo = 0

    def reset(self):
        for p in range(2):
            cur = self.St[p][self.stn[p] % 3]
            self.K.op("dve", lambda e: e.memset(cur[:], 0.0), w=cur.all)

    def tile(self, qt, pair, p_r, p_k, p_v, dwt, dat, dgs, ya_ap, emit_y=True, ydep=None):
        K, cst, prm, omk, lw, PS = self.K, self.cst, self.prm, self.omk, self.lw, self.PS
        t0 = qt * 512
        ps_ = slice(pair * 128, (pair + 1) * 128)
        col = lambda i: prm[:, i + pair:i + pair + 1]
        ld, a, g, kk, kf, cl, W = self.ld, self.a, self.g, self.kk, self.kf, self.cl, self.W
        t1, t2, t3 = self.t1, self.t2, self.t3
        AR, BK, pre, TT = self.AR, self.BK, self.pre, self.TT
        DV = lambda fn, r, w: K.op("dve", fn, r=r, w=w)
        AC = lambda fn, r, w: K.op("act", fn, r=r, w=w)
        mm(K, PS[0], PS[0][:], lw[0:64, 0, ps_], dwt[0:64, :], True, True, r=lw.all + dwt.all)
        mm(K, PS[1], PS[1][:], lw[0:64, 1, ps_], dat[0:64, :], True, True, r=lw.all + dat.all)
        if emit_y:
            mm(K, PS[2], PS[2][:], lw[:, 2, ps_], dgs[:, 0, :], True, False, r=lw.all + dgs.all)
            mm(K, PS[2], PS[2][:], lw[0:32, 3, ps_], dgs[0:32, 1, :], False, True, r=lw.all + dgs.all)
        AC(lambda e: e.activation(out=t1[:], in_=PS[0][:], func=AF.Sigmoid, bias=col(10)), PS[0].all + prm.all, t1.all)
        DV(lambda e: e.tensor_scalar(out=ld[:], in0=t1[:], scalar1=-0.6065306597126334, scalar2=None, op0=ALU.mult),
           t1.all, ld.all)
        AC(lambda e: e.activation(out=a[:], in_=PS[1][:], func=AF.Sigmoid, bias=col(12)), PS[1].all + prm.all, a.all)
        if emit_y:
            AC(lambda e: e.activation(out=g[:], in_=PS[2][:], func=AF.Identity), PS[2].all, g.all)
        DV(lambda e: e.tensor_scalar(out=t1[:], in0=p_k[:], scalar1=col(14), scalar2=None, op0=ALU.mult),
           p_k.all + prm.all, t1.all)
        AC(lambda e: e.activation(out=t2[:], in_=t1[:], func=AF.Square), t1.all, t2.all)
        mm(K, PS[3], PS[3][:], cst["bd1"][:], t2[:], True, True, r=cst["bd1"].all + t2.all)
        DV(lambda e: e.tensor_scalar(out=t2[:], in0=PS[3][:], scalar1=1e-24, scalar2=None, op0=ALU.max),
           PS[3].all, t2.all)
        AC(lambda e: e.activation(out=t2[:], in_=t2[:], func=AF.Sqrt), t2.all, t2.all)
        DV(lambda e: e.reciprocal(out=t2[:], in_=t2[:]), t2.all, t2.all)
        DV(lambda e: e.tensor_tensor(out=kk[:], in0=t1[:], in1=t2[:], op=ALU.mult), t1.all + t2.all, kk.all)
        DV(lambda e: e.tensor_scalar(out=t1[:], in0=a[:], scalar1=col(16), scalar2=omk[:, pair:pair + 1],
                                     op0=ALU.mult, op1=ALU.add), a.all + prm.all + omk.all, t1.all)
        DV(lambda e: e.tensor_tensor(out=kf[:], in0=p_k[:], in1=t1[:], op=ALU.mult), p_k.all + t1.all, kf.all)
        DV(lambda e: e.tensor_tensor_scan(out=cl[:], data0=cst["scanmask"][:], data1=ld[:], initial=0.0,
                                          op0=ALU.mult, op1=ALU.add), cst["scanmask"].all + ld.all, cl.all)
        AC(lambda e: e.activation(out=W[:], in_=cl[:], func=AF.Exp), cl.all, W.all)
        DV(lambda e: e.tensor_tensor(out=t3[:], in0=kk[:], in1=a[:], op=ALU.mult), kk.all + a.all, t3.all)
        AC(lambda e: e.activation(out=t1[:], in_=cl[:], func=AF.Exp, scale=-1.0), cl.all, t1.all)
        DV(lambda e: e.tensor_tensor(out=BK[:, 0, :], in0=t3[:], in1=t1[:], op=ALU.mult), t3.all + t1.all, BK.all)
        DV(lambda e: e.tensor_tensor(out=BK[:, 1, :], in0=kf[:], in1=t1[:], op=ALU.mult), kf.all + t1.all, BK.all)
        DV(lambda e: e.tensor_tensor(out=t2[:], in0=cl[:], in1=ld[:], op=ALU.subtract), cl.all + ld.all, t2.all)
        AC(lambda e: e.activation(out=t2[:], in_=t2[:], func=AF.Exp), t2.all, t2.all)
        DV(lambda e: e.scalar_tensor_tensor(out=AR[:, 0, :], in0=kk[:], scalar=-1.0, in1=t2[:], op0=ALU.mult,
                                            op1=ALU.mult), kk.all + t2.all, AR.all)
        if emit_y:
            DV(lambda e: e.tensor_tensor(out=AR[:, 1, :], in0=p_r[:], in1=W[:], op=ALU.mult), p_r.all + W.all, AR.all)
        cl3 = cl[:].rearrange("p (c s) -> p c s", s=64)
        t13 = t1[:].rearrange("p (c s) -> p c s", s=64)
        DV(lambda e: e.tensor_tensor(out=t13, in0=cl3[:, :, 63:64].to_broadcast([128, 8, 64]), in1=cl3,
                                     op=ALU.subtract), cl.all, t1.all)
        AC(lambda e: e.activation(out=t1[:], in_=t1[:], func=AF.Exp), t1.all, t1.all)
        DV(lambda e: e.tensor_tensor(out=pre[:, 0, :], in0=t3[:], in1=t1[:], op=ALU.mult), t3.all + t1.all, pre.all)
        DV(lambda e: e.tensor_tensor(out=pre[:, 1, :], in0=kf[:], in1=t1[:], op=ALU.mult), kf.all + t1.all, pre.all)
        AC(lambda e: e.activation(out=pre[:, 2, :], in_=p_v[:], func=AF.Identity), p_v.all, pre.all)
        if STOP_AT == 1:
            return
        ident = cst["ident"]
        srcs = [AR[:, 0, :], pre[:, 0, :], pre[:, 1, :], pre[:, 2, :]]
        BhB, KhB = self.BhB, self.KhB
        for qi in range(4):
            pT = PS[6 + qi % 2]
            pTb = pT.t[:].bitcast(BF16)
            for bk in range(4):
                K.op("pe", lambda e: e.transpose(pTb[:, bk * 128:(bk + 1) * 128], srcs[qi][:, bk * 128:(bk + 1) * 128],
                                                 ident[:]), r=AR.all + pre.all + ident.all, w=pT.all)
            AC(lambda e: e.activation(out=TT[:, qi, :], in_=pTb[:, 0:512], func=AF.Identity), pT.all, TT.all)
            if qi in (1, 2):
                BD = BhB if qi == 1 else KhB
                for h2 in range(2):
                    hs_ = slice(h2 * 64, h2 * 64 + 64)
                    AC(lambda e: e.activation(out=BD[hs_, :, h2 * 64:h2 * 64 + 64],
                                              in_=pTb[hs_, 0:512].rearrange("p (b j) -> p b j", j=64),
                                              func=AF.Identity), pT.all, BD.all)
        if STOP_AT == 2:
            return
        X1, X2, X3, X1b, X2b = self.X1, self.X2, self.X3, self.X1b, self.X2b
        Rbd, Pbd, Ghb, Gbd = self.Rbd, self.Pbd, self.Ghb, self.Gbd
        X1v = X1[:].rearrange("p (c h w x) -> p c h w x", h=2, w=2, x=128)
        X2v = X2[:].rearrange("p (c h w x) -> p c h w x", h=2, w=2, x=128)
        X3v = X3[:].rearrange("p (c h x) -> p c h x", h=2, x=128)
        for hf in range(2):
            def ph_iter():
                for cpl in range(2):
                    for hd in range(2):
                        blk = hf * 2 + cpl
                        yield cpl, hd, blk, cpl * 2 + hd, slice(hd * 64, hd * 64 + 64), slice(blk * 128, blk * 128 + 128)
            for cpl, hd, blk, sl, hp, pc in ph_iter():
                mm(K, PS[hd], PS[hd][:, cpl * 256:(cpl + 1) * 256], AR[hp, 0, pc], BK[hp, :, pc], True, True,
                   r=AR.all + BK.all)
                if emit_y:
                    mm(K, PS[2 + hd], PS[2 + hd][:, cpl * 256:(cpl + 1) * 256], BK[hp, 0, pc], AR[hp, :, pc], True, True,
                       r=AR.all + BK.all)
                    mm(K, PS[4 + hd], PS[4 + hd][:, cpl * 128:(cpl + 1) * 128], BK[hp, 1, pc], AR[hp, 1, pc], True, True,
                       r=AR.all + BK.all)
                else:
                    mm(K, PS[2 + hd], PS[2 + hd][:, cpl * 256:cpl * 256 + 128], BK[hp, 0, pc], AR[hp, 0, pc], True, True,
                       r=AR.all + BK.all)
            for hd in range(2):
                DV(lambda e: e.tensor_tensor(out=X1v[:, :, hd], in0=PS[hd][:].rearrange("p (c w x) -> p c w x", w=2, x=128),
                                             in1=cst["mask1"][:].rearrange("p (c w x) -> p c w x", w=2, x=128),
                                             op=ALU.mult), PS[hd].all + cst["mask1"].all, X1.all)
                if emit_y:
                    DV(lambda e: e.tensor_tensor(out=X2v[:, :, hd], in0=PS[2 + hd][:].rearrange("p (c w x) -> p c w x", w=2, x=128),
                                                 in1=cst["mask2"][:].rearrange("p (c w x) -> p c w x", w=2, x=128),
                                                 op=ALU.mult), PS[2 + hd].all + cst["mask2"].all, X2.all)
                    DV(lambda e: e.tensor_tensor(out=X3v[:, :, hd], in0=PS[4 + hd][:, 0:256].rearrange("p (c x) -> p c x", x=128),
                                                 in1=cst["mask3"][:].rearrange("p (c x) -> p c x", x=128),
                                                 op=ALU.mult), PS[4 + hd].all + cst["mask3"].all, X3.all)
                else:
                    DV(lambda e: e.tensor_tensor(out=X2v[:, :, hd, 0, :],
                                                 in0=PS[2 + hd][:].rearrange("p (c w x) -> p c w x", w=2, x=128)[:, :, 0, :],
                                                 in1=cst["mask2"][:].rearrange("p (c w x) -> p c w x", w=2, x=128)[:, :, 0, :],
                                                 op=ALU.mult), PS[2 + hd].all + cst["mask2"].all, X2.all)
            if STOP_AT == 3:
                continue
            X1s = X1[:].rearrange("p (s w x) -> p s w x", w=2, x=128)
            X2s = X2[:].rearrange("p (s w x) -> p s w x", w=2, x=128)
            Tt = self.Tt
            Tcur = Tt[0]
            DV(lambda e: e.tensor_tensor(out=Tcur[:].rearrange("p (s x) -> p s x", x=128), in0=X1s[:, :, 0, :],
                                         in1=cst["id128x4"][:].rearrange("p (s x) -> p s x", x=128), op=ALU.add),
               X1.all + cst["id128x4"].all, Tcur.all)
            Acur = (lambda sl: X2s[:, sl, 0, :], X2.all)
            ATcur = (lambda sl: X1s[:, sl, 0, :], X1.all)
            ti = 0
            for k in range(6):
                do_sq = k <= 4
                do_sqT = k <= 3
                do_T = k >= 1
                for sl in range(4):
                    so = slice(sl * 128, sl * 128 + 128)
                    if do_sq:
                        mm(K, PS[6], PS[6][:, so], ATcur[0](sl), Acur[0](sl), True, True, r=Acur[1] + ATcur[1])
                    if do_sqT:
                        mm(K, PS[7], PS[7][:, so], Acur[0](sl), ATcur[0](sl), True, True, r=Acur[1] + ATcur[1])
                    if do_T:
                        mm(K, PS[5], PS[5][:, so], Acur[0](sl), Tcur[:, so], True, True, r=Acur[1] + Tcur.all)
                if do_T:
                    Tn = Tt[(ti + 1) % 2]
                    DV(lambda e: e.tensor_tensor(out=Tn[:], in0=PS[5][:], in1=Tcur[:], op=ALU.add),
                       PS[5].all + Tcur.all, Tn.all)
                    Tcur = Tn
                    ti += 1
                if do_sq:
                    An = self.A[k % 2]
                    AC(lambda e: e.activation(out=An[:], in_=PS[6][:], func=AF.Identity), PS[6].all, An.all)
                    if do_sqT:
                        ATn = self.AT[k % 2]
                        AC(lambda e: e.activation(out=ATn[:], in_=PS[7][:], func=AF.Identity), PS[7].all, ATn.all)
                        ATcur = ((lambda b: (lambda sl: b[:, sl * 128:sl * 128 + 128]))(ATn), ATn.all)
                    Acur = ((lambda b: (lambda sl: b[:, sl * 128:sl * 128 + 128]))(An), An.all)
            for cpl, hd, blk, sl, hp, pc in ph_iter():
                so = slice(sl * 128, sl * 128 + 128)
                if emit_y:
                    mm(K, PS[0], PS[0][:, so], Tcur[:, so], X2s[:, sl, 1, :], True, True, r=Tcur.all + X2.all)
                mm(K, PS[1], PS[1][:, so], Tcur[:, so], BhB[:, blk * 2 + hd, :], True, True, r=Tcur.all + BhB.all)
            if emit_y:
                AC(lambda e: e.activation(out=X1b[:], in_=PS[0][:], func=AF.Identity), PS[0].all, X1b.all)
            AC(lambda e: e.activation(out=X2b[:], in_=PS[1][:], func=AF.Identity), PS[1].all, X2b.all)
            p3v = PS[3][:].rearrange("p (c x) -> p c x", x=128)
            for cpl, hd, blk, sl, hp, pc in ph_iter():
                so = slice(sl * 128, sl * 128 + 128)
                atT = TT[:, 0, blk * 128 + hd * 64:blk * 128 + hd * 64 + 64]
                if emit_y:
                    mm(K, PS[2], PS[2][hp, cpl * 128:(cpl + 1) * 128], atT, X1b[:, so], True, True, r=TT.all + X1b.all)
                for e2 in range(2):
                    mm(K, PS[3], p3v[hp, cpl * 2 + e2, hd * 64:hd * 64 + 64], atT,
                       X2b[:, sl * 128 + e2 * 64:sl * 128 + e2 * 64 + 64], True, True, r=TT.all + X2b.all)
                if emit_y:
                    mm(K, PS[4], PS[4][:, so], X1s[:, sl, 1, :], X1b[:, so], True, True, r=X1.all + X1b.all)
                mm(K, PS[5], PS[5][:, so], X1s[:, sl, 1, :], X2b[:, so], True, True, r=X1.all + X2b.all)
            for hd in range(2):
                hp = slice(hd * 64, hd * 64 + 64)
                if emit_y:
                  DV(lambda e: e.tensor_tensor(out=Rbd[hp, hf * 4:hf * 4 + 4, hd, :],
                                             in0=PS[2][hp, 0:256].rearrange("p (c t) -> p c t", t=64),
                                             in1=AR[hp, 1, hf * 256:hf * 256 + 256].rearrange("p (c t) -> p c t", t=64),
                                             op=ALU.add), PS[2].all + AR.all, Rbd.all)
                AC(lambda e: e.activation(out=Pbd[hp, hf * 4:hf * 4 + 4, hd * 64:hd * 64 + 64],
                                          in_=p3v[hp, :, hd * 64:hd * 64 + 64], func=AF.Identity), PS[3].all, Pbd.all)
            if emit_y:
              DV(lambda e: e.tensor_tensor(out=Ghb[:, hf * 4:hf * 4 + 4, :], in0=PS[4][:].rearrange("p (s x) -> p s x", x=128),
                                         in1=X3[:].rearrange("p (s x) -> p s x", x=128), op=ALU.add),
               PS[4].all + X3.all, Ghb.all)
            DV(lambda e: e.tensor_tensor(out=Gbd[:, hf * 4:hf * 4 + 4, :], in0=PS[5][:].rearrange("p (s x) -> p s x", x=128),
                                         in1=KhB[:, hf * 4:hf * 4 + 4, :], op=ALU.add),
               PS[5].all + KhB.all, Gbd.all)
        if STOP_AT == 3:
            return
        PY, PSt = PS[6], PS[7]
        for c in range(8):
            blk, e2 = c // 2, c % 2
            Sc = self.St[pair][self.stn[pair] % 3]
            Sn = self.St[pair][(self.stn[pair] + 1) % 3]
            self.stn[pair] += 1
            cc = slice(c * 64, c * 64 + 64)
            mm(K, PSt, PSt[:, cc], Pbd[:, c, :], Sc[:], True, False, r=Pbd.all + Sc.all)
            for hd in range(2):
                hp = slice(hd * 64, hd * 64 + 64)
                vT = TT[:, 3, blk * 128 + hd * 64:blk * 128 + hd * 64 + 64]
                es_ = slice(e2 * 64, e2 * 64 + 64)
                if emit_y:
                    mm(K, PY, PY[hp, cc], Sc[:], Rbd[:, c, hd, :], True, False, r=Sc.all + Rbd.all)
                    mm(K, PY, PY[hp, cc], vT, Ghb[:, blk * 2 + hd, es_], False, True, r=TT.all + Ghb.all)
                mm(K, PSt, PSt[hp, cc], Gbd[:, blk * 2 + hd, es_], vT, False, True, r=TT.all + Gbd.all)
            DV(lambda e: e.scalar_tensor_tensor(out=Sn[:], in0=Sc[:], scalar=W[:, c * 64 + 63:c * 64 + 64],
                                                in1=PSt[:, cc], op0=ALU.mult, op1=ALU.add),
               Sc.all + W.all + PSt.all, Sn.all)
        if not emit_y:
            return
        bd64, bd1 = cst["bd64"], cst["bd1"]
        AC(lambda e: e.activation(out=t1[:], in_=PY[:], func=AF.Identity), PY.all, t1.all)
        AC(lambda e: e.activation(out=t2[:], in_=PY[:], func=AF.Square), PY.all, t2.all)
        mm(K, PS[0], PS[0][:], bd64[:], t1[:], True, True, r=bd64.all + t1.all)
        mm(K, PS[1], PS[1][:], bd64[:], t2[:], True, True, r=bd64.all + t2.all)
        AC(lambda e: e.activation(out=t2[:], in_=PS[0][:], func=AF.Square), PS[0].all, t2.all)
        DV(lambda e: e.tensor_tensor(out=t2[:], in0=PS[1][:], in1=t2[:], op=ALU.subtract), PS[1].all + t2.all, t2.all)
        DV(lambda e: e.tensor_scalar(out=t2[:], in0=t2[:], scalar1=0.0, scalar2=None, op0=ALU.max), t2.all, t2.all)
        AC(lambda e: e.activation(out=t2[:], in_=t2[:], func=AF.Sqrt, bias=GN_EPS), t2.all, t2.all)
        DV(lambda e: e.reciprocal(out=t2[:], in_=t2[:]), t2.all, t2.all)
        DV(lambda e: e.tensor_tensor(out=t1[:], in0=t1[:], in1=PS[0][:], op=ALU.subtract), t1.all + PS[0].all, t1.all)
        DV(lambda e: e.tensor_tensor(out=t1[:], in0=t1[:], in1=t2[:], op=ALU.mult), t1.all + t2.all, t1.all)
        AC(lambda e: e.activation(out=t1[:], in_=t1[:], func=AF.Identity, scale=col(20), bias=col(22)),
           t1.all + prm.all, t1.all)
        DV(lambda e: e.scalar_tensor_tensor(out=t3[:], in0=p_r[:], scalar=col(18), in1=kf[:], op0=ALU.mult,
                                            op1=ALU.mult), p_r.all + kf.all + prm.all, t3.all)
        mm(K, PS[2], PS[2][:], bd1[:], t3[:], True, True, r=bd1.all + t3.all)
        DV(lambda e: e.tensor_tensor(out=t3[:], in0=PS[2][:], in1=p_v[:], op=ALU.mult), PS[2].all + p_v.all, t3.all)
        DV(lambda e: e.tensor_tensor(out=t1[:], in0=t1[:], in1=t3[:], op=ALU.add), t1.all + t3.all, t1.all)
        yo = self.yo[self.nyo % 2]
        self.nyo += 1
        DV(lambda e: e.tensor_tensor(out=yo[:], in0=t1[:], in1=g[:], op=ALU.mult), t1.all + g.all, yo.all)
        if ydep is None:
            K.dma("sp", ya_ap, yo[:], r=yo.all, sem_dep=yo.d[0], is_output=True)
        else:
            K.scratch_events.append(K.dma("sp", ya_ap, yo[:], r=yo.all, sem_dep=yo.d[0]))


def declare_phase1_io(nc, io):
    for name, (n, dt) in CONST_SHAPES.items():
        io["c_" + name] = declare(nc, "c_" + name, [128, n], dt, "ExternalInput")
    io["prm"] = declare(nc, "prm", [128, 32], F32, "ExternalInput")
    io["w2s"] = declare(nc, "w2s", [64, 256], F32, "ExternalInput")
    io["a2s"] = declare(nc, "a2s", [64, 256], F32, "ExternalInput")
    io["g2s"] = declare(nc, "g2s", [160, 256], F32, "ExternalInput")
    io["xT"] = declare(nc, "xT", [D, S], F32, "ExternalInput")
    io["w1c"] = declare(nc, "w1c", [16, 128, 16, 128], F32, "ExternalInput")


def build_phase1_only(n_tiles=16, do_rwkv=True, do_sb=True):
    nc = bass.Bass("TRN2", target_bir_lowering=False)
    io = {}
    io["cT"] = declare(nc, "cT", [128, 16], F32, "ExternalInput")
    io["b_adaT"] = declare(nc, "b_adaT", [128, 96], F32, "ExternalInput")
    io["norm_gT"] = declare(nc, "norm_gT", [128, 64], F32, "ExternalInput")
    io["w_ada"] = declare(nc, "w_ada", [D, 6 * D], F32, "ExternalInput")
    declare_phase1_io(nc, io)
    io["yaT"] = declare(nc, "yaT", [256, S], BF16, "ExternalOutput")
    io["ybT"] = declare(nc, "ybT", [256, S], BF16, "ExternalOutput")
    K = Ctx(nc)
    with K.es:
        cm = setup_common(K, io, need_f=False)
        phase1(K, io, cm, n_tiles=n_tiles, do_rwkv=do_rwkv, do_sb=do_sb)
        K.finish()
    print("phase1: ninst", K.ninst, "nwaits", K.nwaits, "nsems", K.nsems)
    return nc


def phase1_inputs(inp, b, g, consts, light=False):
    cs = slice(256 * g, 256 * g + 256)
    W = inp["w_in"][0]
    def colsel(c0):
        return W[:, c0:c0 + 1024][:, cs]
    blocks = []
    for base in (0, 1024, 2048):
        w = colsel(base)
        blocks += [w[:, 0:128], w[:, 128:256]]
    def pad(w):
        out = np.zeros((D, 128), np.float32)
        out[:, :w.shape[1]] = w
        return out
    blocks += [pad(W[:, 3072:3136]), pad(W[:, 3136:3200]), W[:, 3200:3328], pad(W[:, 3328:3360])]
    for base in (3360, 4384, 5408):
        w = colsel(base)
        blocks += [w[:, 0:128], w[:, 128:256]]
    w1c = np.stack([np.ascontiguousarray(blk.reshape(16, 128, 128).transpose(1, 0, 2)) for blk in blocks])
    mu = inp["mu_shift"][0]
    prm = np.zeros((128, 32), np.float32)
    def padv(v):
        out = np.zeros(128, np.float32)
        out[:v.shape[0]] = v
        return out
    mus = [mu[0:1024][cs][0:128], mu[0:1024][cs][128:256], mu[1024:2048][cs][0:128], mu[1024:2048][cs][128:256],
           mu[2048:3072][cs][0:128], mu[2048:3072][cs][128:256], padv(mu[3072:3136]), padv(mu[3136:3200]),
           mu[3200:3328], padv(mu[3328:3360])]
    for i, v in enumerate(mus):
        prm[:, i] = v
    for j, key in enumerate(["w0", "a0", "k_k", "k_a"]):
        v = inp[key][0][cs]
        prm[:, 10 + 2 * j] = v[0:128]
        prm[:, 11 + 2 * j] = v[128:256]
    rk = inp["r_k"][0].reshape(-1)[cs]
    prm[:, 18], prm[:, 19] = rk[0:128], rk[128:256]
    for j, key in enumerate(["ln_x_w", "ln_x_b"]):
        v = inp[key][0][cs]
        prm[:, 20 + 2 * j] = v[0:128]
        prm[:, 21 + 2 * j] = v[128:256]
    m = {} if light else common_inputs(inp, b)
    if not light:
        m.update({"c_" + k: v for k, v in consts.items()})
        m["xT"] = np.ascontiguousarray(inp["x"][b].T)
    m["prm"] = prm
    m["w2s"] = np.ascontiguousarray(inp["w2"][0][:, cs])
    m["a2s"] = np.ascontiguousarray(inp["a2"][0][:, cs])
    m["g2s"] = np.ascontiguousarray(inp["g2"][0][:, cs])
    m["w1c"] = w1c
    return m


def build_fused(n_tiles=16, own_from=12, n_pass=2):
    nc = bass.Bass("TRN2", target_bir_lowering=False)
    io = {}
    io["cT"] = declare(nc, "cT", [128, 16], F32, "ExternalInput")
    io["b_adaT"] = declare(nc, "b_adaT", [128, 96], F32, "ExternalInput")
    io["norm_gT"] = declare(nc, "norm_gT", [128, 64], F32, "ExternalInput")
    io["w_ada"] = declare(nc, "w_ada", [D, 6 * D], F32, "ExternalInput")
    for name, (n, dt) in CONST_SHAPES.items():
        io["c_" + name] = declare(nc, "c_" + name, [128, n], dt, "ExternalInput")
    io["prm"] = declare(nc, "prm", [4, 128, 32], F32, "ExternalInput")
    io["w2s"] = declare(nc, "w2s", [4, 64, 256], F32, "ExternalInput")
    io["a2s"] = declare(nc, "a2s", [4, 64, 256], F32, "ExternalInput")
    io["g2s"] = declare(nc, "g2s", [4, 160, 256], F32, "ExternalInput")
    io["xT"] = declare(nc, "xT", [D, S], F32, "ExternalInput")
    io["tokmask"] = declare(nc, "tokmask", [128, S], BF16, "ExternalInput")
    io["w1c"] = declare(nc, "w1c", [4, 16, 128, 16, 128], F32, "ExternalInput")
    io["x2T"] = declare(nc, "x2T", [D, 2048], F32, "ExternalInput")
    io["w_gate"] = declare(nc, "w_gate", [D, 2 * D], F32, "ExternalInput")
    io["w_up_rwkv"] = declare(nc, "w_up_rwkv", [C, D], F32, "ExternalInput")
    io["w_up_sb"] = declare(nc, "w_up_sb", [C, D], F32, "ExternalInput")
    io["w_out"] = declare(nc, "w_out", [D, D], F32, "ExternalInput")
    io["w_mlp_in"] = declare(nc, "w_mlp_in", [D, DFF], F32, "ExternalInput")
    io["w_mlp_out"] = declare(nc, "w_mlp_out", [DFF, D], F32, "ExternalInput")
    io["outT"] = declare(nc, "outT", [D, 2048], F32, "ExternalOutput")
    io["x1s"] = declare(nc, "x1s", [D, 2048], F32, "Internal")
    K = Ctx(nc)
    with K.es:
        cm = setup_common(K, io, need_f=True)
        ysc = K.dram("ysc", [2048, 2048], BF16)
        with K.scope():
            phase1(K, io, cm, n_tiles=n_tiles, groups=(0, 1, 2, 3), own_from=own_from, ysc=ysc)
        for E_ in K.engs.values():
            for ev_ in K.scratch_events:
                E_.wait_for(ev_)
        io["yT"] = ysc.t
        phase2(K, io, cm, n_pass=n_pass)
        K.finish()
    print("fused: ninst", K.ninst, "nwaits", K.nwaits, "nsems", K.nsems)
    return nc


def fused_inputs(inp, b, q, consts, n_tiles=16, own_from=12):
    own = (n_tiles - own_from) * 512
    n_real = (q + 1) * own
    n_seq = n_tiles * 512
    xT = np.zeros((D, S), np.float32)
    xT[:, n_seq - n_real:n_seq] = inp["x"][b, 0:n_real].T
    tokmask = np.zeros((128, S), ml_dtypes.bfloat16)
    tokmask[:, n_seq - n_real:n_seq] = 1.0
    m = common_inputs(inp, b)
    m.update({"c_" + k: v for k, v in consts.items()})
    per_g = [phase1_inputs(inp, b, g, consts, light=True) for g in range(4)]
    for key in ("prm", "w2s", "a2s", "g2s", "w1c"):
        m[key] = np.stack([pg[key] for pg in per_g])
    m["xT"] = xT
    m["tokmask"] = tokmask
    x2T = np.zeros((D, 2048), np.float32)
    x2T[:, 0:own] = inp["x"][b, q * own:(q + 1) * own].T
    m["x2T"] = x2T
    m["w_gate"] = np.ascontiguousarray(inp["w_in"][0][:, 6432:])
    for k in ["w_up_rwkv", "w_up_sb", "w_out", "w_mlp_in", "w_mlp_out"]:
        m[k] = inp[k][0]
    return m


def kernel(**inputs):
    inp = {k: np.asarray(v) for k, v in inputs.items()}
    consts = host_consts()
    nc = build_fused()
    in_maps = [fused_inputs(inp, c // 4, c % 4, consts) for c in range(NCORES)]
    res = run_bass_kernel_spmd(nc, in_maps, core_ids=list(range(NCORES)))
    out = np.zeros((NB, S, D), np.float32)
    for c in range(NCORES):
        b, q = c // 4, c % 4
        out[b, q * 2048:(q + 1) * 2048] = res.results[c]["outT"].T
    return out
```

```python
import contextlib
import numpy as np
import ml_dtypes
import concourse.bass as bass
import concourse.mybir as mybir
from concourse.bass_utils import run_bass_kernel_spmd

F32 = mybir.dt.float32
BF16 = mybir.dt.bfloat16
AF = mybir.ActivationFunctionType
ALU = mybir.AluOpType

D = 2048
S = 8192
NB = 2
C = 1024
DFF = 8192
NCORES = 8
NORM_EPS = 1e-6
GN_EPS = 64e-5
SEM_LIMIT = 30000
STOP_AT = 0


class Dep:
    __slots__ = ("name", "lw", "rd", "dsem", "dcnt")

    def __init__(self, name=""):
        self.name = name
        self.lw = []
        self.rd = {}
        self.dsem = None
        self.dcnt = 0


class Eng:
    def __init__(self, K, name, h):
        self.K = K
        self.name = name
        self.h = h
        self.sem = None
        self.cnt = 0
        self.seen = {}
        self.nsem = 0

    def new_sem(self):
        self.sem = self.K.es.enter_context(self.K.nc.semaphore(f"e_{self.name}_{self.nsem}"))
        self.nsem += 1
        self.K.nsems += 1
        self.cnt = 0

    def wait_for(self, ev):
        sem, val, _ = ev
        key = id(sem)
        if self.seen.get(key, 0) < val:
            self.h.wait_ge(sem, val)
            self.seen[key] = val
            self.K.nwaits += 1


class Buf:
    def __init__(self, t, name, nparts=1):
        self.t = t
        self.name = name
        self.d = [Dep(f"{name}.{i}") for i in range(nparts)]

    def __getitem__(self, k):
        return self.t[k]

    def view(self, ap, name):
        b = Buf(ap, name, 1)
        return b

    @property
    def all(self):
        return list(self.d)


class Ctx:
    def __init__(self, nc):
        self.nc = nc
        self.es = contextlib.ExitStack()
        self.sem_es = self.es
        self.scopes = []
        self.nsems = 0
        self.nwaits = 0
        self.ninst = 0
        self.engs = {
            "pe": Eng(self, "pe", nc.tensor),
            "act": Eng(self, "act", nc.scalar),
            "dve": Eng(self, "dve", nc.vector),
            "pool": Eng(self, "pool", nc.gpsimd),
            "sp": Eng(self, "sp", nc.sync),
        }
        for e in self.engs.values():
            e.new_sem()
        self.out_events = []
        self.scratch_events = []

    def sbuf(self, name, shape, dt, nparts=1):
        es = self.scopes[-1][0] if self.scopes else self.es
        t = es.enter_context(self.nc.sbuf_tensor(name, list(shape), dt))
        b = Buf(t, name, nparts)
        if self.scopes:
            self.scopes[-1][1].append(b)
        return b

    @contextlib.contextmanager
    def scope(self):
        es = contextlib.ExitStack()
        self.scopes.append((es, []))
        try:
            yield
        finally:
            _, bufs = self.scopes.pop()
            deps = [d for b in bufs for d in b.d]
            self.barrier(deps)
            es.close()

    def psum(self, name, shape, dt=F32, nparts=1):
        es = self.scopes[-1][0] if self.scopes else self.es
        t = es.enter_context(self.nc.psum_tensor(name, list(shape), dt))
        b = Buf(t, name, nparts)
        if self.scopes:
            self.scopes[-1][1].append(b)
        return b

    def dram(self, name, shape, dt, kind="Internal", nparts=1):
        t = self.nc.dram_tensor(name, list(shape), dt, kind=kind)
        return Buf(t.ap(), name, nparts)

    def _deps(self, E, r, w, same_raw=True):
        for d in r:
            for ev in d.lw:
                if ev[2] == E.name and not same_raw:
                    continue
                E.wait_for(ev)
        for d in w:
            for ev in d.lw:
                if ev[2] == E.name and not same_raw:
                    continue
                E.wait_for(ev)
            for ev in d.rd.values():
                if ev[2] == E.name and not same_raw:
                    continue
                E.wait_for(ev)

    def op(self, eng, fn, r=(), w=()):
        E = self.engs[eng]
        if E.cnt >= SEM_LIMIT:
            E.new_sem()
        self._deps(E, r, w, same_raw=(eng != "pe"))
        inst = fn(E.h)
        E.cnt += 1
        inst.then_inc(E.sem, 1)
        self.ninst += 1
        ev = (E.sem, E.cnt, E.name)
        for d in r:
            d.rd[E.name] = ev
        for d in w:
            d.lw = [ev]
            d.rd = {}
        return ev

    def dma(self, queue, out, in_, r=(), w=(), sem_dep=None, is_output=False, waw=True, **kw):
        E = self.engs[queue]
        if waw:
            self._deps(E, r, w, same_raw=True)
        else:
            self._deps(E, r, (), same_raw=True)
            for d in w:
                for ev in d.rd.values():
                    E.wait_for(ev)
        d0 = sem_dep or (w[0] if len(w) else r[0])
        if d0.dsem is None or d0.dcnt >= SEM_LIMIT:
            d0.dsem = self.es.enter_context(self.nc.semaphore(f"d{self.nsems}"))
            self.nsems += 1
            d0.dcnt = 0
        inst = E.h.dma_start(out=out, in_=in_, **kw)
        d0.dcnt += 16
        inst.then_inc(d0.dsem, 16)
        self.ninst += 1
        ev = (d0.dsem, d0.dcnt, "dma" + str(id(d0)))
        for d in r:
            d.rd[ev[2]] = ev
        for d in w:
            d.lw = [ev]
            d.rd = {}
        if is_output:
            self.out_events.append(ev)
        return ev

    def barrier(self, deps=()):
        evs = []
        for E in self.engs.values():
            if E.cnt > 0:
                evs.append((E.sem, E.cnt, E.name))
        for d in deps:
            evs += d.lw
            evs += list(d.rd.values())
        for E in self.engs.values():
            for ev in evs:
                if ev[2] != E.name:
                    E.wait_for(ev)

    def finish(self):
        E = self.engs["sp"]
        for ev in self.out_events:
            E.wait_for(ev)
        for name, e2 in self.engs.items():
            if name != "sp" and e2.cnt > 0:
                E.wait_for((e2.sem, e2.cnt, name))


def mm(K, ps, ps_ap, lhsT, rhs, start, stop, r=(), w=None):
    wd = w if w is not None else ps.all
    return K.op("pe", lambda e: e.matmul(ps_ap, lhsT=lhsT, rhs=rhs, start=start, stop=stop), r=r, w=wd)


def load_w_cast(K, dst_buf, dst_ap, src_ap, w=None):
    return K.dma("pool", dst_ap, src_ap, w=(w if w is not None else dst_buf.all), waw=False)


class Common:
    pass


def setup_common(K, io, need_f=True):
    nc = K.nc
    cm = Common()
    cm.ones_bf = K.sbuf("ones_bf", [128, 128], BF16)
    K.op("dve", lambda e: e.memset(cm.ones_bf[:], 1.0), w=cm.ones_bf.all)
    cT = K.sbuf("cT_sb", [128, 16], F32)
    K.dma("sp", cT[:], io["cT"], w=cT.all)
    sc = K.sbuf("sc", [128, 16], BF16)
    K.op("act", lambda e: e.activation(out=sc[:], in_=cT[:], func=AF.Silu), r=cT.all, w=sc.all)
    bada = K.sbuf("bada", [128, 96], F32)
    K.dma("sp", bada[:], io["b_adaT"], w=bada.all)
    ng = K.sbuf("ng", [128, 64], F32)
    K.dma("sp", ng[:], io["norm_gT"], w=ng.all)
    nmod = 6 if need_f else 2
    modT = K.sbuf("modT", [128, 96], F32)
    cm.A1 = K.sbuf("A1", [128, 16], F32)
    if need_f:
        cm.Cm = K.sbuf("Cm", [128, 16], F32)
        cm.A2 = K.sbuf("A2", [128, 16], F32)
        cm.Cf = K.sbuf("Cf", [128, 16], F32)
    with K.scope():
        _setup_common_body(K, io, cm, need_f, nmod, modT, sc, bada, ng)
    cm.modT = modT
    cm.B1 = modT
    return cm


def _setup_common_body(K, io, cm, need_f, nmod, modT, sc, bada, ng):
    wst = [K.sbuf(f"wada{i}", [128, 16, 512], BF16) for i in range(2)]
    psm = K.psum("ps_mod", [128, 512], F32)
    w_ada = io["w_ada"].rearrange("(kc p) n -> p kc n", p=128)
    ngrp = nmod * 4
    for g in range(ngrp):
        wt = wst[g % 2]
        load_w_cast(K, wt, wt[:], w_ada[:, :, g * 512:(g + 1) * 512])
        for j in range(4):
            col = g * 4 + j
            for kc in range(16):
                mm(K, psm, psm[:, col:col + 1], wt[:, kc, j * 128:(j + 1) * 128], sc[:, kc:kc + 1],
                   start=(kc == 0), stop=(kc == 15), r=wt.all + sc.all)
    ncol = ngrp * 4
    K.op("dve", lambda e: e.tensor_tensor(out=modT[:, 0:ncol], in0=psm[:, 0:ncol], in1=bada[:, 0:ncol], op=ALU.add),
         r=psm.all + bada.all, w=modT.all)
    K.op("dve", lambda e: e.scalar_tensor_tensor(out=cm.A1[:], in0=modT[:, 16:32], scalar=1.0, in1=ng[:, 0:16],
                                                  op0=ALU.add, op1=ALU.mult), r=modT.all + ng.all, w=cm.A1.all)
    if need_f:
        K.op("dve", lambda e: e.tensor_tensor(out=cm.Cm[:], in0=modT[:, 32:48], in1=ng[:, 16:32], op=ALU.mult),
             r=modT.all + ng.all, w=cm.Cm.all)
        K.op("dve", lambda e: e.scalar_tensor_tensor(out=cm.A2[:], in0=modT[:, 64:80], scalar=1.0, in1=ng[:, 32:48],
                                                      op0=ALU.add, op1=ALU.mult), r=modT.all + ng.all, w=cm.A2.all)
        K.op("dve", lambda e: e.tensor_tensor(out=cm.Cf[:], in0=modT[:, 80:96], in1=ng[:, 48:64], op=ALU.mult),
             r=modT.all + ng.all, w=cm.Cf.all)


def rstd_from_ss(K, ss_ps, sq_t, rstd_t, n):
    K.op("act", lambda e: e.activation(out=sq_t[:], in_=ss_ps[:], func=AF.Sqrt, scale=1.0 / n, bias=NORM_EPS),
         r=ss_ps.all, w=sq_t.all)
    K.op("dve", lambda e: e.reciprocal(out=rstd_t[:], in_=sq_t[:]), r=sq_t.all, w=rstd_t.all)


def phase2(K, io, cm, n_pass=2):
    nc = K.nc
    T = 1024
    ones = cm.ones_bf
    R1 = K.sbuf("R1", [128, 16, T], BF16, nparts=32)
    R2 = K.sbuf("R2", [128, 16, T], BF16, nparts=32)
    Fb = K.sbuf("Fb", [128, 16, T], F32, nparts=16)
    merged_v = Fb.t[:, 0:8, :].bitcast(BF16)
    WA = [K.sbuf(f"WA{i}", [128, 16 * 512], BF16) for i in range(2)]
    WB = [K.sbuf(f"WB{i}", [128, 16 * 256], BF16) for i in range(2)]
    xs = [K.sbuf(f"xs{i}", [128, 512], F32) for i in range(3)]
    sqb = [K.sbuf(f"sqb{i}", [128, 512], BF16) for i in range(2)]
    tt = [K.sbuf(f"tt{i}", [128, 512], F32) for i in range(2)]
    sg = [K.sbuf(f"sg{i}", [128, 512], F32) for i in range(2)]
    sqrt_t = K.sbuf("sqrt_t", [128, 512], F32)
    rstd = K.sbuf("rstd", [128, 512], F32)
    PS = [K.psum(f"ps{i}", [128, 512], F32) for i in range(8)]

    def merged_ap(m, half):
        off = (m % 2) * T + half * 512
        return merged_v[:, m // 2, off:off + 512]

    def merged_dep(m):
        return [Fb.d[m // 2]]

    def mix_ap(m):
        off = (m % 2) * 512
        return Fb.t[:, 8 + m // 2, off:off + 512]

    def mix_dep(m):
        return [Fb.d[8 + m // 2]]

    wg = io["w_gate"].rearrange("(kc p) n -> p kc n", p=128)
    wua = io["w_up_rwkv"].rearrange("(kc p) n -> p kc n", p=128)
    wub = io["w_up_sb"].rearrange("(kc p) n -> p kc n", p=128)
    wo = io["w_out"].rearrange("(kc p) n -> p kc n", p=128)
    w1 = io["w_mlp_in"].rearrange("(kc p) n -> p kc n", p=128)
    w2 = io["w_mlp_out"].rearrange("(kc p) n -> p kc n", p=128)
    xT = io["x2T"].rearrange("(c p) t -> p c t", p=128)
    yT = io["yT"].rearrange("(c p) t -> p c t", p=128)
    outT = io["outT"].rearrange("(c p) t -> p c t", p=128)
    x1s = io["x1s"].rearrange("(c p) t -> p c t", p=128)
    x1dep = [Dep(f"x1s{i}") for i in range(4)]

    cnt = {"xs": 0, "sq": 0, "tt": 0, "sg": 0, "wa": 0, "wb": 0, "ps": 0}

    def nxt(key, lst):
        b = lst[cnt[key] % len(lst)]
        cnt[key] += 1
        return b

    def stats_norm(src_ap_fn, src_dep_fn, ss):
        for m in range(16):
            sq = nxt("sq", sqb)
            K.op("act", lambda e: e.activation(out=sq[:], in_=src_ap_fn(m), func=AF.Square),
                 r=src_dep_fn(m), w=sq.all)
            mm(K, ss, ss[:], ones[:], sq[:], start=(m == 0), stop=(m == 15), r=sq.all + ones.all)
        rstd_from_ss(K, ss, sqrt_t, rstd, float(D))

    for p in range(n_pass):
        t0 = p * T
        for c in range(16):
            evl = K.dma("sp", R2[:, c, :], yT[:, c, t0:t0 + T], w=[R2.d[2 * c], R2.d[2 * c + 1]], sem_dep=R2.d[0], waw=False)
        for d_ in R2.d:
            d_.lw = [evl]
        for half in range(2):
            tk = t0 + half * 512
            ss = PS[7]
            xst = Fb.t[:, 0:8, :].rearrange("p a (b t) -> p (a b) t", t=512)
            for c in range(16):
                K.dma("sp", xst[:, c, :], xT[:, c, tk:tk + 512], w=[Fb.d[c // 2]], sem_dep=Fb.d[c // 2], waw=False)
            stats_norm(lambda m: xst[:, m, :], lambda m: [Fb.d[m // 2]], ss)
            for c in range(16):
                t_ = nxt("tt", tt)
                K.op("dve", lambda e: e.tensor_tensor(out=t_[:], in0=xst[:, c, :], in1=rstd[:], op=ALU.mult),
                     r=[Fb.d[c // 2]] + rstd.all, w=t_.all)
                K.op("act", lambda e: e.activation(out=R1[:, c, half * 512:(half + 1) * 512], in_=t_[:],
                                                    func=AF.Identity, scale=cm.A1[:, c:c + 1],
                                                    bias=cm.B1[:, c:c + 1]),
                     r=t_.all + cm.A1.all + cm.B1.all, w=[R1.d[2 * c + half]])
        for jg in range(8):
            wa = nxt("wa", WA)
            wb = nxt("wb", WB)
            wav = wa.t[:, :].rearrange("p (a n) -> p a n", n=256)
            wbv = wb.t[:, :].rearrange("p (a n) -> p a n", n=256)
            c0 = jg * 256
            load_w_cast(K, wa, wav[:, 0:16, :], wg[:, :, c0:c0 + 256])
            load_w_cast(K, wa, wav[:, 16:32, :], wg[:, :, 2048 + c0:2048 + c0 + 256])
            load_w_cast(K, wb, wbv[:, 0:8, :], wua[:, :, c0:c0 + 256])
            load_w_cast(K, wb, wbv[:, 8:16, :], wub[:, :, c0:c0 + 256])
            for jj in range(2):
                j = jg * 2 + jj
                for half in range(2):
                    hs = slice(half * 512, (half + 1) * 512)
                    base = (cnt["ps"] % 2) * 4
                    cnt["ps"] += 1
                    pga, pgb, pua, pub = PS[base], PS[base + 1], PS[base + 2], PS[base + 3]
                    for kc in range(16):
                        mm(K, pga, pga[:], wav[:, kc, jj * 128:(jj + 1) * 128], R1[:, kc, hs],
                           start=(kc == 0), stop=(kc == 15), r=wa.all + [R1.d[2 * kc + half]])
                    for kc in range(16):
                        mm(K, pgb, pgb[:], wav[:, 16 + kc, jj * 128:(jj + 1) * 128], R1[:, kc, hs],
                           start=(kc == 0), stop=(kc == 15), r=wa.all + [R1.d[2 * kc + half]])
                    for kc in range(8):
                        mm(K, pua, pua[:], wbv[:, kc, jj * 128:(jj + 1) * 128], R2[:, kc, hs],
                           start=(kc == 0), stop=(kc == 7), r=wb.all + [R2.d[2 * kc + half]])
                    for kc in range(8):
                        mm(K, pub, pub[:], wbv[:, 8 + kc, jj * 128:(jj + 1) * 128], R2[:, 8 + kc, hs],
                           start=(kc == 0), stop=(kc == 7), r=wb.all + [R2.d[2 * (8 + kc) + half]])
                    sa = nxt("sg", sg)
                    sb_ = nxt("sg", sg)
                    K.op("act", lambda e: e.activation(out=sa[:], in_=pga[:], func=AF.Sigmoid), r=pga.all, w=sa.all)
                    K.op("act", lambda e: e.activation(out=sb_[:], in_=pgb[:], func=AF.Sigmoid), r=pgb.all, w=sb_.all)
                    K.op("dve", lambda e: e.tensor_tensor(out=sa[:], in0=sa[:], in1=pua[:], op=ALU.mult),
                         r=sa.all + pua.all, w=sa.all)
                    K.op("dve", lambda e: e.tensor_tensor(out=sb_[:], in0=sb_[:], in1=pub[:], op=ALU.mult),
                         r=sb_.all + pub.all, w=sb_.all)
                    K.op("dve", lambda e: e.tensor_tensor(out=merged_ap(j, half), in0=sa[:], in1=sb_[:], op=ALU.add),
                         r=sa.all + sb_.all, w=merged_dep(j))
        for half in range(2):
            tk = t0 + half * 512
            ss = PS[7]
            for mg in range(4):
                wa = nxt("wa", WA)
                wav = wa.t[:, :].rearrange("p (a n) -> p a n", n=512)
                load_w_cast(K, wa, wav[:, :, :], wo[:, :, mg * 512:(mg + 1) * 512])
                for mj in range(4):
                    m = mg * 4 + mj
                    pm = PS[cnt["ps"] % 4]
                    cnt["ps"] += 1
                    for kc in range(16):
                        mm(K, pm, pm[:], wav[:, kc, mj * 128:(mj + 1) * 128], merged_ap(kc, half),
                           start=(kc == 0), stop=(kc == 15), r=wa.all + merged_dep(kc))
                    K.op("act", lambda e: e.activation(out=mix_ap(m), in_=pm[:], func=AF.Identity),
                         r=pm.all, w=mix_dep(m))
                    sq = nxt("sq", sqb)
                    K.op("act", lambda e: e.activation(out=sq[:], in_=pm[:], func=AF.Square), r=pm.all, w=sq.all)
                    mm(K, ss, ss[:], ones[:], sq[:], start=(m == 0), stop=(m == 15), r=sq.all + ones.all)
            rstd_from_ss(K, ss, sqrt_t, rstd, float(D))
            ss2 = PS[6]
            for m in range(16):
                xb = nxt("xs", xs)
                K.dma("sp", xb[:], xT[:, m, tk:tk + 512], w=xb.all)
                t_ = nxt("tt", tt)
                K.op("dve", lambda e: e.tensor_tensor(out=t_[:], in0=mix_ap(m), in1=rstd[:], op=ALU.mult),
                     r=mix_dep(m) + rstd.all, w=t_.all)
                K.op("dve", lambda e: e.scalar_tensor_tensor(out=mix_ap(m), in0=t_[:], scalar=cm.Cm[:, m:m + 1],
                                                              in1=xb[:], op0=ALU.mult, op1=ALU.add),
                     r=t_.all + xb.all + cm.Cm.all, w=mix_dep(m))
                ev_ = K.dma("sp", x1s[:, m, tk:tk + 512], mix_ap(m), r=mix_dep(m), sem_dep=mix_dep(m)[0])
                if m == 0:
                    x1dep[p * 2 + half].lw = []
                x1dep[p * 2 + half].lw.append(ev_)
                sq = nxt("sq", sqb)
                K.op("act", lambda e: e.activation(out=sq[:], in_=mix_ap(m), func=AF.Square), r=mix_dep(m), w=sq.all)
                mm(K, ss2, ss2[:], ones[:], sq[:], start=(m == 0), stop=(m == 15), r=sq.all + ones.all)
            rstd_from_ss(K, ss2, sqrt_t, rstd, float(D))
            for m in range(16):
                t_ = nxt("tt", tt)
                K.op("dve", lambda e: e.tensor_tensor(out=t_[:], in0=mix_ap(m), in1=rstd[:], op=ALU.mult),
                     r=mix_dep(m) + rstd.all, w=t_.all)
                K.op("act", lambda e: e.activation(out=R1[:, m, half * 512:(half + 1) * 512], in_=t_[:],
                                                    func=AF.Identity, scale=cm.A2[:, m:m + 1],
                                                    bias=cm.modT[:, 48 + m:49 + m]),
                     r=t_.all + cm.A2.all + cm.modT.all, w=[R1.d[2 * m + half]])
        for G in range(4):
            for kg in range(4):
                wa = nxt("wa", WA)
                wav = wa.t[:, :].rearrange("p (a n) -> p a n", n=512)
                f0 = G * 2048 + kg * 512
                load_w_cast(K, wa, wav[:, :, :], w1[:, :, f0:f0 + 512])
                for kj in range(4):
                    k = kg * 4 + kj
                    for half in range(2):
                        hs = slice(half * 512, (half + 1) * 512)
                        pa = PS[cnt["ps"] % 4]
                        cnt["ps"] += 1
                        for kc in range(16):
                            mm(K, pa, pa[:], wav[:, kc, kj * 128:(kj + 1) * 128], R1[:, kc, hs],
                               start=(kc == 0), stop=(kc == 15), r=wa.all + [R1.d[2 * kc + half]])
                        r_ = nxt("sg", sg)
                        K.op("act", lambda e: e.activation(out=r_[:], in_=pa[:], func=AF.Relu), r=pa.all, w=r_.all)
                        K.op("dve", lambda e: e.tensor_tensor(out=R2[:, k, hs], in0=r_[:], in1=r_[:], op=ALU.mult),
                             r=r_.all, w=[R2.d[2 * k + half]])
            for mp in range(8):
                wb = nxt("wb", WB)
                wbv = wb.t[:, :].rearrange("p (a n) -> p a n", n=256)
                load_w_cast(K, wb, wbv[:, :, :], w2[:, G * 16:(G + 1) * 16, mp * 256:(mp + 1) * 256])
                for mj in range(2):
                    m = mp * 2 + mj
                    for half in range(2):
                        hs = slice(half * 512, (half + 1) * 512)
                        pf = PS[4 + cnt["ps"] % 2]
                        cnt["ps"] += 1
                        for k in range(16):
                            mm(K, pf, pf[:], wbv[:, k, mj * 128:(mj + 1) * 128], R2[:, k, hs],
                               start=(k == 0), stop=(k == 15), r=wb.all + [R2.d[2 * k + half]])
                        if G == 0:
                            K.op("act", lambda e: e.activation(out=Fb.t[:, m, hs], in_=pf[:], func=AF.Identity),
                                 r=pf.all, w=[Fb.d[m]])
                        else:
                            K.op("dve", lambda e: e.tensor_tensor(out=Fb.t[:, m, hs], in0=Fb.t[:, m, hs], in1=pf[:],
                                                                  op=ALU.add), r=pf.all + [Fb.d[m]], w=[Fb.d[m]])
        for half in range(2):
            tk = t0 + half * 512
            hs = slice(half * 512, (half + 1) * 512)
            ss = PS[7]
            stats_norm(lambda m: Fb.t[:, m, hs], lambda m: [Fb.d[m]], ss)
            for m in range(16):
                xb = nxt("xs", xs)
                K.dma("sp", xb[:], x1s[:, m, tk:tk + 512], r=[x1dep[p * 2 + half]], w=xb.all)
                t_ = nxt("tt", tt)
                K.op("dve", lambda e: e.tensor_tensor(out=t_[:], in0=Fb.t[:, m, hs], in1=rstd[:], op=ALU.mult),
                     r=[Fb.d[m]] + rstd.all, w=t_.all)
                K.op("dve", lambda e: e.scalar_tensor_tensor(out=t_[:], in0=t_[:], scalar=cm.Cf[:, m:m + 1],
                                                              in1=xb[:], op0=ALU.mult, op1=ALU.add),
                     r=t_.all + xb.all + cm.Cf.all, w=t_.all)
                K.dma("sp", outT[:, m, tk:tk + 512], t_[:], r=t_.all, sem_dep=t_.d[0], is_output=True)


def declare(nc, name, shape, dt, kind):
    return nc.dram_tensor(name, list(shape), dt, kind=kind).ap()


def build_phase2_only():
    nc = bass.Bass("TRN2", target_bir_lowering=False)
    io = {}
    io["cT"] = declare(nc, "cT", [128, 16], F32, "ExternalInput")
    io["b_adaT"] = declare(nc, "b_adaT", [128, 96], F32, "ExternalInput")
    io["norm_gT"] = declare(nc, "norm_gT", [128, 64], F32, "ExternalInput")
    io["w_ada"] = declare(nc, "w_ada", [D, 6 * D], F32, "ExternalInput")
    io["x2T"] = declare(nc, "x2T", [D, 2048], F32, "ExternalInput")
    io["yT"] = declare(nc, "yT", [D, 2048], BF16, "ExternalInput")
    io["w_gate"] = declare(nc, "w_gate", [D, 2 * D], F32, "ExternalInput")
    io["w_up_rwkv"] = declare(nc, "w_up_rwkv", [C, D], F32, "ExternalInput")
    io["w_up_sb"] = declare(nc, "w_up_sb", [C, D], F32, "ExternalInput")
    io["w_out"] = declare(nc, "w_out", [D, D], F32, "ExternalInput")
    io["w_mlp_in"] = declare(nc, "w_mlp_in", [D, DFF], F32, "ExternalInput")
    io["w_mlp_out"] = declare(nc, "w_mlp_out", [DFF, D], F32, "ExternalInput")
    io["outT"] = declare(nc, "outT", [D, 2048], F32, "ExternalOutput")
    io["x1s"] = declare(nc, "x1s", [D, 2048], F32, "Internal")
    K = Ctx(nc)
    with K.es:
        cm = setup_common(K, io, need_f=True)
        phase2(K, io, cm)
        K.finish()
    print("phase2: ninst", K.ninst, "nwaits", K.nwaits, "nsems", K.nsems)
    return nc


def cols128(v):
    return np.ascontiguousarray(v.reshape(-1, 128).T)


def common_inputs(inp, b):
    return {
        "cT": cols128(inp["c"][b]),
        "b_adaT": cols128(inp["b_ada"][0]),
        "norm_gT": cols128(inp["norm_g"][0].reshape(-1)),
        "w_ada": inp["w_ada"][0],
    }


def host_consts():
    p = np.arange(128)[:, None]
    c128_ = np.arange(128)[None, :]
    same = (p // 64) == (c128_ // 64)
    lo = (same & ((c128_ % 64) < (p % 64))).astype(np.float32)
    ups = (same & ((p % 64) < (c128_ % 64))).astype(np.float32)
    upi = (same & ((p % 64) <= (c128_ % 64))).astype(np.float32)
    cst = {}
    cst["mask1"] = np.tile(lo, (1, 4))
    cst["mask2"] = np.tile(np.concatenate([ups, upi], axis=1), (1, 2))
    cst["mask3x2"] = np.tile(upi, (1, 4))
    cst["id128x8"] = np.tile((p == c128_).astype(np.float32), (1, 8))
    c128 = np.arange(128)[None, :]
    cst["ident"] = (p == c128).astype(np.float32)
    cst["negtri"] = -(p >= c128).astype(np.float32)
    t512 = np.arange(512)[None, :]
    cst["maskd"] = np.concatenate([((p + 128 * d) < t512).astype(np.float32) for d in range(4)], axis=1)
    bf = {k: v.astype(ml_dtypes.bfloat16) for k, v in cst.items()}
    bf["bd1"] = ((p // 64) == (c128 // 64)).astype(np.float32)
    bf["bd64"] = bf["bd1"] / 64.0
    bf["scanmask"] = np.tile((np.arange(512)[None, :] % 64 != 0).astype(np.float32), (128, 1))
    return bf


CONST_SHAPES = {"mask1": (512, BF16), "mask2": (512, BF16), "mask3x2": (512, BF16), "id128x8": (1024, BF16),
                "ident": (128, BF16), "negtri": (128, BF16), "maskd": (2048, BF16), "bd1": (128, F32),
                "bd64": (128, F32), "scanmask": (512, F32)}


def phase1(K, io, cm, n_tiles=16, do_rwkv=True, do_sb=True, groups=(None,), own_from=0, ysc=None):
    nc = K.nc
    ones = cm.ones_bf
    cst = {}
    for name, (n, dt) in CONST_SHAPES.items():
        cst[name] = K.sbuf("cs_" + name, [128, n], dt)
        K.dma("sp", cst[name][:], io["c_" + name], w=cst[name].all)
    prm = K.sbuf("prm_sb", [128, 32], F32)
    omk = K.sbuf("omk", [128, 2], F32)
    lw = K.sbuf("lw", [128, 4, 256], BF16)
    tokm = K.sbuf("tokm", [128, 512], BF16) if "tokmask" in io else None

    def load_group(g):
        sel = (lambda ap: ap) if g is None else (lambda ap: ap[g])
        K.dma("sp", prm[:], sel(io["prm"]), w=prm.all)
        K.op("dve", lambda e: e.tensor_scalar(out=omk[:], in0=prm[:, 16:18], scalar1=-1.0, scalar2=1.0,
                                               op0=ALU.mult, op1=ALU.add), r=prm.all, w=omk.all)
        K.dma("pool", lw[0:64, 0, :], sel(io["w2s"]), w=lw.all, waw=False)
        K.dma("pool", lw[0:64, 1, :], sel(io["a2s"]), w=lw.all, waw=False)
        K.dma("pool", lw[:, 2, :], sel(io["g2s"])[0:128, :], w=lw.all, waw=False)
        K.dma("pool", lw[0:32, 3, :], sel(io["g2s"])[128:160, :], w=lw.all, waw=False)

    KT = [K.sbuf(f"KT{i}", [128, S], BF16, nparts=16) for i in range(2)]
    Vtm = K.sbuf("Vtm", [128, 64, 256], BF16, nparts=16)
    hT = K.sbuf("hT", [128, 16, 512], BF16, nparts=16)
    raw = [K.sbuf(f"raw{i}", [128, 513], F32) for i in range(2)]
    lastcol = K.sbuf("lastcol", [128, 10], F32)
    wbuf = [K.sbuf(f"w1b{i}", [128, 16, 128], BF16) for i in range(2)]
    xs = [K.sbuf(f"p1xs{i}", [128, 512], F32) for i in range(2)]
    sqb = [K.sbuf(f"p1sq{i}", [128, 512], BF16) for i in range(2)]
    sqrt_t = K.sbuf("p1sqrt", [128, 512], F32) if not do_rwkv else None
    rstd = K.sbuf("p1rstd", [128, 512], F32) if not do_rwkv else None
    PSall = K.psum("p1psall", [128, 4096], F32)
    PS = [PSall.view(PSall.t[:, i * 512:(i + 1) * 512], f"p1ps{i}") for i in range(8)]
    PW = []
    for i in range(4):
        wb_ = Buf(PSall.t[:, i * 1024:(i + 1) * 1024], f"p1pw{i}", 0)
        wb_.d = [PS[2 * i].d[0], PS[2 * i + 1].d[0]]
        PW.append(wb_)
    pr1 = K.sbuf("p_r0", [128, 512], F32)
    pk1 = K.sbuf("p_k0", [128, 512], F32)
    pv1 = K.sbuf("p_v0", [128, 512], F32)
    pr_, pk_, pv_ = [pr1, pr1], [pk1, pk1], [pv1, pv1]
    dwt = K.sbuf("dwt", [64, 512], BF16)
    dat = K.sbuf("dat", [64, 512], BF16)
    dgs = K.sbuf("dgs", [128, 2, 512], BF16)
    q8 = [K.sbuf(f"q8_{i}", [128, 512], BF16) for i in range(2)]
    vtmp = K.sbuf("vtmp", [128, 512], BF16)
    tmpf = [K.sbuf(f"tmpf{i}", [128, 512], F32) for i in range(2)] if not do_rwkv else [None, None]
    cnt = {"xs": 0, "sq": 0, "w": 0, "ps": 0, "tf": 0}

    def nxt(key, lst):
        b = lst[cnt[key] % len(lst)]
        cnt[key] += 1
        return b

    xT = io["xT"].rearrange("(c p) t -> p c t", p=128)

    rw = RwkvState(K, cst, prm, omk, lw, PS, PW) if do_rwkv else None
    if rw is not None:
        tmpf[:] = [rw.t1, rw.t2]
        sqrt_t, rstd = rw.t3, rw.W
    for g in groups:
        load_group(g)
        K.op("dve", lambda e: e.memset(lastcol[:], 0.0), w=lastcol.all)
        if rw is not None:
            rw.reset()
        w1c = io["w1c"] if g is None else io["w1c"][g]
        if ysc is None:
            ya_dst = lambda pair, qt: io["yaT"][pair * 128:(pair + 1) * 128, qt * 512:(qt + 1) * 512]
            yb_dst = lambda hd, qt: io["ybT"][hd * 64:(hd + 1) * 64, qt * 512:(qt + 1) * 512]
            ydep = None
        else:
            gg = g
            ya_dst = lambda pair, qt: ysc[gg * 256 + pair * 128:gg * 256 + (pair + 1) * 128,
                                          (qt - own_from) * 512:(qt - own_from + 1) * 512]
            yb_dst = lambda hd, qt: ysc[1024 + gg * 256 + hd * 64:1024 + gg * 256 + (hd + 1) * 64,
                                        (qt - own_from) * 512:(qt - own_from + 1) * 512]
            ydep = ysc.d[0]
        _phase1_tiles(K, io, cm, n_tiles, do_rwkv, do_sb, own_from, w1c, ya_dst, yb_dst, ydep, rw, xT, tokm,
                      cst, ones, prm, PS, KT, Vtm, hT, raw, lastcol, wbuf, xs, sqb, sqrt_t, rstd, pr_, pk_, pv_,
                      dwt, dat, dgs, q8, vtmp, tmpf, cnt, nxt)


def _phase1_tiles(K, io, cm, n_tiles, do_rwkv, do_sb, own_from, w1c, ya_dst, yb_dst, ydep, rw, xT, tokm,
                  cst, ones, prm, PS, KT, Vtm, hT, raw, lastcol, wbuf, xs, sqb, sqrt_t, rstd, pr_, pk_, pv_,
                  dwt, dat, dgs, q8, vtmp, tmpf, cnt, nxt):
    for qt in range(n_tiles):
        t0 = qt * 512
        ss = PS[7]
        for c in range(16):
            xb = nxt("xs", xs)
            K.dma("sp", xb[:], xT[:, c, t0:t0 + 512], w=xb.all)
            sq = nxt("sq", sqb)
            K.op("act", lambda e: e.activation(out=sq[:], in_=xb[:], func=AF.Square), r=xb.all, w=sq.all)
            mm(K, ss, ss[:], ones[:], sq[:], start=(c == 0), stop=(c == 15), r=sq.all + ones.all)
        rstd_from_ss(K, ss, sqrt_t, rstd, float(D))
        for c in range(16):
            xb = nxt("xs", xs)
            K.dma("sp", xb[:], xT[:, c, t0:t0 + 512], w=xb.all)
            K.op("dve", lambda e: e.tensor_tensor(out=xb[:], in0=xb[:], in1=rstd[:], op=ALU.mult),
                 r=xb.all + rstd.all, w=xb.all)
            K.op("act", lambda e: e.activation(out=hT[:, c, :], in_=xb[:], func=AF.Identity,
                                                scale=cm.A1[:, c:c + 1], bias=cm.B1[:, c:c + 1]),
                 r=xb.all + cm.A1.all + cm.B1.all, w=[hT.d[c]])
            if tokm is not None:
                if c == 0:
                    K.dma("sp", tokm[:], io["tokmask"][:, t0:t0 + 512], w=tokm.all)
                K.op("dve", lambda e: e.tensor_tensor(out=hT[:, c, :], in0=hT[:, c, :], in1=tokm[:], op=ALU.mult),
                     r=[hT.d[c]] + tokm.all, w=[hT.d[c]])
        own = qt >= own_from
        def do_chunk(ch):
            if (ch < 10 and not do_rwkv) or (ch >= 10 and not do_sb):
                return
            if ch in (10, 11) and not own:
                return
            if ch in (0, 1, 8, 9) and qt < own_from - 1:
                return
            wb = nxt("w", wbuf)
            load_w_cast(K, wb, wb[:], w1c[ch])
            pp = PS[cnt["ps"] % 4]
            cnt["ps"] += 1
            M = {6: 64, 7: 64, 9: 32}.get(ch, 128)
            for kc in range(16):
                mm(K, pp, pp[0:M, :], wb[:, kc, 0:M], hT[:, kc, :], start=(kc == 0), stop=(kc == 15),
                   r=wb.all + [hT.d[kc]])
            if ch < 10:
                rb = raw[ch % 2]
                K.op("dve", lambda e: e.tensor_copy(out=rb[0:M, 0:1], in_=lastcol[0:M, ch:ch + 1]),
                     r=lastcol.all, w=rb.all)
                K.op("act", lambda e: e.activation(out=rb[0:M, 1:513], in_=pp[0:M, :], func=AF.Identity),
                     r=pp.all, w=rb.all)
                df = nxt("tf", tmpf)
                K.op("dve", lambda e: e.tensor_tensor(out=df[0:M, :], in0=rb[0:M, 0:512], in1=rb[0:M, 1:513],
                                                      op=ALU.subtract), r=rb.all, w=df.all)
                if ch < 6:
                    dst = [pr_, pk_, pv_][ch // 2][ch % 2]
                    K.op("dve", lambda e: e.scalar_tensor_tensor(out=dst[:], in0=df[:], scalar=prm[:, ch:ch + 1],
                                                                  in1=rb[:, 1:513], op0=ALU.mult, op1=ALU.add),
                         r=df.all + rb.all + prm.all, w=dst.all)
                else:
                    K.op("dve", lambda e: e.scalar_tensor_tensor(out=df[0:M, :], in0=df[0:M, :],
                                                                  scalar=prm[0:M, ch:ch + 1], in1=rb[0:M, 1:513],
                                                                  op0=ALU.mult, op1=ALU.add),
                         r=df.all + rb.all + prm.all, w=df.all)
                    if ch == 6:
                        K.op("act", lambda e: e.activation(out=dwt[:], in_=df[0:64, :], func=AF.Tanh),
                             r=df.all, w=dwt.all)
                    elif ch == 7:
                        K.op("act", lambda e: e.activation(out=dat[:], in_=df[0:64, :], func=AF.Identity),
                             r=df.all, w=dat.all)
                    elif ch == 8:
                        K.op("act", lambda e: e.activation(out=dgs[:, 0, :], in_=df[:], func=AF.Sigmoid),
                             r=df.all, w=dgs.all)
                    else:
                        K.op("act", lambda e: e.activation(out=dgs[0:32, 1, :], in_=df[0:32, :], func=AF.Sigmoid),
                             r=df.all, w=dgs.all)
                K.op("dve", lambda e: e.tensor_copy(out=lastcol[0:M, ch:ch + 1], in_=rb[0:M, 512:513]),
                     r=rb.all, w=lastcol.all)
            elif ch < 12:
                K.op("act", lambda e: e.activation(out=q8[ch - 10][:], in_=pp[:], func=AF.Identity, scale=0.125),
                     r=pp.all, w=q8[ch - 10].all)
            elif ch < 14:
                K.op("act", lambda e: e.activation(out=KT[ch - 12][:, t0:t0 + 512], in_=pp[:], func=AF.Identity),
                     r=pp.all, w=[KT[ch - 12].d[qt]])
            else:
                pair = ch - 14
                K.op("act", lambda e: e.activation(out=vtmp[:], in_=pp[:], func=AF.Identity), r=pp.all, w=vtmp.all)
                pT = PS[6]
                pTb = pT.t[:].bitcast(BF16)
                for bk in range(4):
                    K.op("pe", lambda e: e.transpose(pTb[:, bk * 128:(bk + 1) * 128], vtmp[:, bk * 128:(bk + 1) * 128],
                                                     cst["ident"][:]),
                         r=vtmp.all + cst["ident"].all, w=pT.all)
                K.op("dve", lambda e: e.tensor_copy(
                    out=Vtm[:, 4 * qt:4 * qt + 4, pair * 128:(pair + 1) * 128],
                    in_=pTb[:, 0:512].rearrange("p (b c) -> p b c", c=128)), r=pT.all, w=[Vtm.d[qt]])
        for ch in (6, 7, 8, 9):
            do_chunk(ch)
        for pair in range(2):
            for ch in (pair, 2 + pair, 4 + pair):
                do_chunk(ch)
            if do_rwkv:
                rw.tile(qt, pair, pr_[pair], pk_[pair], pv_[pair], dwt, dat, dgs, ya_dst(pair, qt) if own else None, own, ydep)
        for ch in range(10, 16):
            do_chunk(ch)
        if do_sb and own:
            K.barrier()
            sb_tile(K, cst, ones, PS, qt, q8, KT, Vtm, yb_dst, ydep, tmpf, rw=rw)
            K.barrier()


_sbst = {}


def sb_tile(K, cst, ones, PS, qt, q8, KT, Vtm, yb_dst, ydep, tmpf, rw=None):
    st = _sbst.get(id(K))
    if st is None:
        st = {}
        if rw is None:
            st["e"] = [K.sbuf(f"sb_e{i}", [128, 512], F32) for i in range(2)]
            st["sp"] = [K.sbuf(f"sb_sp{i}", [128, 512], BF16) for i in range(3)]
            st["ar"] = [K.sbuf(f"sb_ar{i}", [128, 512], F32) for i in range(2)]
            st["att"] = [K.sbuf(f"sb_att{i}", [128, 512], BF16) for i in range(3)]
            st["Cc"] = K.sbuf("sb_Cc", [128, 512], F32)
        else:
            st["e"] = [rw.kk.view(rw.kk[:], "sbv_e0"), rw.cl.view(rw.cl[:], "sbv_e1")]
            st["ar"] = [rw.t1.view(rw.t1[:], "sbv_ar0"), rw.t2.view(rw.t2[:], "sbv_ar1")]
            st["Cc"] = rw.t3.view(rw.t3[:], "sbv_Cc")
            st["sp"] = [rw.AR.view(rw.AR[:, 0, :], "sbv_sp0"), rw.AR.view(rw.AR[:, 1, :], "sbv_sp1"),
                        rw.BK.view(rw.BK[:, 0, :], "sbv_sp2")]
            st["att"] = [rw.BK.view(rw.BK[:, 1, :], "sbv_at0"), rw.pre.view(rw.pre[:, 0, :], "sbv_at1"),
                         rw.pre.view(rw.pre[:, 1, :], "sbv_at2")]
        st["yo"] = [K.sbuf(f"sb_yo{i}", [64, 512], BF16) for i in range(2)]
        st["n"] = 0
        _sbst[id(K)] = st
    t0 = qt * 512
    negtri = cst["negtri"]
    maskd = cst["maskd"]
    Cc = st["Cc"]
    for hd in range(4):
        pair, bp = hd // 2, 64 * (hd % 2)
        qv = q8[pair][bp:bp + 64, :]
        qd = q8[pair].all
        jmax = 4 * qt + 3
        js = list(range(jmax, -1, -1))
        ypsum = PS[6]
        info = {}

        def stage1a(idx):
            j = js[idx]
            d = j - 4 * qt
            kT = KT[pair][bp:bp + 64, j * 128:(j + 1) * 128]
            kd = [KT[pair].d[j // 4]]
            zp = PS[idx % 2]
            mm(K, zp, zp[:], kT, qv, True, True, r=kd + qd)
            e = st["e"][idx % 2]
            sp = st["sp"][idx % 3]
            K.op("act", lambda en: en.activation(out=e[:], in_=zp[:], func=AF.Exp), r=zp.all, w=e.all)
            info[idx] = (j, d, kT, kd, sp, e)

        def stage1b(idx):
            j, d, kT, kd, sp, e = info[idx]
            K.op("act", lambda en: en.activation(out=sp[:], in_=e[:], func=AF.Ln, bias=1.0), r=e.all, w=sp.all)
            if d >= 0:
                K.op("dve", lambda en: en.tensor_tensor(out=sp[:], in0=sp[:], in1=maskd[:, d * 512:(d + 1) * 512],
                                                        op=ALU.mult), r=sp.all + maskd.all, w=sp.all)
            info[idx] = (j, d, kT, kd, sp)

        def stage2(idx):
            j, d, kT, kd, sp = info[idx]
            ap_ = PS[2 + idx % 2]
            cp = PS[4 + idx % 2]
            mm(K, ap_, ap_[:], kT, qv, True, False, r=kd + qd)
            mm(K, ap_, ap_[:], negtri[:], sp[:], False, True, r=sp.all + negtri.all)
            last = (idx == len(js) - 1)
            if not last:
                mm(K, cp, cp[:], ones[:], sp[:], True, True, r=sp.all + ones.all)
            att = st["att"][idx % 3]
            if idx == 0:
                K.op("act", lambda en: en.activation(out=att[:], in_=ap_[:], func=AF.Exp), r=ap_.all, w=att.all)
                if not last:
                    K.op("dve", lambda en: en.tensor_copy(out=Cc[:], in_=cp[:]), r=cp.all, w=Cc.all)
            else:
                ar = st["ar"][idx % 2]
                K.op("dve", lambda en: en.tensor_tensor(out=ar[:], in0=ap_[:], in1=Cc[:], op=ALU.subtract),
                     r=ap_.all + Cc.all, w=ar.all)
                K.op("act", lambda en: en.activation(out=att[:], in_=ar[:], func=AF.Exp), r=ar.all, w=att.all)
                if not last:
                    K.op("dve", lambda en: en.tensor_tensor(out=Cc[:], in0=Cc[:], in1=cp[:], op=ALU.add),
                         r=cp.all + Cc.all, w=Cc.all)
            if d >= 0:
                K.op("dve", lambda en: en.tensor_tensor(out=att[:], in0=att[:], in1=maskd[:, d * 512:(d + 1) * 512],
                                                        op=ALU.mult), r=att.all + maskd.all, w=att.all)
            info[idx] = (j, att)

        def stage3(idx):
            j, att = info[idx]
            mm(K, ypsum, ypsum[0:64, :], Vtm[:, j, hd * 64:(hd + 1) * 64], att[:], idx == 0, idx == len(js) - 1,
               r=att.all + [Vtm.d[j // 4]])

        n = len(js)
        for step in range(n + 2):
            if step < n:
                stage1a(step)
            if 0 <= step - 1 < n:
                stage2(step - 1)
            if step < n:
                stage1b(step)
            if 0 <= step - 2 < n:
                stage3(step - 2)
        yo = st["yo"][st["n"] % 2]
        st["n"] += 1
        K.op("act", lambda en: en.activation(out=yo[:], in_=ypsum[0:64, :], func=AF.Identity), r=ypsum.all, w=yo.all)
        if ydep is None:
            K.dma("sp", yb_dst(hd, qt), yo[:], r=yo.all, sem_dep=yo.d[0], is_output=True)
        else:
            K.scratch_events.append(K.dma("sp", yb_dst(hd, qt), yo[:], r=yo.all, sem_dep=yo.d[0]))


class RwkvState:
    def __init__(self, K, cst, prm, omk, lw, PS, PW):
        self.K, self.cst, self.prm, self.omk, self.lw, self.PS, self.PW = K, cst, prm, omk, lw, PS, PW
        sb = K.sbuf
        self.St = [[sb(f"St{p}_{i}", [128, 64], BF16) for i in range(3)] for p in range(2)]
        for p in range(2):
            K.op("dve", lambda e: e.memset(self.St[p][0][:], 0.0), w=self.St[p][0].all)
        self.stn = [0, 0]
        f = lambda n: sb("rw_" + n, [128, 512], F32)
        self.ld, self.a, self.g, self.kk, self.kf, self.cl, self.W = (f(n) for n in ("ld", "a", "g", "kk", "kf", "cl", "W"))
        self.t1, self.t2, self.t3 = f("t1"), f("t2"), f("t3")
        self.AR = sb("rw_AR", [128, 2, 512], BF16)
        self.BK = sb("rw_BK", [128, 2, 512], BF16)
        self.pre = sb("rw_pre", [128, 3, 512], BF16)
        self.TT = sb("rw_TT", [128, 4, 512], BF16)
        self.X1 = sb("rw_X1", [128, 2048], BF16)
        self.X2 = sb("rw_X2", [128, 2048], BF16)
        self.X3 = sb("rw_X3", [128, 1024], BF16)
        self.A = [sb(f"rw_A{i}", [128, 1024], BF16) for i in range(2)]
        self.AT = [sb(f"rw_AT{i}", [128, 1024], BF16) for i in range(2)]
        self.Tt = [sb(f"rw_Tt{i}", [128, 1024], BF16) for i in range(2)]
        self.X1b = sb("rw_X1b", [128, 1024], BF16)
        self.X2b = sb("rw_X2b", [128, 1024], BF16)
        self.Rbd = sb("rw_Rbd", [128, 8, 2, 64], BF16)
        self.Pbd = sb("rw_Pbd", [128, 8, 128], BF16)
        self.Ghb = sb("rw_Ghb", [128, 8, 128], BF16)
        self.Gbd = sb("rw_Gbd", [128, 8, 128], BF16)
        self.BhB = sb("rw_BhB", [128, 8, 128], BF16)
        self.KhB = sb("rw_KhB", [128, 8, 128], BF16)
        for b_ in (self.Rbd, self.Pbd, self.BhB, self.KhB):
            K.op("dve", lambda e: e.memset(b_[:], 0.0), w=b_.all)
        self.yo = [sb(f"rw_yo{i}", [128, 512], BF16) for i in range(2)]
        self.nyo = 0

    def reset(self):
        for p in range(2):
            cur = self.St[p][self.stn[p] % 3]
            self.K.op("dve", lambda e: e.memset(cur[:], 0.0), w=cur.all)

    def tile(self, qt, pair, p_r, p_k, p_v, dwt, dat, dgs, ya_ap, emit_y=True, ydep=None):
        K, cst, prm, omk, lw, PS = self.K, self.cst, self.prm, self.omk, self.lw, self.PS
        t0 = qt * 512
        ps_ = slice(pair * 128, (pair + 1) * 128)
        col = lambda i: prm[:, i + pair:i + pair + 1]
        ld, a, g, kk, kf, cl, W = self.ld, self.a, self.g, self.kk, self.kf, self.cl, self.W
        t1, t2, t3 = self.t1, self.t2, self.t3
        AR, BK, pre, TT = self.AR, self.BK, self.pre, self.TT
        DV = lambda fn, r, w: K.op("dve", fn, r=r, w=w)
        AC = lambda fn, r, w: K.op("act", fn, r=r, w=w)
        mm(K, PS[0], PS[0][:], lw[0:64, 0, ps_], dwt[0:64, :], True, True, r=lw.all + dwt.all)
        mm(K, PS[1], PS[1][:], lw[0:64, 1, ps_], dat[0:64, :], True, True, r=lw.all + dat.all)
        if emit_y:
            mm(K, PS[2], PS[2][:], lw[:, 2, ps_], dgs[:, 0, :], True, False, r=lw.all + dgs.all)
            mm(K, PS[2], PS[2][:], lw[0:32, 3, ps_], dgs[0:32, 1, :], False, True, r=lw.all + dgs.all)
        AC(lambda e: e.activation(out=t1[:], in_=PS[0][:], func=AF.Sigmoid, bias=col(10)), PS[0].all + prm.all, t1.all)
        DV(lambda e: e.tensor_scalar(out=ld[:], in0=t1[:], scalar1=-0.6065306597126334, scalar2=None, op0=ALU.mult),
           t1.all, ld.all)
        AC(lambda e: e.activation(out=a[:], in_=PS[1][:], func=AF.Sigmoid, bias=col(12)), PS[1].all + prm.all, a.all)
        if emit_y:
            AC(lambda e: e.activation(out=g[:], in_=PS[2][:], func=AF.Identity), PS[2].all, g.all)
        DV(lambda e: e.tensor_scalar(out=t1[:], in0=p_k[:], scalar1=col(14), scalar2=None, op0=ALU.mult),
           p_k.all + prm.all, t1.all)
        AC(lambda e: e.activation(out=t2[:], in_=t1[:], func=AF.Square), t1.all, t2.all)
        mm(K, PS[3], PS[3][:], cst["bd1"][:], t2[:], True, True, r=cst["bd1"].all + t2.all)
        DV(lambda e: e.tensor_scalar(out=t2[:], in0=PS[3][:], scalar1=1e-24, scalar2=None, op0=ALU.max),
           PS[3].all, t2.all)
        AC(lambda e: e.activation(out=t2[:], in_=t2[:], func=AF.Sqrt), t2.all, t2.all)
        DV(lambda e: e.reciprocal(out=t2[:], in_=t2[:]), t2.all, t2.all)
        DV(lambda e: e.tensor_tensor(out=kk[:], in0=t1[:], in1=t2[:], op=ALU.mult), t1.all + t2.all, kk.all)
        DV(lambda e: e.tensor_scalar(out=t1[:], in0=a[:], scalar1=col(16), scalar2=omk[:, pair:pair + 1],
                                     op0=ALU.mult, op1=ALU.add), a.all + prm.all + omk.all, t1.all)
        DV(lambda e: e.tensor_tensor(out=kf[:], in0=p_k[:], in1=t1[:], op=ALU.mult), p_k.all + t1.all, kf.all)
        DV(lambda e: e.tensor_tensor_scan(out=cl[:], data0=cst["scanmask"][:], data1=ld[:], initial=0.0,
                                          op0=ALU.mult, op1=ALU.add), cst["scanmask"].all + ld.all, cl.all)
        AC(lambda e: e.activation(out=W[:], in_=cl[:], func=AF.Exp), cl.all, W.all)
        DV(lambda e: e.tensor_tensor(out=t3[:], in0=kk[:], in1=a[:], op=ALU.mult), kk.all + a.all, t3.all)
        AC(lambda e: e.activation(out=t1[:], in_=cl[:], func=AF.Exp, scale=-1.0), cl.all, t1.all)
        DV(lambda e: e.tensor_tensor(out=BK[:, 0, :], in0=t3[:], in1=t1[:], op=ALU.mult), t3.all + t1.all, BK.all)
        DV(lambda e: e.tensor_tensor(out=BK[:, 1, :], in0=kf[:], in1=t1[:], op=ALU.mult), kf.all + t1.all, BK.all)
        DV(lambda e: e.tensor_tensor(out=t2[:], in0=cl[:], in1=ld[:], op=ALU.subtract), cl.all + ld.all, t2.all)
        AC(lambda e: e.activation(out=t2[:], in_=t2[:], func=AF.Exp), t2.all, t2.all)
        DV(lambda e: e.scalar_tensor_tensor(out=AR[:, 0, :], in0=kk[:], scalar=-1.0, in1=t2[:], op0=ALU.mult,
                                            op1=ALU.mult), kk.all + t2.all, AR.all)
        if emit_y:
            DV(lambda e: e.tensor_tensor(out=AR[:, 1, :], in0=p_r[:], in1=W[:], op=ALU.mult), p_r.all + W.all, AR.all)
        cl3 = cl[:].rearrange("p (c s) -> p c s", s=64)
        t13 = t1[:].rearrange("p (c s) -> p c s", s=64)
        DV(lambda e: e.tensor_tensor(out=t13, in0=cl3[:, :, 63:64].to_broadcast([128, 8, 64]), in1=cl3,
                                     op=ALU.subtract), cl.all, t1.all)
        AC(lambda e: e.activation(out=t1[:], in_=t1[:], func=AF.Exp), t1.all, t1.all)
        DV(lambda e: e.tensor_tensor(out=pre[:, 0, :], in0=t3[:], in1=t1[:], op=ALU.mult), t3.all + t1.all, pre.all)
        DV(lambda e: e.tensor_tensor(out=pre[:, 1, :], in0=kf[:], in1=t1[:], op=ALU.mult), kf.all + t1.all, pre.all)
        AC(lambda e: e.activation(out=pre[:, 2, :], in_=p_v[:], func=AF.Identity), p_v.all, pre.all)
        if STOP_AT == 1:
            return
        ident = cst["ident"]
        srcs = [AR[:, 0, :], pre[:, 0, :], pre[:, 1, :], pre[:, 2, :]]
        BhB, KhB = self.BhB, self.KhB
        for qi in range(4):
            pT = PS[6 + qi % 2]
            pTb = pT.t[:].bitcast(BF16)
            for bk in range(4):
                K.op("pe", lambda e: e.transpose(pTb[:, bk * 128:(bk + 1) * 128], srcs[qi][:, bk * 128:(bk + 1) * 128],
                                                 ident[:]), r=AR.all + pre.all + ident.all, w=pT.all)
            AC(lambda e: e.activation(out=TT[:, qi, :], in_=pTb[:, 0:512], func=AF.Identity), pT.all, TT.all)
            if qi in (1, 2):
                BD = BhB if qi == 1 else KhB
                for h2 in range(2):
                    hs_ = slice(h2 * 64, h2 * 64 + 64)
                    AC(lambda e: e.activation(out=BD[hs_, :, h2 * 64:h2 * 64 + 64],
                                              in_=pTb[hs_, 0:512].rearrange("p (b j) -> p b j", j=64),
                                              func=AF.Identity), pT.all, BD.all)
        if STOP_AT == 2:
            return
        X1, X2, X3, X1b, X2b = self.X1, self.X2, self.X3, self.X1b, self.X2b
        Rbd, Pbd, Ghb, Gbd = self.Rbd, self.Pbd, self.Ghb, self.Gbd
        X1s = X1[:].rearrange("p (s w x) -> p s w x", w=2, x=128)
        X2s = X2[:].rearrange("p (s w x) -> p s w x", w=2, x=128)
        X3s = X3[:].rearrange("p (s x) -> p s x", x=128)
        PW = self.PW

        def sl_iter():
            for blk in range(4):
                for hd in range(2):
                    yield blk, hd, blk * 2 + hd, slice(hd * 64, hd * 64 + 64), slice(blk * 128, blk * 128 + 128)
        for blk, hd, sl, hp, pc in sl_iter():
            b1 = PS[hd * 2 + blk // 2]
            b2 = PS[4 + hd * 2 + blk // 2]
            co = (blk % 2) * 256
            mm(K, b1, b1[:, co:co + 256], AR[hp, 0, pc], BK[hp, :, pc], True, True, r=AR.all + BK.all)
            if emit_y:
                mm(K, b2, b2[:, co:co + 256], BK[hp, 0, pc], AR[hp, :, pc], True, True, r=AR.all + BK.all)
            else:
                mm(K, b2, b2[:, co:co + 128], BK[hp, 0, pc], AR[hp, 0, pc], True, True, r=AR.all + BK.all)
        m1v = cst["mask1"][:].rearrange("p (c w x) -> p c w x", w=2, x=128)
        m2v = cst["mask2"][:].rearrange("p (c w x) -> p c w x", w=2, x=128)
        X1v = X1[:].rearrange("p (b h w x) -> p b h w x", h=2, w=2, x=128)
        X2v = X2[:].rearrange("p (b h w x) -> p b h w x", h=2, w=2, x=128)
        X3v = X3[:].rearrange("p (b h x) -> p b h x", h=2, x=128)
        for hd in range(2):
            for bb in range(2):
                b1 = PS[hd * 2 + bb]
                b2 = PS[4 + hd * 2 + bb]
                DV(lambda e: e.tensor_tensor(out=X1v[:, bb * 2:bb * 2 + 2, hd],
                                             in0=b1[:].rearrange("p (c w x) -> p c w x", w=2, x=128), in1=m1v,
                                             op=ALU.mult), b1.all + cst["mask1"].all, X1.all)
                if emit_y:
                    DV(lambda e: e.tensor_tensor(out=X2v[:, bb * 2:bb * 2 + 2, hd],
                                                 in0=b2[:].rearrange("p (c w x) -> p c w x", w=2, x=128), in1=m2v,
                                                 op=ALU.mult), b2.all + cst["mask2"].all, X2.all)
                else:
                    DV(lambda e: e.tensor_tensor(out=X2v[:, bb * 2:bb * 2 + 2, hd, 0, :],
                                                 in0=b2[:].rearrange("p (c w x) -> p c w x", w=2, x=128)[:, :, 0, :],
                                                 in1=m2v[:, :, 0, :], op=ALU.mult), b2.all + cst["mask2"].all, X2.all)
        if emit_y:
            for blk, hd, sl, hp, pc in sl_iter():
                b3 = PS[hd * 2]
                mm(K, b3, b3[:, blk * 128:(blk + 1) * 128], BK[hp, 1, pc], AR[hp, 1, pc], True, True, r=AR.all + BK.all)
            for hd in range(2):
                b3 = PS[hd * 2]
                DV(lambda e: e.tensor_tensor(out=X3v[:, :, hd], in0=b3[:].rearrange("p (c x) -> p c x", x=128),
                                             in1=cst["mask3x2"][:].rearrange("p (c x) -> p c x", x=128),
                                             op=ALU.mult), b3.all + cst["mask3x2"].all, X3.all)
        Tt = self.Tt
        Tcur = Tt[0]
        DV(lambda e: e.tensor_tensor(out=Tcur[:].rearrange("p (s x) -> p s x", x=128), in0=X1s[:, :, 0, :],
                                     in1=cst["id128x8"][:].rearrange("p (s x) -> p s x", x=128), op=ALU.add),
           X1.all + cst["id128x8"].all, Tcur.all)
        Acur = (lambda sl: X2s[:, sl, 0, :], X2.all)
        ATcur = (lambda sl: X1s[:, sl, 0, :], X1.all)
        ti = 0
        WA_, WAT_, WT_ = PW[0], PW[1], PW[2]
        for k in range(6):
            do_sq = k <= 4
            do_sqT = k <= 3
            do_T = k >= 1
            for sl in range(8):
                so = slice(sl * 128, sl * 128 + 128)
                if do_sq:
                    mm(K, WA_, WA_[:, so], ATcur[0](sl), Acur[0](sl), True, True, r=Acur[1] + ATcur[1])
                if do_sqT:
                    mm(K, WAT_, WAT_[:, so], Acur[0](sl), ATcur[0](sl), True, True, r=Acur[1] + ATcur[1])
                if do_T:
                    mm(K, WT_, WT_[:, so], Acur[0](sl), Tcur[:, so], True, True, r=Acur[1] + Tcur.all)
            if do_T:
                Tn = Tt[(ti + 1) % 2]
                DV(lambda e: e.tensor_tensor(out=Tn[:], in0=WT_[:], in1=Tcur[:], op=ALU.add),
                   WT_.all + Tcur.all, Tn.all)
                Tcur = Tn
                ti += 1
            if do_sq:
                An = self.A[k % 2]
                AC(lambda e: e.activation(out=An[:], in_=WA_[:], func=AF.Identity), WA_.all, An.all)
                if do_sqT:
                    ATn = self.AT[k % 2]
                    AC(lambda e: e.activation(out=ATn[:], in_=WAT_[:], func=AF.Identity), WAT_.all, ATn.all)
                    ATcur = ((lambda b: (lambda sl: b[:, sl * 128:sl * 128 + 128]))(ATn), ATn.all)
                Acur = ((lambda b: (lambda sl: b[:, sl * 128:sl * 128 + 128]))(An), An.all)
        WX1, WX2 = PW[3], PW[0]
        for blk, hd, sl, hp, pc in sl_iter():
            so = slice(sl * 128, sl * 128 + 128)
            if emit_y:
                mm(K, WX1, WX1[:, so], Tcur[:, so], X2s[:, sl, 1, :], True, True, r=Tcur.all + X2.all)
            mm(K, WX2, WX2[:, so], Tcur[:, so], BhB[:, sl, :], True, True, r=Tcur.all + BhB.all)
        if emit_y:
            AC(lambda e: e.activation(out=X1b[:], in_=WX1[:], func=AF.Identity), WX1.all, X1b.all)
        AC(lambda e: e.activation(out=X2b[:], in_=WX2[:], func=AF.Identity), WX2.all, X2b.all)
        WP, WGh, WG = PW[2], PW[3], PW[0]
        p3v = WP[:].rearrange("p (c x) -> p c x", x=128)
        for blk, hd, sl, hp, pc in sl_iter():
            so = slice(sl * 128, sl * 128 + 128)
            atT = TT[:, 0, blk * 128 + hd * 64:blk * 128 + hd * 64 + 64]
            if emit_y:
                mm(K, PS[2], PS[2][hp, blk * 128:(blk + 1) * 128], atT, X1b[:, so], True, True, r=TT.all + X1b.all)
            for e2 in range(2):
                mm(K, WP, p3v[hp, blk * 2 + e2, hd * 64:hd * 64 + 64], atT,
                   X2b[:, sl * 128 + e2 * 64:sl * 128 + e2 * 64 + 64], True, True, r=TT.all + X2b.all)
            if emit_y:
                mm(K, WGh, WGh[:, so], X1s[:, sl, 1, :], X1b[:, so], True, True, r=X1.all + X1b.all)
            mm(K, WG, WG[:, so], X1s[:, sl, 1, :], X2b[:, so], True, True, r=X1.all + X2b.all)
        for hd in range(2):
            hp = slice(hd * 64, hd * 64 + 64)
            if emit_y:
                DV(lambda e: e.tensor_tensor(out=Rbd[hp, :, hd, :],
                                             in0=PS[2][hp, :].rearrange("p (c t) -> p c t", t=64),
                                             in1=AR[hp, 1, :].rearrange("p (c t) -> p c t", t=64),
                                             op=ALU.add), PS[2].all + AR.all, Rbd.all)
            AC(lambda e: e.activation(out=Pbd[hp, :, hd * 64:hd * 64 + 64],
                                      in_=p3v[hp, :, hd * 64:hd * 64 + 64], func=AF.Identity), WP.all, Pbd.all)
        if emit_y:
            DV(lambda e: e.tensor_tensor(out=Ghb[:], in0=WGh[:].rearrange("p (s x) -> p s x", x=128), in1=X3s,
                                         op=ALU.add), WGh.all + X3.all, Ghb.all)
        DV(lambda e: e.tensor_tensor(out=Gbd[:], in0=WG[:].rearrange("p (s x) -> p s x", x=128), in1=KhB[:], op=ALU.add),
           WG.all + KhB.all, Gbd.all)
        PY, PSt = PS[6], PS[7]
        for c in range(8):
            blk, e2 = c // 2, c % 2
            Sc = self.St[pair][self.stn[pair] % 3]
            Sn = self.St[pair][(self.stn[pair] + 1) % 3]
            self.stn[pair] += 1
            cc = slice(c * 64, c * 64 + 64)
            mm(K, PSt, PSt[:, cc], Pbd[:, c, :], Sc[:], True, False, r=Pbd.all + Sc.all)
            for hd in range(2):
                hp = slice(hd * 64, hd * 64 + 64)
                vT = TT[:, 3, blk * 128 + hd * 64:blk * 128 + hd * 64 + 64]
                es_ = slice(e2 * 64, e2 * 64 + 64)
                if emit_y:
                    mm(K, PY, PY[hp, cc], Sc[:], Rbd[:, c, hd, :], True, False, r=Sc.all + Rbd.all)
                    mm(K, PY, PY[hp, cc], vT, Ghb[:, blk * 2 + hd, es_], False, True, r=TT.all + Ghb.all)
                mm(K, PSt, PSt[hp, cc], Gbd[:, blk * 2 + hd, es_], vT, False, True, r=TT.all + Gbd.all)
            DV(lambda e: e.scalar_tensor_tensor(out=Sn[:], in0=Sc[:], scalar=W[:, c * 64 + 63:c * 64 + 64],
                                                in1=PSt[:, cc], op0=ALU.mult, op1=ALU.add),
               Sc.all + W.all + PSt.all, Sn.all)
        if not emit_y:
            return
        bd64, bd1 = cst["bd64"], cst["bd1"]
        AC(lambda e: e.activation(out=t1[:], in_=PY[:], func=AF.Identity), PY.all, t1.all)
        AC(lambda e: e.activation(out=t2[:], in_=PY[:], func=AF.Square), PY.all, t2.all)
        mm(K, PS[0], PS[0][:], bd64[:], t1[:], True, True, r=bd64.all + t1.all)
        mm(K, PS[1], PS[1][:], bd64[:], t2[:], True, True, r=bd64.all + t2.all)
        AC(lambda e: e.activation(out=t2[:], in_=PS[0][:], func=AF.Square), PS[0].all, t2.all)
        DV(lambda e: e.tensor_tensor(out=t2[:], in0=PS[1][:], in1=t2[:], op=ALU.subtract), PS[1].all + t2.all, t2.all)
        DV(lambda e: e.tensor_scalar(out=t2[:], in0=t2[:], scalar1=0.0, scalar2=None, op0=ALU.max), t2.all, t2.all)
        AC(lambda e: e.activation(out=t2[:], in_=t2[:], func=AF.Sqrt, bias=GN_EPS), t2.all, t2.all)
        DV(lambda e: e.reciprocal(out=t2[:], in_=t2[:]), t2.all, t2.all)
        DV(lambda e: e.tensor_tensor(out=t1[:], in0=t1[:], in1=PS[0][:], op=ALU.subtract), t1.all + PS[0].all, t1.all)
        DV(lambda e: e.tensor_tensor(out=t1[:], in0=t1[:], in1=t2[:], op=ALU.mult), t1.all + t2.all, t1.all)
        AC(lambda e: e.activation(out=t1[:], in_=t1[:], func=AF.Identity, scale=col(20), bias=col(22)),
           t1.all + prm.all, t1.all)
        DV(lambda e: e.scalar_tensor_tensor(out=t3[:], in0=p_r[:], scalar=col(18), in1=kf[:], op0=ALU.mult,
                                            op1=ALU.mult), p_r.all + kf.all + prm.all, t3.all)
        mm(K, PS[2], PS[2][:], bd1[:], t3[:], True, True, r=bd1.all + t3.all)
        DV(lambda e: e.tensor_tensor(out=t3[:], in0=PS[2][:], in1=p_v[:], op=ALU.mult), PS[2].all + p_v.all, t3.all)
        DV(lambda e: e.tensor_tensor(out=t1[:], in0=t1[:], in1=t3[:], op=ALU.add), t1.all + t3.all, t1.all)
        yo = self.yo[self.nyo % 2]
        self.nyo += 1
        DV(lambda e: e.tensor_tensor(out=yo[:], in0=t1[:], in1=g[:], op=ALU.mult), t1.all + g.all, yo.all)
        if ydep is None:
            K.dma("sp", ya_ap, yo[:], r=yo.all, sem_dep=yo.d[0], is_output=True)
        else:
            K.scratch_events.append(K.dma("sp", ya_ap, yo[:], r=yo.all, sem_dep=yo.d[0]))


def declare_phase1_io(nc, io):
    for name, (n, dt) in CONST_SHAPES.items():
        io["c_" + name] = declare(nc, "c_" + name, [128, n], dt, "ExternalInput")
    io["prm"] = declare(nc, "prm", [128, 32], F32, "ExternalInput")
    io["w2s"] = declare(nc, "w2s", [64, 256], F32, "ExternalInput")
    io["a2s"] = declare(nc, "a2s", [64, 256], F32, "ExternalInput")
    io["g2s"] = declare(nc, "g2s", [160, 256], F32, "ExternalInput")
    io["xT"] = declare(nc, "xT", [D, S], F32, "ExternalInput")
    io["w1c"] = declare(nc, "w1c", [16, 128, 16, 128], F32, "ExternalInput")


def build_phase1_only(n_tiles=16, do_rwkv=True, do_sb=True):
    nc = bass.Bass("TRN2", target_bir_lowering=False)
    io = {}
    io["cT"] = declare(nc, "cT", [128, 16], F32, "ExternalInput")
    io["b_adaT"] = declare(nc, "b_adaT", [128, 96], F32, "ExternalInput")
    io["norm_gT"] = declare(nc, "norm_gT", [128, 64], F32, "ExternalInput")
    io["w_ada"] = declare(nc, "w_ada", [D, 6 * D], F32, "ExternalInput")
    declare_phase1_io(nc, io)
    io["yaT"] = declare(nc, "yaT", [256, S], BF16, "ExternalOutput")
    io["ybT"] = declare(nc, "ybT", [256, S], BF16, "ExternalOutput")
    K = Ctx(nc)
    with K.es:
        cm = setup_common(K, io, need_f=False)
        phase1(K, io, cm, n_tiles=n_tiles, do_rwkv=do_rwkv, do_sb=do_sb)
        K.finish()
    print("phase1: ninst", K.ninst, "nwaits", K.nwaits, "nsems", K.nsems)
    return nc


def phase1_inputs(inp, b, g, consts, light=False):
    cs = slice(256 * g, 256 * g + 256)
    W = inp["w_in"][0]
    def colsel(c0):
        return W[:, c0:c0 + 1024][:, cs]
    blocks = []
    for base in (0, 1024, 2048):
        w = colsel(base)
        blocks += [w[:, 0:128], w[:, 128:256]]
    def pad(w):
        out = np.zeros((D, 128), np.float32)
        out[:, :w.shape[1]] = w
        return out
    blocks += [pad(W[:, 3072:3136]), pad(W[:, 3136:3200]), W[:, 3200:3328], pad(W[:, 3328:3360])]
    for base in (3360, 4384, 5408):
        w = colsel(base)
        blocks += [w[:, 0:128], w[:, 128:256]]
    w1c = np.stack([np.ascontiguousarray(blk.reshape(16, 128, 128).transpose(1, 0, 2)) for blk in blocks])
    mu = inp["mu_shift"][0]
    prm = np.zeros((128, 32), np.float32)
    def padv(v):
        out = np.zeros(128, np.float32)
        out[:v.shape[0]] = v
        return out
    mus = [mu[0:1024][cs][0:128], mu[0:1024][cs][128:256], mu[1024:2048][cs][0:128], mu[1024:2048][cs][128:256],
           mu[2048:3072][cs][0:128], mu[2048:3072][cs][128:256], padv(mu[3072:3136]), padv(mu[3136:3200]),
           mu[3200:3328], padv(mu[3328:3360])]
    for i, v in enumerate(mus):
        prm[:, i] = v
    for j, key in enumerate(["w0", "a0", "k_k", "k_a"]):
        v = inp[key][0][cs]
        prm[:, 10 + 2 * j] = v[0:128]
        prm[:, 11 + 2 * j] = v[128:256]
    rk = inp["r_k"][0].reshape(-1)[cs]
    prm[:, 18], prm[:, 19] = rk[0:128], rk[128:256]
    for j, key in enumerate(["ln_x_w", "ln_x_b"]):
        v = inp[key][0][cs]
        prm[:, 20 + 2 * j] = v[0:128]
        prm[:, 21 + 2 * j] = v[128:256]
    m = {} if light else common_inputs(inp, b)
    if not light:
        m.update({"c_" + k: v for k, v in consts.items()})
        m["xT"] = np.ascontiguousarray(inp["x"][b].T)
    m["prm"] = prm
    m["w2s"] = np.ascontiguousarray(inp["w2"][0][:, cs])
    m["a2s"] = np.ascontiguousarray(inp["a2"][0][:, cs])
    m["g2s"] = np.ascontiguousarray(inp["g2"][0][:, cs])
    m["w1c"] = w1c
    return m


def build_fused(n_tiles=16, own_from=12, n_pass=2):
    nc = bass.Bass("TRN2", target_bir_lowering=False)
    io = {}
    io["cT"] = declare(nc, "cT", [128, 16], F32, "ExternalInput")
    io["b_adaT"] = declare(nc, "b_adaT", [128, 96], F32, "ExternalInput")
    io["norm_gT"] = declare(nc, "norm_gT", [128, 64], F32, "ExternalInput")
    io["w_ada"] = declare(nc, "w_ada", [D, 6 * D], F32, "ExternalInput")
    for name, (n, dt) in CONST_SHAPES.items():
        io["c_" + name] = declare(nc, "c_" + name, [128, n], dt, "ExternalInput")
    io["prm"] = declare(nc, "prm", [4, 128, 32], F32, "ExternalInput")
    io["w2s"] = declare(nc, "w2s", [4, 64, 256], F32, "ExternalInput")
    io["a2s"] = declare(nc, "a2s", [4, 64, 256], F32, "ExternalInput")
    io["g2s"] = declare(nc, "g2s", [4, 160, 256], F32, "ExternalInput")
    io["xT"] = declare(nc, "xT", [D, S], F32, "ExternalInput")
    io["tokmask"] = declare(nc, "tokmask", [128, S], BF16, "ExternalInput")
    io["w1c"] = declare(nc, "w1c", [4, 16, 128, 16, 128], F32, "ExternalInput")
    io["x2T"] = declare(nc, "x2T", [D, 2048], F32, "ExternalInput")
    io["w_gate"] = declare(nc, "w_gate", [D, 2 * D], F32, "ExternalInput")
    io["w_up_rwkv"] = declare(nc, "w_up_rwkv", [C, D], F32, "ExternalInput")
    io["w_up_sb"] = declare(nc, "w_up_sb", [C, D], F32, "ExternalInput")
    io["w_out"] = declare(nc, "w_out", [D, D], F32, "ExternalInput")
    io["w_mlp_in"] = declare(nc, "w_mlp_in", [D, DFF], F32, "ExternalInput")
    io["w_mlp_out"] = declare(nc, "w_mlp_out", [DFF, D], F32, "ExternalInput")
    io["outT"] = declare(nc, "outT", [D, 2048], F32, "ExternalOutput")
    io["x1s"] = declare(nc, "x1s", [D, 2048], F32, "Internal")
    K = Ctx(nc)
    with K.es:
        cm = setup_common(K, io, need_f=True)
        ysc = K.dram("ysc", [2048, 2048], BF16)
        with K.scope():
            phase1(K, io, cm, n_tiles=n_tiles, groups=(0, 1, 2, 3), own_from=own_from, ysc=ysc)
        for E_ in K.engs.values():
            for ev_ in K.scratch_events:
                E_.wait_for(ev_)
        io["yT"] = ysc.t
        phase2(K, io, cm, n_pass=n_pass)
        K.finish()
    print("fused: ninst", K.ninst, "nwaits", K.nwaits, "nsems", K.nsems)
    return nc


def fused_inputs(inp, b, q, consts, n_tiles=16, own_from=12):
    own = (n_tiles - own_from) * 512
    n_real = (q + 1) * own
    n_seq = n_tiles * 512
    xT = np.zeros((D, S), np.float32)
    xT[:, n_seq - n_real:n_seq] = inp["x"][b, 0:n_real].T
    tokmask = np.zeros((128, S), ml_dtypes.bfloat16)
    tokmask[:, n_seq - n_real:n_seq] = 1.0
    m = common_inputs(inp, b)
    m.update({"c_" + k: v for k, v in consts.items()})
    per_g = [phase1_inputs(inp, b, g, consts, light=True) for g in range(4)]
    for key in ("prm", "w2s", "a2s", "g2s", "w1c"):
        m[key] = np.stack([pg[key] for pg in per_g])
    m["xT"] = xT
    m["tokmask"] = tokmask
    x2T = np.zeros((D, 2048), np.float32)
    x2T[:, 0:own] = inp["x"][b, q * own:(q + 1) * own].T
    m["x2T"] = x2T
    m["w_gate"] = np.ascontiguousarray(inp["w_in"][0][:, 6432:])
    for k in ["w_up_rwkv", "w_up_sb", "w_out", "w_mlp_in", "w_mlp_out"]:
        m[k] = inp[k][0]
    return m


def kernel(**inputs):
    inp = {k: np.asarray(v) for k, v in inputs.items()}
    consts = host_consts()
    nc = build_fused()
    in_maps = [fused_inputs(inp, c // 4, c % 4, consts) for c in range(NCORES)]
    res = run_bass_kernel_spmd(nc, in_maps, core_ids=list(range(NCORES)))
    out = np.zeros((NB, S, D), np.float32)
    for c in range(NCORES):
        b, q = c // 4, c % 4
        out[b, q * 2048:(q + 1) * 2048] = res.results[c]["outT"].T
    return out
```

```python
import contextlib
import numpy as np
import ml_dtypes
import concourse.bass as bass
import concourse.mybir as mybir
from concourse.bass_utils import run_bass_kernel_spmd

F32 = mybir.dt.float32
BF16 = mybir.dt.bfloat16
AF = mybir.ActivationFunctionType
ALU = mybir.AluOpType

D = 2048
S = 8192
NB = 2
C = 1024
DFF = 8192
NCORES = 8
NORM_EPS = 1e-6
GN_EPS = 64e-5
SEM_LIMIT = 30000
STOP_AT = 0


class Dep:
    __slots__ = ("name", "lw", "rd", "dsem", "dcnt")

    def __init__(self, name=""):
        self.name = name
        self.lw = []
        self.rd = {}
        self.dsem = None
        self.dcnt = 0


class Eng:
    def __init__(self, K, name, h):
        self.K = K
        self.name = name
        self.h = h
        self.sem = None
        self.cnt = 0
        self.seen = {}
        self.nsem = 0

    def new_sem(self):
        self.sem = self.K.es.enter_context(self.K.nc.semaphore(f"e_{self.name}_{self.nsem}"))
        self.nsem += 1
        self.K.nsems += 1
        self.cnt = 0

    def wait_for(self, ev):
        sem, val, _ = ev
        key = id(sem)
        if self.seen.get(key, 0) < val:
            self.h.wait_ge(sem, val)
            self.seen[key] = val
            self.K.nwaits += 1


class Buf:
    def __init__(self, t, name, nparts=1):
        self.t = t
        self.name = name
        self.d = [Dep(f"{name}.{i}") for i in range(nparts)]

    def __getitem__(self, k):
        return self.t[k]

    def view(self, ap, name):
        b = Buf(ap, name, 1)
        return b

    @property
    def all(self):
        return list(self.d)


class Ctx:
    def __init__(self, nc):
        self.nc = nc
        self.es = contextlib.ExitStack()
        self.sem_es = self.es
        self.scopes = []
        self.nsems = 0
        self.nwaits = 0
        self.ninst = 0
        self.engs = {
            "pe": Eng(self, "pe", nc.tensor),
            "act": Eng(self, "act", nc.scalar),
            "dve": Eng(self, "dve", nc.vector),
            "pool": Eng(self, "pool", nc.gpsimd),
            "sp": Eng(self, "sp", nc.sync),
        }
        for e in self.engs.values():
            e.new_sem()
        self.out_events = []
        self.scratch_events = []

    def sbuf(self, name, shape, dt, nparts=1):
        es = self.scopes[-1][0] if self.scopes else self.es
        t = es.enter_context(self.nc.sbuf_tensor(name, list(shape), dt))
        b = Buf(t, name, nparts)
        if self.scopes:
            self.scopes[-1][1].append(b)
        return b

    @contextlib.contextmanager
    def scope(self):
        es = contextlib.ExitStack()
        self.scopes.append((es, []))
        try:
            yield
        finally:
            _, bufs = self.scopes.pop()
            deps = [d for b in bufs for d in b.d]
            self.barrier(deps)
            es.close()

    def psum(self, name, shape, dt=F32, nparts=1):
        es = self.scopes[-1][0] if self.scopes else self.es
        t = es.enter_context(self.nc.psum_tensor(name, list(shape), dt))
        b = Buf(t, name, nparts)
        if self.scopes:
            self.scopes[-1][1].append(b)
        return b

    def dram(self, name, shape, dt, kind="Internal", nparts=1):
        t = self.nc.dram_tensor(name, list(shape), dt, kind=kind)
        return Buf(t.ap(), name, nparts)

    def _deps(self, E, r, w, same_raw=True):
        for d in r:
            for ev in d.lw:
                if ev[2] == E.name and not same_raw:
                    continue
                E.wait_for(ev)
        for d in w:
            for ev in d.lw:
                if ev[2] == E.name and not same_raw:
                    continue
                E.wait_for(ev)
            for ev in d.rd.values():
                if ev[2] == E.name and not same_raw:
                    continue
                E.wait_for(ev)

    def op(self, eng, fn, r=(), w=()):
        E = self.engs[eng]
        if E.cnt >= SEM_LIMIT:
            E.new_sem()
        self._deps(E, r, w, same_raw=(eng != "pe"))
        inst = fn(E.h)
        E.cnt += 1
        inst.then_inc(E.sem, 1)
        self.ninst += 1
        ev = (E.sem, E.cnt, E.name)
        for d in r:
            d.rd[E.name] = ev
        for d in w:
            d.lw = [ev]
            d.rd = {}
        return ev

    def dma(self, queue, out, in_, r=(), w=(), sem_dep=None, is_output=False, waw=True, **kw):
        E = self.engs[queue]
        if waw:
            self._deps(E, r, w, same_raw=True)
        else:
            self._deps(E, r, (), same_raw=True)
            for d in w:
                for ev in d.rd.values():
                    E.wait_for(ev)
        d0 = sem_dep or (w[0] if len(w) else r[0])
        if d0.dsem is None or d0.dcnt >= SEM_LIMIT:
            d0.dsem = self.es.enter_context(self.nc.semaphore(f"d{self.nsems}"))
            self.nsems += 1
            d0.dcnt = 0
        inst = E.h.dma_start(out=out, in_=in_, **kw)
        d0.dcnt += 16
        inst.then_inc(d0.dsem, 16)
        self.ninst += 1
        ev = (d0.dsem, d0.dcnt, "dma" + str(id(d0)))
        for d in r:
            d.rd[ev[2]] = ev
        for d in w:
            d.lw = [ev]
            d.rd = {}
        if is_output:
            self.out_events.append(ev)
        return ev

    def barrier(self, deps=()):
        evs = []
        for E in self.engs.values():
            if E.cnt > 0:
                evs.append((E.sem, E.cnt, E.name))
        for d in deps:
            evs += d.lw
            evs += list(d.rd.values())
        for E in self.engs.values():
            for ev in evs:
                if ev[2] != E.name:
                    E.wait_for(ev)

    def finish(self):
        E = self.engs["sp"]
        for ev in self.out_events:
            E.wait_for(ev)
        for name, e2 in self.engs.items():
            if name != "sp" and e2.cnt > 0:
                E.wait_for((e2.sem, e2.cnt, name))


def mm(K, ps, ps_ap, lhsT, rhs, start, stop, r=(), w=None):
    wd = w if w is not None else ps.all
    return K.op("pe", lambda e: e.matmul(ps_ap, lhsT=lhsT, rhs=rhs, start=start, stop=stop), r=r, w=wd)


def load_w_cast(K, dst_buf, dst_ap, src_ap, w=None):
    return K.dma("pool", dst_ap, src_ap, w=(w if w is not None else dst_buf.all), waw=False)


class Common:
    pass


def setup_common(K, io, need_f=True):
    nc = K.nc
    cm = Common()
    cm.ones_bf = K.sbuf("ones_bf", [128, 128], BF16)
    K.op("dve", lambda e: e.memset(cm.ones_bf[:], 1.0), w=cm.ones_bf.all)
    cT = K.sbuf("cT_sb", [128, 16], F32)
    K.dma("sp", cT[:], io["cT"], w=cT.all)
    sc = K.sbuf("sc", [128, 16], BF16)
    K.op("act", lambda e: e.activation(out=sc[:], in_=cT[:], func=AF.Silu), r=cT.all, w=sc.all)
    bada = K.sbuf("bada", [128, 96], F32)
    K.dma("sp", bada[:], io["b_adaT"], w=bada.all)
    ng = K.sbuf("ng", [128, 64], F32)
    K.dma("sp", ng[:], io["norm_gT"], w=ng.all)
    nmod = 6 if need_f else 2
    modT = K.sbuf("modT", [128, 96], F32)
    cm.A1 = K.sbuf("A1", [128, 16], F32)
    if need_f:
        cm.Cm = K.sbuf("Cm", [128, 16], F32)
        cm.A2 = K.sbuf("A2", [128, 16], F32)
        cm.Cf = K.sbuf("Cf", [128, 16], F32)
    with K.scope():
        _setup_common_body(K, io, cm, need_f, nmod, modT, sc, bada, ng)
    cm.modT = modT
    cm.B1 = modT
    return cm


def _setup_common_body(K, io, cm, need_f, nmod, modT, sc, bada, ng):
    wst = [K.sbuf(f"wada{i}", [128, 16, 512], BF16) for i in range(2)]
    psm = K.psum("ps_mod", [128, 512], F32)
    w_ada = io["w_ada"].rearrange("(kc p) n -> p kc n", p=128)
    ngrp = nmod * 4
    for g in range(ngrp):
        wt = wst[g % 2]
        load_w_cast(K, wt, wt[:], w_ada[:, :, g * 512:(g + 1) * 512])
        for j in range(4):
            col = g * 4 + j
            for kc in range(16):
                mm(K, psm, psm[:, col:col + 1], wt[:, kc, j * 128:(j + 1) * 128], sc[:, kc:kc + 1],
                   start=(kc == 0), stop=(kc == 15), r=wt.all + sc.all)
    ncol = ngrp * 4
    K.op("dve", lambda e: e.tensor_tensor(out=modT[:, 0:ncol], in0=psm[:, 0:ncol], in1=bada[:, 0:ncol], op=ALU.add),
         r=psm.all + bada.all, w=modT.all)
    K.op("dve", lambda e: e.scalar_tensor_tensor(out=cm.A1[:], in0=modT[:, 16:32], scalar=1.0, in1=ng[:, 0:16],
                                                  op0=ALU.add, op1=ALU.mult), r=modT.all + ng.all, w=cm.A1.all)
    if need_f:
        K.op("dve", lambda e: e.tensor_tensor(out=cm.Cm[:], in0=modT[:, 32:48], in1=ng[:, 16:32], op=ALU.mult),
             r=modT.all + ng.all, w=cm.Cm.all)
        K.op("dve", lambda e: e.scalar_tensor_tensor(out=cm.A2[:], in0=modT[:, 64:80], scalar=1.0, in1=ng[:, 32:48],
                                                      op0=ALU.add, op1=ALU.mult), r=modT.all + ng.all, w=cm.A2.all)
        K.op("dve", lambda e: e.tensor_tensor(out=cm.Cf[:], in0=modT[:, 80:96], in1=ng[:, 48:64], op=ALU.mult),
             r=modT.all + ng.all, w=cm.Cf.all)


def rstd_from_ss(K, ss_ps, sq_t, rstd_t, n):
    K.op("act", lambda e: e.activation(out=sq_t[:], in_=ss_ps[:], func=AF.Ln, scale=1.0 / n, bias=NORM_EPS),
         r=ss_ps.all, w=sq_t.all)
    K.op("act", lambda e: e.activation(out=rstd_t[:], in_=sq_t[:], func=AF.Exp, scale=-0.5), r=sq_t.all, w=rstd_t.all)


def phase2(K, io, cm, n_pass=2):
    nc = K.nc
    T = 1024
    ones = cm.ones_bf
    R1 = K.sbuf("R1", [128, 16, T], BF16, nparts=32)
    R2 = K.sbuf("R2", [128, 16, T], BF16, nparts=32)
    Fb = K.sbuf("Fb", [128, 16, T], F32, nparts=16)
    merged_v = Fb.t[:, 0:8, :].bitcast(BF16)
    WA = [K.sbuf(f"WA{i}", [128, 16 * 512], BF16) for i in range(2)]
    WB = [K.sbuf(f"WB{i}", [128, 16 * 256], BF16) for i in range(2)]
    xs = [K.sbuf(f"xs{i}", [128, 512], F32) for i in range(3)]
    sqb = [K.sbuf(f"sqb{i}", [128, 512], BF16) for i in range(2)]
    tt = [K.sbuf(f"tt{i}", [128, 512], F32) for i in range(2)]
    sg = [K.sbuf(f"sg{i}", [128, 512], F32) for i in range(2)]
    sqrt_t = K.sbuf("sqrt_t", [128, 512], F32)
    rstd = K.sbuf("rstd", [128, 512], F32)
    PS = [K.psum(f"ps{i}", [128, 512], F32) for i in range(8)]

    def merged_ap(m, half):
        off = (m % 2) * T + half * 512
        return merged_v[:, m // 2, off:off + 512]

    def merged_dep(m):
        return [Fb.d[m // 2]]

    def mix_ap(m):
        off = (m % 2) * 512
        return Fb.t[:, 8 + m // 2, off:off + 512]

    def mix_dep(m):
        return [Fb.d[8 + m // 2]]

    wg = io["w_gate"].rearrange("(kc p) n -> p kc n", p=128)
    wua = io["w_up_rwkv"].rearrange("(kc p) n -> p kc n", p=128)
    wub = io["w_up_sb"].rearrange("(kc p) n -> p kc n", p=128)
    wo = io["w_out"].rearrange("(kc p) n -> p kc n", p=128)
    w1 = io["w_mlp_in"].rearrange("(kc p) n -> p kc n", p=128)
    w2 = io["w_mlp_out"].rearrange("(kc p) n -> p kc n", p=128)
    xT = io["x2T"].rearrange("(c p) t -> p c t", p=128)
    yT = io["yT"].rearrange("(c p) t -> p c t", p=128)
    outT = io["outT"].rearrange("(c p) t -> p c t", p=128)
    x1s = io["x1s"].rearrange("(c p) t -> p c t", p=128)
    x1dep = [Dep(f"x1s{i}") for i in range(4)]

    cnt = {"xs": 0, "sq": 0, "tt": 0, "sg": 0, "wa": 0, "wb": 0, "ps": 0}

    def nxt(key, lst):
        b = lst[cnt[key] % len(lst)]
        cnt[key] += 1
        return b

    def stats_norm(src_ap_fn, src_dep_fn, ss):
        for m in range(16):
            sq = nxt("sq", sqb)
            K.op("act", lambda e: e.activation(out=sq[:], in_=src_ap_fn(m), func=AF.Square),
                 r=src_dep_fn(m), w=sq.all)
            mm(K, ss, ss[:], ones[:], sq[:], start=(m == 0), stop=(m == 15), r=sq.all + ones.all)
        rstd_from_ss(K, ss, sqrt_t, rstd, float(D))

    for p in range(n_pass):
        t0 = p * T
        for c in range(16):
            evl = K.dma("sp", R2[:, c, :], yT[:, c, t0:t0 + T], w=[R2.d[2 * c], R2.d[2 * c + 1]], sem_dep=R2.d[0], waw=False)
        for d_ in R2.d:
            d_.lw = [evl]
        for half in range(2):
            tk = t0 + half * 512
            ss = PS[7]
            xst = Fb.t[:, 0:8, :].rearrange("p a (b t) -> p (a b) t", t=512)
            for c in range(16):
                K.dma("sp", xst[:, c, :], xT[:, c, tk:tk + 512], w=[Fb.d[c // 2]], sem_dep=Fb.d[c // 2], waw=False)
            stats_norm(lambda m: xst[:, m, :], lambda m: [Fb.d[m // 2]], ss)
            for c in range(16):
                t_ = nxt("tt", tt)
                K.op("dve", lambda e: e.tensor_tensor(out=t_[:], in0=xst[:, c, :], in1=rstd[:], op=ALU.mult),
                     r=[Fb.d[c // 2]] + rstd.all, w=t_.all)
                K.op("act", lambda e: e.activation(out=R1[:, c, half * 512:(half + 1) * 512], in_=t_[:],
                                                    func=AF.Identity, scale=cm.A1[:, c:c + 1],
                                                    bias=cm.B1[:, c:c + 1]),
                     r=t_.all + cm.A1.all + cm.B1.all, w=[R1.d[2 * c + half]])
        for jg in range(8):
            wa = nxt("wa", WA)
            wb = nxt("wb", WB)
            wav = wa.t[:, :].rearrange("p (a n) -> p a n", n=256)
            wbv = wb.t[:, :].rearrange("p (a n) -> p a n", n=256)
            c0 = jg * 256
            load_w_cast(K, wa, wav[:, 0:16, :], wg[:, :, c0:c0 + 256])
            load_w_cast(K, wa, wav[:, 16:32, :], wg[:, :, 2048 + c0:2048 + c0 + 256])
            load_w_cast(K, wb, wbv[:, 0:8, :], wua[:, :, c0:c0 + 256])
            load_w_cast(K, wb, wbv[:, 8:16, :], wub[:, :, c0:c0 + 256])
            for jj in range(2):
                j = jg * 2 + jj
                for half in range(2):
                    hs = slice(half * 512, (half + 1) * 512)
                    base = (cnt["ps"] % 2) * 4
                    cnt["ps"] += 1
                    pga, pgb, pua, pub = PS[base], PS[base + 1], PS[base + 2], PS[base + 3]
                    for kc in range(16):
                        mm(K, pga, pga[:], wav[:, kc, jj * 128:(jj + 1) * 128], R1[:, kc, hs],
                           start=(kc == 0), stop=(kc == 15), r=wa.all + [R1.d[2 * kc + half]])
                    for kc in range(16):
                        mm(K, pgb, pgb[:], wav[:, 16 + kc, jj * 128:(jj + 1) * 128], R1[:, kc, hs],
                           start=(kc == 0), stop=(kc == 15), r=wa.all + [R1.d[2 * kc + half]])
                    for kc in range(8):
                        mm(K, pua, pua[:], wbv[:, kc, jj * 128:(jj + 1) * 128], R2[:, kc, hs],
                           start=(kc == 0), stop=(kc == 7), r=wb.all + [R2.d[2 * kc + half]])
                    for kc in range(8):
                        mm(K, pub, pub[:], wbv[:, 8 + kc, jj * 128:(jj + 1) * 128], R2[:, 8 + kc, hs],
                           start=(kc == 0), stop=(kc == 7), r=wb.all + [R2.d[2 * (8 + kc) + half]])
                    sa = nxt("sg", sg)
                    sb_ = nxt("sg", sg)
                    K.op("act", lambda e: e.activation(out=sa[:], in_=pga[:], func=AF.Sigmoid), r=pga.all, w=sa.all)
                    K.op("act", lambda e: e.activation(out=sb_[:], in_=pgb[:], func=AF.Sigmoid), r=pgb.all, w=sb_.all)
                    K.op("dve", lambda e: e.tensor_tensor(out=sa[:], in0=sa[:], in1=pua[:], op=ALU.mult),
                         r=sa.all + pua.all, w=sa.all)
                    K.op("dve", lambda e: e.tensor_tensor(out=sb_[:], in0=sb_[:], in1=pub[:], op=ALU.mult),
                         r=sb_.all + pub.all, w=sb_.all)
                    K.op("dve", lambda e: e.tensor_tensor(out=merged_ap(j, half), in0=sa[:], in1=sb_[:], op=ALU.add),
                         r=sa.all + sb_.all, w=merged_dep(j))
        for half in range(2):
            tk = t0 + half * 512
            ss = PS[7]
            for mg in range(4):
                wa = nxt("wa", WA)
                wav = wa.t[:, :].rearrange("p (a n) -> p a n", n=512)
                load_w_cast(K, wa, wav[:, :, :], wo[:, :, mg * 512:(mg + 1) * 512])
                for mj in range(4):
                    m = mg * 4 + mj
                    pm = PS[cnt["ps"] % 4]
                    cnt["ps"] += 1
                    for kc in range(16):
                        mm(K, pm, pm[:], wav[:, kc, mj * 128:(mj + 1) * 128], merged_ap(kc, half),
                           start=(kc == 0), stop=(kc == 15), r=wa.all + merged_dep(kc))
                    K.op("act", lambda e: e.activation(out=mix_ap(m), in_=pm[:], func=AF.Identity),
                         r=pm.all, w=mix_dep(m))
                    sq = nxt("sq", sqb)
                    K.op("act", lambda e: e.activation(out=sq[:], in_=pm[:], func=AF.Square), r=pm.all, w=sq.all)
                    mm(K, ss, ss[:], ones[:], sq[:], start=(m == 0), stop=(m == 15), r=sq.all + ones.all)
            rstd_from_ss(K, ss, sqrt_t, rstd, float(D))
            ss2 = PS[6]
            for m in range(16):
                xb = nxt("xs", xs)
                K.dma("sp", xb[:], xT[:, m, tk:tk + 512], w=xb.all)
                t_ = nxt("tt", tt)
                K.op("dve", lambda e: e.tensor_tensor(out=t_[:], in0=mix_ap(m), in1=rstd[:], op=ALU.mult),
                     r=mix_dep(m) + rstd.all, w=t_.all)
                K.op("dve", lambda e: e.scalar_tensor_tensor(out=mix_ap(m), in0=t_[:], scalar=cm.Cm[:, m:m + 1],
                                                              in1=xb[:], op0=ALU.mult, op1=ALU.add),
                     r=t_.all + xb.all + cm.Cm.all, w=mix_dep(m))
                ev_ = K.dma("sp", x1s[:, m, tk:tk + 512], mix_ap(m), r=mix_dep(m), sem_dep=mix_dep(m)[0])
                if m == 0:
                    x1dep[p * 2 + half].lw = []
                x1dep[p * 2 + half].lw.append(ev_)
                sq = nxt("sq", sqb)
                K.op("act", lambda e: e.activation(out=sq[:], in_=mix_ap(m), func=AF.Square), r=mix_dep(m), w=sq.all)
                mm(K, ss2, ss2[:], ones[:], sq[:], start=(m == 0), stop=(m == 15), r=sq.all + ones.all)
            rstd_from_ss(K, ss2, sqrt_t, rstd, float(D))
            for m in range(16):
                t_ = nxt("tt", tt)
                K.op("dve", lambda e: e.tensor_tensor(out=t_[:], in0=mix_ap(m), in1=rstd[:], op=ALU.mult),
                     r=mix_dep(m) + rstd.all, w=t_.all)
                K.op("act", lambda e: e.activation(out=R1[:, m, half * 512:(half + 1) * 512], in_=t_[:],
                                                    func=AF.Identity, scale=cm.A2[:, m:m + 1],
                                                    bias=cm.modT[:, 48 + m:49 + m]),
                     r=t_.all + cm.A2.all + cm.modT.all, w=[R1.d[2 * m + half]])
        for G in range(4):
            for kg in range(4):
                wa = nxt("wa", WA)
                wav = wa.t[:, :].rearrange("p (a n) -> p a n", n=512)
                f0 = G * 2048 + kg * 512
                load_w_cast(K, wa, wav[:, :, :], w1[:, :, f0:f0 + 512])
                for kj in range(4):
                    k = kg * 4 + kj
                    for half in range(2):
                        hs = slice(half * 512, (half + 1) * 512)
                        pa = PS[cnt["ps"] % 4]
                        cnt["ps"] += 1
                        for kc in range(16):
                            mm(K, pa, pa[:], wav[:, kc, kj * 128:(kj + 1) * 128], R1[:, kc, hs],
                               start=(kc == 0), stop=(kc == 15), r=wa.all + [R1.d[2 * kc + half]])
                        r_ = nxt("sg", sg)
                        K.op("act", lambda e: e.activation(out=r_[:], in_=pa[:], func=AF.Relu), r=pa.all, w=r_.all)
                        K.op("dve", lambda e: e.tensor_tensor(out=R2[:, k, hs], in0=r_[:], in1=r_[:], op=ALU.mult),
                             r=r_.all, w=[R2.d[2 * k + half]])
            for mp in range(8):
                wb = nxt("wb", WB)
                wbv = wb.t[:, :].rearrange("p (a n) -> p a n", n=256)
                load_w_cast(K, wb, wbv[:, :, :], w2[:, G * 16:(G + 1) * 16, mp * 256:(mp + 1) * 256])
                for mj in range(2):
                    m = mp * 2 + mj
                    for half in range(2):
                        hs = slice(half * 512, (half + 1) * 512)
                        pf = PS[4 + cnt["ps"] % 2]
                        cnt["ps"] += 1
                        for k in range(16):
                            mm(K, pf, pf[:], wbv[:, k, mj * 128:(mj + 1) * 128], R2[:, k, hs],
                               start=(k == 0), stop=(k == 15), r=wb.all + [R2.d[2 * k + half]])
                        if G == 0:
                            K.op("act", lambda e: e.activation(out=Fb.t[:, m, hs], in_=pf[:], func=AF.Identity),
                                 r=pf.all, w=[Fb.d[m]])
                        else:
                            K.op("dve", lambda e: e.tensor_tensor(out=Fb.t[:, m, hs], in0=Fb.t[:, m, hs], in1=pf[:],
                                                                  op=ALU.add), r=pf.all + [Fb.d[m]], w=[Fb.d[m]])
        for half in range(2):
            tk = t0 + half * 512
            hs = slice(half * 512, (half + 1) * 512)
            ss = PS[7]
            stats_norm(lambda m: Fb.t[:, m, hs], lambda m: [Fb.d[m]], ss)
            for m in range(16):
                xb = nxt("xs", xs)
                K.dma("sp", xb[:], x1s[:, m, tk:tk + 512], r=[x1dep[p * 2 + half]], w=xb.all)
                t_ = nxt("tt", tt)
                K.op("dve", lambda e: e.tensor_tensor(out=t_[:], in0=Fb.t[:, m, hs], in1=rstd[:], op=ALU.mult),
                     r=[Fb.d[m]] + rstd.all, w=t_.all)
                K.op("dve", lambda e: e.scalar_tensor_tensor(out=t_[:], in0=t_[:], scalar=cm.Cf[:, m:m + 1],
                                                              in1=xb[:], op0=ALU.mult, op1=ALU.add),
                     r=t_.all + xb.all + cm.Cf.all, w=t_.all)
                K.dma("sp", outT[:, m, tk:tk + 512], t_[:], r=t_.all, sem_dep=t_.d[0], is_output=True)


def declare(nc, name, shape, dt, kind):
    return nc.dram_tensor(name, list(shape), dt, kind=kind).ap()


def build_phase2_only():
    nc = bass.Bass("TRN2", target_bir_lowering=False)
    io = {}
    io["cT"] = declare(nc, "cT", [128, 16], F32, "ExternalInput")
    io["b_adaT"] = declare(nc, "b_adaT", [128, 96], F32, "ExternalInput")
    io["norm_gT"] = declare(nc, "norm_gT", [128, 64], F32, "ExternalInput")
    io["w_ada"] = declare(nc, "w_ada", [D, 6 * D], F32, "ExternalInput")
    io["x2T"] = declare(nc, "x2T", [D, 2048], F32, "ExternalInput")
    io["yT"] = declare(nc, "yT", [D, 2048], BF16, "ExternalInput")
    io["w_gate"] = declare(nc, "w_gate", [D, 2 * D], F32, "ExternalInput")
    io["w_up_rwkv"] = declare(nc, "w_up_rwkv", [C, D], F32, "ExternalInput")
    io["w_up_sb"] = declare(nc, "w_up_sb", [C, D], F32, "ExternalInput")
    io["w_out"] = declare(nc, "w_out", [D, D], F32, "ExternalInput")
    io["w_mlp_in"] = declare(nc, "w_mlp_in", [D, DFF], F32, "ExternalInput")
    io["w_mlp_out"] = declare(nc, "w_mlp_out", [DFF, D], F32, "ExternalInput")
    io["outT"] = declare(nc, "outT", [D, 2048], F32, "ExternalOutput")
    io["x1s"] = declare(nc, "x1s", [D, 2048], F32, "Internal")
    K = Ctx(nc)
    with K.es:
        cm = setup_common(K, io, need_f=True)
        phase2(K, io, cm)
        K.finish()
    print("phase2: ninst", K.ninst, "nwaits", K.nwaits, "nsems", K.nsems)
    return nc


def cols128(v):
    return np.ascontiguousarray(v.reshape(-1, 128).T)


def common_inputs(inp, b):
    return {
        "cT": cols128(inp["c"][b]),
        "b_adaT": cols128(inp["b_ada"][0]),
        "norm_gT": cols128(inp["norm_g"][0].reshape(-1)),
        "w_ada": inp["w_ada"][0],
    }


def host_consts():
    p = np.arange(128)[:, None]
    c128_ = np.arange(128)[None, :]
    same = (p // 64) == (c128_ // 64)
    lo = (same & ((c128_ % 64) < (p % 64))).astype(np.float32)
    ups = (same & ((p % 64) < (c128_ % 64))).astype(np.float32)
    upi = (same & ((p % 64) <= (c128_ % 64))).astype(np.float32)
    cst = {}
    cst["mask1"] = np.tile(lo, (1, 4))
    cst["mask2"] = np.tile(np.concatenate([ups, upi], axis=1), (1, 2))
    cst["mask3x2"] = np.tile(upi, (1, 4))
    cst["id128x8"] = np.tile((p == c128_).astype(np.float32), (1, 8))
    c128 = np.arange(128)[None, :]
    cst["ident"] = (p == c128).astype(np.float32)
    cst["negtri"] = -(p >= c128).astype(np.float32)
    t512 = np.arange(512)[None, :]
    cst["maskd"] = np.concatenate([((p + 128 * d) < t512).astype(np.float32) for d in range(4)], axis=1)
    bf = {k: v.astype(ml_dtypes.bfloat16) for k, v in cst.items()}
    bf["bd1"] = ((p // 64) == (c128 // 64)).astype(np.float32)
    bf["bd64"] = bf["bd1"] / 64.0
    bf["scanmask"] = np.tile((np.arange(512)[None, :] % 64 != 0).astype(np.float32), (128, 1))
    return bf


CONST_SHAPES = {"mask1": (512, BF16), "mask2": (512, BF16), "mask3x2": (512, BF16), "id128x8": (1024, BF16),
                "ident": (128, BF16), "negtri": (128, BF16), "maskd": (2048, BF16), "bd1": (128, F32),
                "bd64": (128, F32), "scanmask": (512, F32)}


def phase1(K, io, cm, n_tiles=16, do_rwkv=True, do_sb=True, groups=(None,), own_from=0, ysc=None):
    nc = K.nc
    ones = cm.ones_bf
    cst = {}
    for name, (n, dt) in CONST_SHAPES.items():
        cst[name] = K.sbuf("cs_" + name, [128, n], dt)
        K.dma("sp", cst[name][:], io["c_" + name], w=cst[name].all)
    prm = K.sbuf("prm_sb", [128, 32], F32)
    omk = K.sbuf("omk", [128, 2], F32)
    lw = K.sbuf("lw", [128, 4, 256], BF16)
    tokm = K.sbuf("tokm", [128, 512], BF16) if "tokmask" in io else None

    def load_group(g):
        sel = (lambda ap: ap) if g is None else (lambda ap: ap[g])
        K.dma("sp", prm[:], sel(io["prm"]), w=prm.all)
        K.op("dve", lambda e: e.tensor_scalar(out=omk[:], in0=prm[:, 16:18], scalar1=-1.0, scalar2=1.0,
                                               op0=ALU.mult, op1=ALU.add), r=prm.all, w=omk.all)
        K.dma("pool", lw[0:64, 0, :], sel(io["w2s"]), w=lw.all, waw=False)
        K.dma("pool", lw[0:64, 1, :], sel(io["a2s"]), w=lw.all, waw=False)
        K.dma("pool", lw[:, 2, :], sel(io["g2s"])[0:128, :], w=lw.all, waw=False)
        K.dma("pool", lw[0:32, 3, :], sel(io["g2s"])[128:160, :], w=lw.all, waw=False)

    KT = [K.sbuf(f"KT{i}", [128, S], BF16, nparts=16) for i in range(2)]
    Vtm = K.sbuf("Vtm", [128, 64, 256], BF16, nparts=16)
    hT = K.sbuf("hT", [128, 16, 512], BF16, nparts=16)
    raw = [K.sbuf(f"raw{i}", [128, 513], F32) for i in range(2)]
    lastcol = K.sbuf("lastcol", [128, 10], F32)
    wbuf = [K.sbuf(f"w1b{i}", [128, 16, 128], BF16) for i in range(2)]
    xs = [K.sbuf(f"p1xs{i}", [128, 512], F32) for i in range(2)]
    sqb = [K.sbuf(f"p1sq{i}", [128, 512], BF16) for i in range(2)]
    sqrt_t = K.sbuf("p1sqrt", [128, 512], F32) if not do_rwkv else None
    rstd = K.sbuf("p1rstd", [128, 512], F32) if not do_rwkv else None
    PSall = K.psum("p1psall", [128, 4096], F32)
    PS = [PSall.view(PSall.t[:, i * 512:(i + 1) * 512], f"p1ps{i}") for i in range(8)]
    PW = []
    for i in range(4):
        wb_ = Buf(PSall.t[:, i * 1024:(i + 1) * 1024], f"p1pw{i}", 0)
        wb_.d = [PS[2 * i].d[0], PS[2 * i + 1].d[0]]
        PW.append(wb_)
    pr1 = K.sbuf("p_r0", [128, 512], F32)
    pk1 = K.sbuf("p_k0", [128, 512], F32)
    pv1 = K.sbuf("p_v0", [128, 512], F32)
    pr_, pk_, pv_ = [pr1, pr1], [pk1, pk1], [pv1, pv1]
    dwt = K.sbuf("dwt", [64, 512], BF16)
    dat = K.sbuf("dat", [64, 512], BF16)
    dgs = K.sbuf("dgs", [128, 2, 512], BF16)
    q8 = [K.sbuf(f"q8_{i}", [128, 512], BF16) for i in range(2)]
    vtmp = K.sbuf("vtmp", [128, 512], BF16)
    tmpf = [K.sbuf(f"tmpf{i}", [128, 512], F32) for i in range(2)] if not do_rwkv else [None, None]
    cnt = {"xs": 0, "sq": 0, "w": 0, "ps": 0, "tf": 0}

    def nxt(key, lst):
        b = lst[cnt[key] % len(lst)]
        cnt[key] += 1
        return b

    xT = io["xT"].rearrange("(c p) t -> p c t", p=128)

    rw = RwkvState(K, cst, prm, omk, lw, PS, PW) if do_rwkv else None
    if rw is not None:
        tmpf[:] = [rw.t1, rw.t2]
        sqrt_t, rstd = rw.t3, rw.W
    for g in groups:
        load_group(g)
        K.op("dve", lambda e: e.memset(lastcol[:], 0.0), w=lastcol.all)
        if rw is not None:
            rw.reset()
        w1c = io["w1c"] if g is None else io["w1c"][g]
        if ysc is None:
            ya_dst = lambda pair, qt: io["yaT"][pair * 128:(pair + 1) * 128, qt * 512:(qt + 1) * 512]
            yb_dst = lambda hd, qt: io["ybT"][hd * 64:(hd + 1) * 64, qt * 512:(qt + 1) * 512]
            ydep = None
        else:
            gg = g
            ya_dst = lambda pair, qt: ysc[gg * 256 + pair * 128:gg * 256 + (pair + 1) * 128,
                                          (qt - own_from) * 512:(qt - own_from + 1) * 512]
            yb_dst = lambda hd, qt: ysc[1024 + gg * 256 + hd * 64:1024 + gg * 256 + (hd + 1) * 64,
                                        (qt - own_from) * 512:(qt - own_from + 1) * 512]
            ydep = ysc.d[0]
        _phase1_tiles(K, io, cm, n_tiles, do_rwkv, do_sb, own_from, w1c, ya_dst, yb_dst, ydep, rw, xT, tokm,
                      cst, ones, prm, PS, KT, Vtm, hT, raw, lastcol, wbuf, xs, sqb, sqrt_t, rstd, pr_, pk_, pv_,
                      dwt, dat, dgs, q8, vtmp, tmpf, cnt, nxt)


def _phase1_tiles(K, io, cm, n_tiles, do_rwkv, do_sb, own_from, w1c, ya_dst, yb_dst, ydep, rw, xT, tokm,
                  cst, ones, prm, PS, KT, Vtm, hT, raw, lastcol, wbuf, xs, sqb, sqrt_t, rstd, pr_, pk_, pv_,
                  dwt, dat, dgs, q8, vtmp, tmpf, cnt, nxt):
    for qt in range(n_tiles):
        t0 = qt * 512
        ss = PS[7]
        for c in range(16):
            xb = nxt("xs", xs)
            K.dma("sp", xb[:], xT[:, c, t0:t0 + 512], w=xb.all)
            sq = nxt("sq", sqb)
            K.op("act", lambda e: e.activation(out=sq[:], in_=xb[:], func=AF.Square), r=xb.all, w=sq.all)
            mm(K, ss, ss[:], ones[:], sq[:], start=(c == 0), stop=(c == 15), r=sq.all + ones.all)
        rstd_from_ss(K, ss, sqrt_t, rstd, float(D))
        for c in range(16):
            xb = nxt("xs", xs)
            K.dma("sp", xb[:], xT[:, c, t0:t0 + 512], w=xb.all)
            K.op("dve", lambda e: e.tensor_tensor(out=xb[:], in0=xb[:], in1=rstd[:], op=ALU.mult),
                 r=xb.all + rstd.all, w=xb.all)
            K.op("act", lambda e: e.activation(out=hT[:, c, :], in_=xb[:], func=AF.Identity,
                                                scale=cm.A1[:, c:c + 1], bias=cm.B1[:, c:c + 1]),
                 r=xb.all + cm.A1.all + cm.B1.all, w=[hT.d[c]])
            if tokm is not None:
                if c == 0:
                    K.dma("sp", tokm[:], io["tokmask"][:, t0:t0 + 512], w=tokm.all)
                K.op("dve", lambda e: e.tensor_tensor(out=hT[:, c, :], in0=hT[:, c, :], in1=tokm[:], op=ALU.mult),
                     r=[hT.d[c]] + tokm.all, w=[hT.d[c]])
        own = qt >= own_from
        def do_chunk(ch):
            if (ch < 10 and not do_rwkv) or (ch >= 10 and not do_sb):
                return
            if ch in (10, 11) and not own:
                return
            if ch in (0, 1, 8, 9) and qt < own_from - 1:
                return
            wb = nxt("w", wbuf)
            load_w_cast(K, wb, wb[:], w1c[ch])
            pp = PS[cnt["ps"] % 4]
            cnt["ps"] += 1
            M = {6: 64, 7: 64, 9: 32}.get(ch, 128)
            for kc in range(16):
                mm(K, pp, pp[0:M, :], wb[:, kc, 0:M], hT[:, kc, :], start=(kc == 0), stop=(kc == 15),
                   r=wb.all + [hT.d[kc]])
            if ch < 10:
                rb = raw[ch % 2]
                K.op("dve", lambda e: e.tensor_copy(out=rb[0:M, 0:1], in_=lastcol[0:M, ch:ch + 1]),
                     r=lastcol.all, w=rb.all)
                K.op("act", lambda e: e.activation(out=rb[0:M, 1:513], in_=pp[0:M, :], func=AF.Identity),
                     r=pp.all, w=rb.all)
                df = nxt("tf", tmpf)
                K.op("dve", lambda e: e.tensor_tensor(out=df[0:M, :], in0=rb[0:M, 0:512], in1=rb[0:M, 1:513],
                                                      op=ALU.subtract), r=rb.all, w=df.all)
                if ch < 6:
                    dst = [pr_, pk_, pv_][ch // 2][ch % 2]
                    K.op("dve", lambda e: e.scalar_tensor_tensor(out=dst[:], in0=df[:], scalar=prm[:, ch:ch + 1],
                                                                  in1=rb[:, 1:513], op0=ALU.mult, op1=ALU.add),
                         r=df.all + rb.all + prm.all, w=dst.all)
                else:
                    K.op("dve", lambda e: e.scalar_tensor_tensor(out=df[0:M, :], in0=df[0:M, :],
                                                                  scalar=prm[0:M, ch:ch + 1], in1=rb[0:M, 1:513],
                                                                  op0=ALU.mult, op1=ALU.add),
                         r=df.all + rb.all + prm.all, w=df.all)
                    if ch == 6:
                        K.op("act", lambda e: e.activation(out=dwt[:], in_=df[0:64, :], func=AF.Tanh),
                             r=df.all, w=dwt.all)
                    elif ch == 7:
                        K.op("act", lambda e: e.activation(out=dat[:], in_=df[0:64, :], func=AF.Identity),
                             r=df.all, w=dat.all)
                    elif ch == 8:
                        K.op("act", lambda e: e.activation(out=dgs[:, 0, :], in_=df[:], func=AF.Sigmoid),
                             r=df.all, w=dgs.all)
                    else:
                        K.op("act", lambda e: e.activation(out=dgs[0:32, 1, :], in_=df[0:32, :], func=AF.Sigmoid),
                             r=df.all, w=dgs.all)
                K.op("dve", lambda e: e.tensor_copy(out=lastcol[0:M, ch:ch + 1], in_=rb[0:M, 512:513]),
                     r=rb.all, w=lastcol.all)
            elif ch < 12:
                K.op("act", lambda e: e.activation(out=q8[ch - 10][:], in_=pp[:], func=AF.Identity, scale=0.125),
                     r=pp.all, w=q8[ch - 10].all)
            elif ch < 14:
                K.op("act", lambda e: e.activation(out=KT[ch - 12][:, t0:t0 + 512], in_=pp[:], func=AF.Identity),
                     r=pp.all, w=[KT[ch - 12].d[qt]])
            else:
                pair = ch - 14
                K.op("act", lambda e: e.activation(out=vtmp[:], in_=pp[:], func=AF.Identity), r=pp.all, w=vtmp.all)
                pT = PS[6]
                pTb = pT.t[:].bitcast(BF16)
                for bk in range(4):
                    K.op("pe", lambda e: e.transpose(pTb[:, bk * 128:(bk + 1) * 128], vtmp[:, bk * 128:(bk + 1) * 128],
                                                     cst["ident"][:]),
                         r=vtmp.all + cst["ident"].all, w=pT.all)
                K.op("dve", lambda e: e.tensor_copy(
                    out=Vtm[:, 4 * qt:4 * qt + 4, pair * 128:(pair + 1) * 128],
                    in_=pTb[:, 0:512].rearrange("p (b c) -> p b c", c=128)), r=pT.all, w=[Vtm.d[qt]])
        for ch in (6, 7, 8, 9):
            do_chunk(ch)
        for pair in range(2):
            for ch in (pair, 2 + pair, 4 + pair):
                do_chunk(ch)
            if do_rwkv:
                rw.tile(qt, pair, pr_[pair], pk_[pair], pv_[pair], dwt, dat, dgs, ya_dst(pair, qt) if own else None, own, ydep)
        for ch in range(10, 16):
            do_chunk(ch)
        if do_sb and own:
            K.barrier()
            sb_tile(K, cst, ones, PS, qt, q8, KT, Vtm, yb_dst, ydep, tmpf, rw=rw)
            K.barrier()


_sbst = {}


def sb_tile(K, cst, ones, PS, qt, q8, KT, Vtm, yb_dst, ydep, tmpf, rw=None):
    st = _sbst.get(id(K))
    if st is None:
        st = {}
        if rw is None:
            st["e"] = [K.sbuf(f"sb_e{i}", [128, 512], F32) for i in range(5)]
            st["w"] = [K.sbuf(f"sb_w{i}", [128, 512], F32) for i in range(2)]
            st["sp"] = [K.sbuf(f"sb_sp{i}", [128, 512], BF16) for i in range(3)]
            st["ar"] = [K.sbuf(f"sb_ar{i}", [128, 512], F32) for i in range(2)]
            st["att"] = [K.sbuf(f"sb_att{i}", [128, 512], BF16) for i in range(3)]
            st["Cc"] = K.sbuf("sb_Cc", [128, 512], F32)
        else:
            st["e"] = [rw.kk.view(rw.kk[:], "sbv_e0"), rw.cl.view(rw.cl[:], "sbv_e1"), rw.kf.view(rw.kf[:], "sbv_e2"),
                       rw.g.view(rw.g[:], "sbv_e3"), rw.W.view(rw.W[:], "sbv_e4")]
            st["w"] = [rw.ld.view(rw.ld[:], "sbv_w0"), rw.a.view(rw.a[:], "sbv_w1")]
            st["ar"] = [rw.t1.view(rw.t1[:], "sbv_ar0"), rw.t2.view(rw.t2[:], "sbv_ar1")]
            st["Cc"] = rw.t3.view(rw.t3[:], "sbv_Cc")
            st["sp"] = [rw.AR.view(rw.AR[:, 0, :], "sbv_sp0"), rw.AR.view(rw.AR[:, 1, :], "sbv_sp1"),
                        rw.BK.view(rw.BK[:, 0, :], "sbv_sp2")]
            st["att"] = [rw.BK.view(rw.BK[:, 1, :], "sbv_at0"), rw.pre.view(rw.pre[:, 0, :], "sbv_at1"),
                         rw.pre.view(rw.pre[:, 1, :], "sbv_at2")]
        st["yo"] = [K.sbuf(f"sb_yo{i}", [64, 512], BF16) for i in range(2)]
        st["n"] = 0
        _sbst[id(K)] = st
    t0 = qt * 512
    negtri = cst["negtri"]
    maskd = cst["maskd"]
    Cc = st["Cc"]
    for hd in range(4):
        pair, bp = hd // 2, 64 * (hd % 2)
        qv = q8[pair][bp:bp + 64, :]
        qd = q8[pair].all
        jmax = 4 * qt + 3
        js = list(range(jmax, -1, -1))
        ypsum = PS[6]
        info = {}

        NE = len(st["e"])

        def s1(idx):
            j = js[idx]
            d = j - 4 * qt
            kT = KT[pair][bp:bp + 64, j * 128:(j + 1) * 128]
            kd = [KT[pair].d[j // 4]]
            zp = PS[idx % 2]
            mm(K, zp, zp[:], kT, qv, True, True, r=kd + qd)
            e = st["e"][idx % NE]
            K.op("act", lambda en: en.activation(out=e[:], in_=zp[:], func=AF.Exp), r=zp.all, w=e.all)
            info[idx] = dict(j=j, d=d, e=e, sp=st["sp"][idx % 3], att=st["att"][idx % 3], wv=st["w"][idx % 2],
                             ar=st["ar"][idx % 2], ap=PS[2 + idx % 2], cp=PS[4 + idx % 2], last=(idx == len(js) - 1))

        def s2(idx):
            I = info[idx]
            sp, e, d = I["sp"], I["e"], I["d"]
            K.op("act", lambda en: en.activation(out=sp[:], in_=e[:], func=AF.Ln, bias=1.0), r=e.all, w=sp.all)
            if d >= 0:
                K.op("dve", lambda en: en.tensor_tensor(out=sp[:], in0=sp[:], in1=maskd[:, d * 512:(d + 1) * 512],
                                                        op=ALU.mult), r=sp.all + maskd.all, w=sp.all)

        def s3(idx):
            I = info[idx]
            sp, ap_, cp, ar, last = I["sp"], I["ap"], I["cp"], I["ar"], I["last"]
            mm(K, ap_, ap_[:], negtri[:], sp[:], True, True, r=sp.all + negtri.all)
            if not last:
                mm(K, cp, cp[:], ones[:], sp[:], True, True, r=sp.all + ones.all)
            if idx == 0:
                K.op("dve", lambda en: en.tensor_copy(out=ar[:], in_=ap_[:]), r=ap_.all, w=ar.all)
                if not last:
                    K.op("dve", lambda en: en.tensor_copy(out=Cc[:], in_=cp[:]), r=cp.all, w=Cc.all)
            else:
                K.op("dve", lambda en: en.tensor_tensor(out=ar[:], in0=ap_[:], in1=Cc[:], op=ALU.subtract),
                     r=ap_.all + Cc.all, w=ar.all)
                if not last:
                    K.op("dve", lambda en: en.tensor_tensor(out=Cc[:], in0=Cc[:], in1=cp[:], op=ALU.add),
                         r=cp.all + Cc.all, w=Cc.all)

        def s4(idx):
            I = info[idx]
            wv, ar = I["wv"], I["ar"]
            K.op("act", lambda en: en.activation(out=wv[:], in_=ar[:], func=AF.Exp), r=ar.all, w=wv.all)

        def s5(idx):
            I = info[idx]
            att, e, wv, d = I["att"], I["e"], I["wv"], I["d"]
            meng = "pool" if idx % 2 == 0 else "dve"
            K.op(meng, lambda en: en.tensor_tensor(out=att[:], in0=e[:], in1=wv[:], op=ALU.mult),
                 r=e.all + wv.all, w=att.all)
            if d >= 0:
                K.op(meng, lambda en: en.tensor_tensor(out=att[:], in0=att[:], in1=maskd[:, d * 512:(d + 1) * 512],
                                                       op=ALU.mult), r=att.all + maskd.all, w=att.all)

        def s6(idx):
            I = info[idx]
            mm(K, ypsum, ypsum[0:64, :], Vtm[:, I["j"], hd * 64:(hd + 1) * 64], I["att"][:], idx == 0,
               idx == len(js) - 1, r=I["att"].all + [Vtm.d[I["j"] // 4]])

        n = len(js)
        stages = [s1, s2, s3, s4, s5, s6]
        for step in range(n + len(stages) - 1):
            for k_, fn_ in enumerate(stages):
                if 0 <= step - k_ < n:
                    fn_(step - k_)
        yo = st["yo"][st["n"] % 2]
        st["n"] += 1
        K.op("act", lambda en: en.activation(out=yo[:], in_=ypsum[0:64, :], func=AF.Identity), r=ypsum.all, w=yo.all)
        if ydep is None:
            K.dma("sp", yb_dst(hd, qt), yo[:], r=yo.all, sem_dep=yo.d[0], is_output=True)
        else:
            K.scratch_events.append(K.dma("sp", yb_dst(hd, qt), yo[:], r=yo.all, sem_dep=yo.d[0]))


class RwkvState:
    def __init__(self, K, cst, prm, omk, lw, PS, PW):
        self.K, self.cst, self.prm, self.omk, self.lw, self.PS, self.PW = K, cst, prm, omk, lw, PS, PW
        sb = K.sbuf
        self.St = [[sb(f"St{p}_{i}", [128, 64], BF16) for i in range(3)] for p in range(2)]
        for p in range(2):
            K.op("dve", lambda e: e.memset(self.St[p][0][:], 0.0), w=self.St[p][0].all)
        self.stn = [0, 0]
        f = lambda n: sb("rw_" + n, [128, 512], F32)
        self.ld, self.a, self.g, self.kk, self.kf, self.cl, self.W = (f(n) for n in ("ld", "a", "g", "kk", "kf", "cl", "W"))
        self.t1, self.t2, self.t3 = f("t1"), f("t2"), f("t3")
        self.AR = sb("rw_AR", [128, 2, 512], BF16)
        self.BK = sb("rw_BK", [128, 2, 512], BF16)
        self.pre = sb("rw_pre", [128, 3, 512], BF16)
        self.TT = sb("rw_TT", [128, 4, 512], BF16)
        self.X1 = sb("rw_X1", [128, 2048], BF16)
        self.X2 = sb("rw_X2", [128, 2048], BF16)
        self.X3 = sb("rw_X3", [128, 1024], BF16)
        self.A = [sb(f"rw_A{i}", [128, 1024], BF16) for i in range(2)]
        self.AT = [sb(f"rw_AT{i}", [128, 1024], BF16) for i in range(2)]
        self.Tt = [sb(f"rw_Tt{i}", [128, 1024], BF16) for i in range(2)]
        self.X1b = sb("rw_X1b", [128, 1024], BF16)
        self.X2b = sb("rw_X2b", [128, 1024], BF16)
        self.Rbd = sb("rw_Rbd", [128, 8, 2, 64], BF16)
        self.Pbd = sb("rw_Pbd", [128, 8, 128], BF16)
        self.Ghb = sb("rw_Ghb", [128, 8, 128], BF16)
        self.Gbd = sb("rw_Gbd", [128, 8, 128], BF16)
        self.BhB = sb("rw_BhB", [128, 8, 128], BF16)
        self.KhB = sb("rw_KhB", [128, 8, 128], BF16)
        for b_ in (self.Rbd, self.Pbd, self.BhB, self.KhB):
            K.op("dve", lambda e: e.memset(b_[:], 0.0), w=b_.all)
        self.yo = [sb(f"rw_yo{i}", [128, 512], BF16) for i in range(2)]
        self.nyo = 0

    def reset(self):
        for p in range(2):
            cur = self.St[p][self.stn[p] % 3]
            self.K.op("dve", lambda e: e.memset(cur[:], 0.0), w=cur.all)

    def tile(self, qt, pair, p_r, p_k, p_v, dwt, dat, dgs, ya_ap, emit_y=True, ydep=None):
        K, cst, prm, omk, lw, PS = self.K, self.cst, self.prm, self.omk, self.lw, self.PS
        t0 = qt * 512
        ps_ = slice(pair * 128, (pair + 1) * 128)
        col = lambda i: prm[:, i + pair:i + pair + 1]
        ld, a, g, kk, kf, cl, W = self.ld, self.a, self.g, self.kk, self.kf, self.cl, self.W
        t1, t2, t3 = self.t1, self.t2, self.t3
        AR, BK, pre, TT = self.AR, self.BK, self.pre, self.TT
        DV = lambda fn, r, w: K.op("dve", fn, r=r, w=w)
        AC = lambda fn, r, w: K.op("act", fn, r=r, w=w)
        mm(K, PS[0], PS[0][:], lw[0:64, 0, ps_], dwt[0:64, :], True, True, r=lw.all + dwt.all)
        mm(K, PS[1], PS[1][:], lw[0:64, 1, ps_], dat[0:64, :], True, True, r=lw.all + dat.all)
        if emit_y:
            mm(K, PS[2], PS[2][:], lw[:, 2, ps_], dgs[:, 0, :], True, False, r=lw.all + dgs.all)
            mm(K, PS[2], PS[2][:], lw[0:32, 3, ps_], dgs[0:32, 1, :], False, True, r=lw.all + dgs.all)
        AC(lambda e: e.activation(out=t1[:], in_=PS[0][:], func=AF.Sigmoid, bias=col(10)), PS[0].all + prm.all, t1.all)
        DV(lambda e: e.tensor_scalar(out=ld[:], in0=t1[:], scalar1=-0.6065306597126334, scalar2=None, op0=ALU.mult),
           t1.all, ld.all)
        AC(lambda e: e.activation(out=a[:], in_=PS[1][:], func=AF.Sigmoid, bias=col(12)), PS[1].all + prm.all, a.all)
        if emit_y:
            AC(lambda e: e.activation(out=g[:], in_=PS[2][:], func=AF.Identity), PS[2].all, g.all)
        DV(lambda e: e.tensor_scalar(out=t1[:], in0=p_k[:], scalar1=col(14), scalar2=None, op0=ALU.mult),
           p_k.all + prm.all, t1.all)
        AC(lambda e: e.activation(out=t2[:], in_=t1[:], func=AF.Square), t1.all, t2.all)
        mm(K, PS[3], PS[3][:], cst["bd1"][:], t2[:], True, True, r=cst["bd1"].all + t2.all)
        DV(lambda e: e.tensor_scalar(out=t2[:], in0=PS[3][:], scalar1=1e-24, scalar2=None, op0=ALU.max),
           PS[3].all, t2.all)
        AC(lambda e: e.activation(out=t3[:], in_=t2[:], func=AF.Ln, scale=float(2.0 ** 40)), t2.all, t3.all)
        AC(lambda e: e.activation(out=t2[:], in_=t3[:], func=AF.Exp, scale=-0.5, bias=20.0 * 0.6931471805599453),
           t3.all, t2.all)
        DV(lambda e: e.tensor_tensor(out=kk[:], in0=t1[:], in1=t2[:], op=ALU.mult), t1.all + t2.all, kk.all)
        DV(lambda e: e.tensor_scalar(out=t1[:], in0=a[:], scalar1=col(16), scalar2=omk[:, pair:pair + 1],
                                     op0=ALU.mult, op1=ALU.add), a.all + prm.all + omk.all, t1.all)
        DV(lambda e: e.tensor_tensor(out=kf[:], in0=p_k[:], in1=t1[:], op=ALU.mult), p_k.all + t1.all, kf.all)
        DV(lambda e: e.tensor_tensor_scan(out=cl[:], data0=cst["scanmask"][:], data1=ld[:], initial=0.0,
                                          op0=ALU.mult, op1=ALU.add), cst["scanmask"].all + ld.all, cl.all)
        AC(lambda e: e.activation(out=W[:], in_=cl[:], func=AF.Exp), cl.all, W.all)
        DV(lambda e: e.tensor_tensor(out=t3[:], in0=kk[:], in1=a[:], op=ALU.mult), kk.all + a.all, t3.all)
        AC(lambda e: e.activation(out=t1[:], in_=cl[:], func=AF.Exp, scale=-1.0), cl.all, t1.all)
        DV(lambda e: e.tensor_tensor(out=BK[:, 0, :], in0=t3[:], in1=t1[:], op=ALU.mult), t3.all + t1.all, BK.all)
        DV(lambda e: e.tensor_tensor(out=BK[:, 1, :], in0=kf[:], in1=t1[:], op=ALU.mult), kf.all + t1.all, BK.all)
        DV(lambda e: e.tensor_tensor(out=t2[:], in0=cl[:], in1=ld[:], op=ALU.subtract), cl.all + ld.all, t2.all)
        AC(lambda e: e.activation(out=t2[:], in_=t2[:], func=AF.Exp), t2.all, t2.all)
        DV(lambda e: e.scalar_tensor_tensor(out=AR[:, 0, :], in0=kk[:], scalar=-1.0, in1=t2[:], op0=ALU.mult,
                                            op1=ALU.mult), kk.all + t2.all, AR.all)
        if emit_y:
            DV(lambda e: e.tensor_tensor(out=AR[:, 1, :], in0=p_r[:], in1=W[:], op=ALU.mult), p_r.all + W.all, AR.all)
        cl3 = cl[:].rearrange("p (c s) -> p c s", s=64)
        t13 = t1[:].rearrange("p (c s) -> p c s", s=64)
        DV(lambda e: e.tensor_tensor(out=t13, in0=cl3[:, :, 63:64].to_broadcast([128, 8, 64]), in1=cl3,
                                     op=ALU.subtract), cl.all, t1.all)
        AC(lambda e: e.activation(out=t1[:], in_=t1[:], func=AF.Exp), t1.all, t1.all)
        DV(lambda e: e.tensor_tensor(out=pre[:, 0, :], in0=t3[:], in1=t1[:], op=ALU.mult), t3.all + t1.all, pre.all)
        DV(lambda e: e.tensor_tensor(out=pre[:, 1, :], in0=kf[:], in1=t1[:], op=ALU.mult), kf.all + t1.all, pre.all)
        AC(lambda e: e.activation(out=pre[:, 2, :], in_=p_v[:], func=AF.Identity), p_v.all, pre.all)
        if STOP_AT == 1:
            return
        ident = cst["ident"]
        srcs = [AR[:, 0, :], pre[:, 0, :], pre[:, 1, :], pre[:, 2, :]]
        BhB, KhB = self.BhB, self.KhB
        for qi in range(4):
            pT = PS[6 + qi % 2]
            pTb = pT.t[:].bitcast(BF16)
            for bk in range(4):
                K.op("pe", lambda e: e.transpose(pTb[:, bk * 128:(bk + 1) * 128], srcs[qi][:, bk * 128:(bk + 1) * 128],
                                                 ident[:]), r=AR.all + pre.all + ident.all, w=pT.all)
            AC(lambda e: e.activation(out=TT[:, qi, :], in_=pTb[:, 0:512], func=AF.Identity), pT.all, TT.all)
            if qi in (1, 2):
                BD = BhB if qi == 1 else KhB
                for h2 in range(2):
                    hs_ = slice(h2 * 64, h2 * 64 + 64)
                    AC(lambda e: e.activation(out=BD[hs_, :, h2 * 64:h2 * 64 + 64],
                                              in_=pTb[hs_, 0:512].rearrange("p (b j) -> p b j", j=64),
                                              func=AF.Identity), pT.all, BD.all)
        if STOP_AT == 2:
            return
        X1, X2, X3, X1b, X2b = self.X1, self.X2, self.X3, self.X1b, self.X2b
        Rbd, Pbd, Ghb, Gbd = self.Rbd, self.Pbd, self.Ghb, self.Gbd
        X1s = X1[:].rearrange("p (s w x) -> p s w x", w=2, x=128)
        X2s = X2[:].rearrange("p (s w x) -> p s w x", w=2, x=128)
        X3s = X3[:].rearrange("p (s x) -> p s x", x=128)
        PW = self.PW

        def sl_iter():
            for blk in range(4):
                for hd in range(2):
                    yield blk, hd, blk * 2 + hd, slice(hd * 64, hd * 64 + 64), slice(blk * 128, blk * 128 + 128)
        for blk, hd, sl, hp, pc in sl_iter():
            b1 = PS[hd * 2 + blk // 2]
            b2 = PS[4 + hd * 2 + blk // 2]
            co = (blk % 2) * 256
            mm(K, b1, b1[:, co:co + 256], AR[hp, 0, pc], BK[hp, :, pc], True, True, r=AR.all + BK.all)
            if emit_y:
                mm(K, b2, b2[:, co:co + 256], BK[hp, 0, pc], AR[hp, :, pc], True, True, r=AR.all + BK.all)
            else:
                mm(K, b2, b2[:, co:co + 128], BK[hp, 0, pc], AR[hp, 0, pc], True, True, r=AR.all + BK.all)
        m1v = cst["mask1"][:].rearrange("p (c w x) -> p c w x", w=2, x=128)
        m2v = cst["mask2"][:].rearrange("p (c w x) -> p c w x", w=2, x=128)
        X1v = X1[:].rearrange("p (b h w x) -> p b h w x", h=2, w=2, x=128)
        X2v = X2[:].rearrange("p (b h w x) -> p b h w x", h=2, w=2, x=128)
        X3v = X3[:].rearrange("p (b h x) -> p b h x", h=2, x=128)
        for hd in range(2):
            for bb in range(2):
                b1 = PS[hd * 2 + bb]
                b2 = PS[4 + hd * 2 + bb]
                DV(lambda e: e.tensor_tensor(out=X1v[:, bb * 2:bb * 2 + 2, hd],
                                             in0=b1[:].rearrange("p (c w x) -> p c w x", w=2, x=128), in1=m1v,
                                             op=ALU.mult), b1.all + cst["mask1"].all, X1.all)
                if emit_y:
                    DV(lambda e: e.tensor_tensor(out=X2v[:, bb * 2:bb * 2 + 2, hd],
                                                 in0=b2[:].rearrange("p (c w x) -> p c w x", w=2, x=128), in1=m2v,
                                                 op=ALU.mult), b2.all + cst["mask2"].all, X2.all)
                else:
                    DV(lambda e: e.tensor_tensor(out=X2v[:, bb * 2:bb * 2 + 2, hd, 0, :],
                                                 in0=b2[:].rearrange("p (c w x) -> p c w x", w=2, x=128)[:, :, 0, :],
                                                 in1=m2v[:, :, 0, :], op=ALU.mult), b2.all + cst["mask2"].all, X2.all)
        if emit_y:
            for blk, hd, sl, hp, pc in sl_iter():
                b3 = PS[hd * 2]
                mm(K, b3, b3[:, blk * 128:(blk + 1) * 128], BK[hp, 1, pc], AR[hp, 1, pc], True, True, r=AR.all + BK.all)
            for hd in range(2):
                b3 = PS[hd * 2]
                DV(lambda e: e.tensor_tensor(out=X3v[:, :, hd], in0=b3[:].rearrange("p (c x) -> p c x", x=128),
                                             in1=cst["mask3x2"][:].rearrange("p (c x) -> p c x", x=128),
                                             op=ALU.mult), b3.all + cst["mask3x2"].all, X3.all)
        Tt = self.Tt
        Tcur = Tt[0]
        DV(lambda e: e.tensor_tensor(out=Tcur[:].rearrange("p (s x) -> p s x", x=128), in0=X1s[:, :, 0, :],
                                     in1=cst["id128x8"][:].rearrange("p (s x) -> p s x", x=128), op=ALU.add),
           X1.all + cst["id128x8"].all, Tcur.all)
        Acur = (lambda sl: X2s[:, sl, 0, :], X2.all)
        ATcur = (lambda sl: X1s[:, sl, 0, :], X1.all)
        ti = 0
        WA_, WAT_, WT_ = PW[0], PW[1], PW[2]
        for k in range(6):
            do_sq = k <= 4
            do_sqT = k <= 3
            do_T = k >= 1
            for sl in range(8):
                so = slice(sl * 128, sl * 128 + 128)
                if do_sq:
                    mm(K, WA_, WA_[:, so], ATcur[0](sl), Acur[0](sl), True, True, r=Acur[1] + ATcur[1])
                if do_sqT:
                    mm(K, WAT_, WAT_[:, so], Acur[0](sl), ATcur[0](sl), True, True, r=Acur[1] + ATcur[1])
                if do_T:
                    mm(K, WT_, WT_[:, so], Acur[0](sl), Tcur[:, so], True, True, r=Acur[1] + Tcur.all)
            if do_T:
                Tn = Tt[(ti + 1) % 2]
                DV(lambda e: e.tensor_tensor(out=Tn[:], in0=WT_[:], in1=Tcur[:], op=ALU.add),
                   WT_.all + Tcur.all, Tn.all)
                Tcur = Tn
                ti += 1
            if do_sq:
                An = self.A[k % 2]
                AC(lambda e: e.activation(out=An[:], in_=WA_[:], func=AF.Identity), WA_.all, An.all)
                if do_sqT:
                    ATn = self.AT[k % 2]
                    AC(lambda e: e.activation(out=ATn[:], in_=WAT_[:], func=AF.Identity), WAT_.all, ATn.all)
                    ATcur = ((lambda b: (lambda sl: b[:, sl * 128:sl * 128 + 128]))(ATn), ATn.all)
                Acur = ((lambda b: (lambda sl: b[:, sl * 128:sl * 128 + 128]))(An), An.all)
        WX1, WX2 = PW[3], PW[0]
        for blk, hd, sl, hp, pc in sl_iter():
            so = slice(sl * 128, sl * 128 + 128)
            if emit_y:
                mm(K, WX1, WX1[:, so], Tcur[:, so], X2s[:, sl, 1, :], True, True, r=Tcur.all + X2.all)
            mm(K, WX2, WX2[:, so], Tcur[:, so], BhB[:, sl, :], True, True, r=Tcur.all + BhB.all)
        if emit_y:
            AC(lambda e: e.activation(out=X1b[:], in_=WX1[:], func=AF.Identity), WX1.all, X1b.all)
        AC(lambda e: e.activation(out=X2b[:], in_=WX2[:], func=AF.Identity), WX2.all, X2b.all)
        WP, WGh, WG = PW[2], PW[3], PW[0]
        p3v = WP[:].rearrange("p (c x) -> p c x", x=128)
        for blk, hd, sl, hp, pc in sl_iter():
            so = slice(sl * 128, sl * 128 + 128)
            atT = TT[:, 0, blk * 128 + hd * 64:blk * 128 + hd * 64 + 64]
            if emit_y:
                mm(K, PS[2], PS[2][hp, blk * 128:(blk + 1) * 128], atT, X1b[:, so], True, True, r=TT.all + X1b.all)
            for e2 in range(2):
                mm(K, WP, p3v[hp, blk * 2 + e2, hd * 64:hd * 64 + 64], atT,
                   X2b[:, sl * 128 + e2 * 64:sl * 128 + e2 * 64 + 64], True, True, r=TT.all + X2b.all)
            if emit_y:
                mm(K, WGh, WGh[:, so], X1s[:, sl, 1, :], X1b[:, so], True, True, r=X1.all + X1b.all)
            mm(K, WG, WG[:, so], X1s[:, sl, 1, :], X2b[:, so], True, True, r=X1.all + X2b.all)
        for hd in range(2):
            hp = slice(hd * 64, hd * 64 + 64)
            if emit_y:
                DV(lambda e: e.tensor_tensor(out=Rbd[hp, :, hd, :],
                                             in0=PS[2][hp, :].rearrange("p (c t) -> p c t", t=64),
                                             in1=AR[hp, 1, :].rearrange("p (c t) -> p c t", t=64),
                                             op=ALU.add), PS[2].all + AR.all, Rbd.all)
            AC(lambda e: e.activation(out=Pbd[hp, :, hd * 64:hd * 64 + 64],
                                      in_=p3v[hp, :, hd * 64:hd * 64 + 64], func=AF.Identity), WP.all, Pbd.all)
        if emit_y:
            DV(lambda e: e.tensor_tensor(out=Ghb[:], in0=WGh[:].rearrange("p (s x) -> p s x", x=128), in1=X3s,
                                         op=ALU.add), WGh.all + X3.all, Ghb.all)
        DV(lambda e: e.tensor_tensor(out=Gbd[:], in0=WG[:].rearrange("p (s x) -> p s x", x=128), in1=KhB[:], op=ALU.add),
           WG.all + KhB.all, Gbd.all)
        PY, PSt = PS[6], PS[7]
        for c in range(8):
            blk, e2 = c // 2, c % 2
            Sc = self.St[pair][self.stn[pair] % 3]
            Sn = self.St[pair][(self.stn[pair] + 1) % 3]
            self.stn[pair] += 1
            cc = slice(c * 64, c * 64 + 64)
            mm(K, PSt, PSt[:, cc], Pbd[:, c, :], Sc[:], True, False, r=Pbd.all + Sc.all)
            for hd in range(2):
                hp = slice(hd * 64, hd * 64 + 64)
                vT = TT[:, 3, blk * 128 + hd * 64:blk * 128 + hd * 64 + 64]
                es_ = slice(e2 * 64, e2 * 64 + 64)
                if emit_y:
                    mm(K, PY, PY[hp, cc], Sc[:], Rbd[:, c, hd, :], True, False, r=Sc.all + Rbd.all)
                    mm(K, PY, PY[hp, cc], vT, Ghb[:, blk * 2 + hd, es_], False, True, r=TT.all + Ghb.all)
                mm(K, PSt, PSt[hp, cc], Gbd[:, blk * 2 + hd, es_], vT, False, True, r=TT.all + Gbd.all)
            DV(lambda e: e.scalar_tensor_tensor(out=Sn[:], in0=Sc[:], scalar=W[:, c * 64 + 63:c * 64 + 64],
                                                in1=PSt[:, cc], op0=ALU.mult, op1=ALU.add),
               Sc.all + W.all + PSt.all, Sn.all)
        if not emit_y:
            return
        bd64, bd1 = cst["bd64"], cst["bd1"]
        AC(lambda e: e.activation(out=t1[:], in_=PY[:], func=AF.Identity), PY.all, t1.all)
        AC(lambda e: e.activation(out=t2[:], in_=PY[:], func=AF.Square), PY.all, t2.all)
        mm(K, PS[0], PS[0][:], bd64[:], t1[:], True, True, r=bd64.all + t1.all)
        mm(K, PS[1], PS[1][:], bd64[:], t2[:], True, True, r=bd64.all + t2.all)
        AC(lambda e: e.activation(out=t2[:], in_=PS[0][:], func=AF.Square), PS[0].all, t2.all)
        DV(lambda e: e.tensor_tensor(out=t2[:], in0=PS[1][:], in1=t2[:], op=ALU.subtract), PS[1].all + t2.all, t2.all)
        DV(lambda e: e.tensor_scalar(out=t2[:], in0=t2[:], scalar1=0.0, scalar2=None, op0=ALU.max), t2.all, t2.all)
        AC(lambda e: e.activation(out=t2[:], in_=t2[:], func=AF.Sqrt, bias=GN_EPS), t2.all, t2.all)
        DV(lambda e: e.reciprocal(out=t2[:], in_=t2[:]), t2.all, t2.all)
        DV(lambda e: e.tensor_tensor(out=t1[:], in0=t1[:], in1=PS[0][:], op=ALU.subtract), t1.all + PS[0].all, t1.all)
        DV(lambda e: e.tensor_tensor(out=t1[:], in0=t1[:], in1=t2[:], op=ALU.mult), t1.all + t2.all, t1.all)
        AC(lambda e: e.activation(out=t1[:], in_=t1[:], func=AF.Identity, scale=col(20), bias=col(22)),
           t1.all + prm.all, t1.all)
        DV(lambda e: e.scalar_tensor_tensor(out=t3[:], in0=p_r[:], scalar=col(18), in1=kf[:], op0=ALU.mult,
                                            op1=ALU.mult), p_r.all + kf.all + prm.all, t3.all)
        mm(K, PS[2], PS[2][:], bd1[:], t3[:], True, True, r=bd1.all + t3.all)
        DV(lambda e: e.tensor_tensor(out=t3[:], in0=PS[2][:], in1=p_v[:], op=ALU.mult), PS[2].all + p_v.all, t3.all)
        DV(lambda e: e.tensor_tensor(out=t1[:], in0=t1[:], in1=t3[:], op=ALU.add), t1.all + t3.all, t1.all)
        yo = self.yo[self.nyo % 2]
        self.nyo += 1
        DV(lambda e: e.tensor_tensor(out=yo[:], in0=t1[:], in1=g[:], op=ALU.mult), t1.all + g.all, yo.all)
        if ydep is None:
            K.dma("sp", ya_ap, yo[:], r=yo.all, sem_dep=yo.d[0], is_output=True)
        else:
            K.scratch_events.append(K.dma("sp", ya_ap, yo[:], r=yo.all, sem_dep=yo.d[0]))


def declare_phase1_io(nc, io):
    for name, (n, dt) in CONST_SHAPES.items():
        io["c_" + name] = declare(nc, "c_" + name, [128, n], dt, "ExternalInput")
    io["prm"] = declare(nc, "prm", [128, 32], F32, "ExternalInput")
    io["w2s"] = declare(nc, "w2s", [64, 256], F32, "ExternalInput")
    io["a2s"] = declare(nc, "a2s", [64, 256], F32, "ExternalInput")
    io["g2s"] = declare(nc, "g2s", [160, 256], F32, "ExternalInput")
    io["xT"] = declare(nc, "xT", [D, S], F32, "ExternalInput")
    io["w1c"] = declare(nc, "w1c", [16, 128, 16, 128], F32, "ExternalInput")


def build_phase1_only(n_tiles=16, do_rwkv=True, do_sb=True):
    nc = bass.Bass("TRN2", target_bir_lowering=False)
    io = {}
    io["cT"] = declare(nc, "cT", [128, 16], F32, "ExternalInput")
    io["b_adaT"] = declare(nc, "b_adaT", [128, 96], F32, "ExternalInput")
    io["norm_gT"] = declare(nc, "norm_gT", [128, 64], F32, "ExternalInput")
    io["w_ada"] = declare(nc, "w_ada", [D, 6 * D], F32, "ExternalInput")
    declare_phase1_io(nc, io)
    io["yaT"] = declare(nc, "yaT", [256, S], BF16, "ExternalOutput")
    io["ybT"] = declare(nc, "ybT", [256, S], BF16, "ExternalOutput")
    K = Ctx(nc)
    with K.es:
        cm = setup_common(K, io, need_f=False)
        phase1(K, io, cm, n_tiles=n_tiles, do_rwkv=do_rwkv, do_sb=do_sb)
        K.finish()
    print("phase1: ninst", K.ninst, "nwaits", K.nwaits, "nsems", K.nsems)
    return nc


def phase1_inputs(inp, b, g, consts, light=False):
    cs = slice(256 * g, 256 * g + 256)
    W = inp["w_in"][0]
    def colsel(c0):
        return W[:, c0:c0 + 1024][:, cs]
    blocks = []
    for base in (0, 1024, 2048):
        w = colsel(base)
        blocks += [w[:, 0:128], w[:, 128:256]]
    def pad(w):
        out = np.zeros((D, 128), np.float32)
        out[:, :w.shape[1]] = w
        return out
    blocks += [pad(W[:, 3072:3136]), pad(W[:, 3136:3200]), W[:, 3200:3328], pad(W[:, 3328:3360])]
    for base in (3360, 4384, 5408):
        w = colsel(base)
        blocks += [w[:, 0:128], w[:, 128:256]]
    w1c = np.stack([np.ascontiguousarray(blk.reshape(16, 128, 128).transpose(1, 0, 2)) for blk in blocks])
    mu = inp["mu_shift"][0]
    prm = np.zeros((128, 32), np.float32)
    def padv(v):
        out = np.zeros(128, np.float32)
        out[:v.shape[0]] = v
        return out
    mus = [mu[0:1024][cs][0:128], mu[0:1024][cs][128:256], mu[1024:2048][cs][0:128], mu[1024:2048][cs][128:256],
           mu[2048:3072][cs][0:128], mu[2048:3072][cs][128:256], padv(mu[3072:3136]), padv(mu[3136:3200]),
           mu[3200:3328], padv(mu[3328:3360])]
    for i, v in enumerate(mus):
        prm[:, i] = v
    for j, key in enumerate(["w0", "a0", "k_k", "k_a"]):
        v = inp[key][0][cs]
        prm[:, 10 + 2 * j] = v[0:128]
        prm[:, 11 + 2 * j] = v[128:256]
    rk = inp["r_k"][0].reshape(-1)[cs]
    prm[:, 18], prm[:, 19] = rk[0:128], rk[128:256]
    for j, key in enumerate(["ln_x_w", "ln_x_b"]):
        v = inp[key][0][cs]
        prm[:, 20 + 2 * j] = v[0:128]
        prm[:, 21 + 2 * j] = v[128:256]
    m = {} if light else common_inputs(inp, b)
    if not light:
        m.update({"c_" + k: v for k, v in consts.items()})
        m["xT"] = np.ascontiguousarray(inp["x"][b].T)
    m["prm"] = prm
    m["w2s"] = np.ascontiguousarray(inp["w2"][0][:, cs])
    m["a2s"] = np.ascontiguousarray(inp["a2"][0][:, cs])
    m["g2s"] = np.ascontiguousarray(inp["g2"][0][:, cs])
    m["w1c"] = w1c
    return m


def build_fused(n_tiles=16, own_from=12, n_pass=2):
    nc = bass.Bass("TRN2", target_bir_lowering=False)
    io = {}
    io["cT"] = declare(nc, "cT", [128, 16], F32, "ExternalInput")
    io["b_adaT"] = declare(nc, "b_adaT", [128, 96], F32, "ExternalInput")
    io["norm_gT"] = declare(nc, "norm_gT", [128, 64], F32, "ExternalInput")
    io["w_ada"] = declare(nc, "w_ada", [D, 6 * D], F32, "ExternalInput")
    for name, (n, dt) in CONST_SHAPES.items():
        io["c_" + name] = declare(nc, "c_" + name, [128, n], dt, "ExternalInput")
    io["prm"] = declare(nc, "prm", [4, 128, 32], F32, "ExternalInput")
    io["w2s"] = declare(nc, "w2s", [4, 64, 256], F32, "ExternalInput")
    io["a2s"] = declare(nc, "a2s", [4, 64, 256], F32, "ExternalInput")
    io["g2s"] = declare(nc, "g2s", [4, 160, 256], F32, "ExternalInput")
    io["xT"] = declare(nc, "xT", [D, S], F32, "ExternalInput")
    io["tokmask"] = declare(nc, "tokmask", [128, S], BF16, "ExternalInput")
    io["w1c"] = declare(nc, "w1c", [4, 16, 128, 16, 128], F32, "ExternalInput")
    io["x2T"] = declare(nc, "x2T", [D, 2048], F32, "ExternalInput")
    io["w_gate"] = declare(nc, "w_gate", [D, 2 * D], F32, "ExternalInput")
    io["w_up_rwkv"] = declare(nc, "w_up_rwkv", [C, D], F32, "ExternalInput")
    io["w_up_sb"] = declare(nc, "w_up_sb", [C, D], F32, "ExternalInput")
    io["w_out"] = declare(nc, "w_out", [D, D], F32, "ExternalInput")
    io["w_mlp_in"] = declare(nc, "w_mlp_in", [D, DFF], F32, "ExternalInput")
    io["w_mlp_out"] = declare(nc, "w_mlp_out", [DFF, D], F32, "ExternalInput")
    io["outT"] = declare(nc, "outT", [D, 2048], F32, "ExternalOutput")
    io["x1s"] = declare(nc, "x1s", [D, 2048], F32, "Internal")
    K = Ctx(nc)
    with K.es:
        cm = setup_common(K, io, need_f=True)
        ysc = K.dram("ysc", [2048, 2048], BF16)
        with K.scope():
            phase1(K, io, cm, n_tiles=n_tiles, groups=(0, 1, 2, 3), own_from=own_from, ysc=ysc)
        for E_ in K.engs.values():
            for ev_ in K.scratch_events:
                E_.wait_for(ev_)
        io["yT"] = ysc.t
        phase2(K, io, cm, n_pass=n_pass)
        K.finish()
    print("fused: ninst", K.ninst, "nwaits", K.nwaits, "nsems", K.nsems)
    return nc


def fused_inputs(inp, b, q, consts, n_tiles=16, own_from=12):
    own = (n_tiles - own_from) * 512
    n_real = (q + 1) * own
    n_seq = n_tiles * 512
    xT = np.zeros((D, S), np.float32)
    xT[:, n_seq - n_real:n_seq] = inp["x"][b, 0:n_real].T
    tokmask = np.zeros((128, S), ml_dtypes.bfloat16)
    tokmask[:, n_seq - n_real:n_seq] = 1.0
    m = common_inputs(inp, b)
    m.update({"c_" + k: v for k, v in consts.items()})
    per_g = [phase1_inputs(inp, b, g, consts, light=True) for g in range(4)]
    for key in ("prm", "w2s", "a2s", "g2s", "w1c"):
        m[key] = np.stack([pg[key] for pg in per_g])
    m["xT"] = xT
    m["tokmask"] = tokmask
    x2T = np.zeros((D, 2048), np.float32)
    x2T[:, 0:own] = inp["x"][b, q * own:(q + 1) * own].T
    m["x2T"] = x2T
    m["w_gate"] = np.ascontiguousarray(inp["w_in"][0][:, 6432:])
    for k in ["w_up_rwkv", "w_up_sb", "w_out", "w_mlp_in", "w_mlp_out"]:
        m[k] = inp[k][0]
    return m


def kernel(**inputs):
    inp = {k: np.asarray(v) for k, v in inputs.items()}
    consts = host_consts()
    nc = build_fused()
    in_maps = [fused_inputs(inp, c // 4, c % 4, consts) for c in range(NCORES)]
    res = run_bass_kernel_spmd(nc, in_maps, core_ids=list(range(NCORES)))
    out = np.zeros((NB, S, D), np.float32)
    for c in range(NCORES):
        b, q = c // 4, c % 4
        out[b, q * 2048:(q + 1) * 2048] = res.results[c]["outT"].T
    return out
```

```python
import contextlib
import numpy as np
import ml_dtypes
import concourse.bass as bass
import concourse.mybir as mybir
from concourse.bass_utils import run_bass_kernel_spmd

F32 = mybir.dt.float32
BF16 = mybir.dt.bfloat16
AF = mybir.ActivationFunctionType
ALU = mybir.AluOpType

D = 2048
S = 8192
NB = 2
C = 1024
DFF = 8192
NCORES = 8
NORM_EPS = 1e-6
GN_EPS = 64e-5
SEM_LIMIT = 30000
STOP_AT = 0


class Dep:
    __slots__ = ("name", "lw", "rd", "dsem", "dcnt")

    def __init__(self, name=""):
        self.name = name
        self.lw = []
        self.rd = {}
        self.dsem = None
        self.dcnt = 0


class Eng:
    def __init__(self, K, name, h):
        self.K = K
        self.name = name
        self.h = h
        self.sem = None
        self.cnt = 0
        self.seen = {}
        self.nsem = 0

    def new_sem(self):
        self.sem = self.K.es.enter_context(self.K.nc.semaphore(f"e_{self.name}_{self.nsem}"))
        self.nsem += 1
        self.K.nsems += 1
        self.cnt = 0

    def wait_for(self, ev):
        sem, val, _ = ev
        key = id(sem)
        if self.seen.get(key, 0) < val:
            self.h.wait_ge(sem, val)
            self.seen[key] = val
            self.K.nwaits += 1


class Buf:
    def __init__(self, t, name, nparts=1):
        self.t = t
        self.name = name
        self.d = [Dep(f"{name}.{i}") for i in range(nparts)]

    def __getitem__(self, k):
        return self.t[k]

    def view(self, ap, name):
        b = Buf(ap, name, 1)
        return b

    @property
    def all(self):
        return list(self.d)


class Ctx:
    def __init__(self, nc):
        self.nc = nc
        self.es = contextlib.ExitStack()
        self.sem_es = self.es
        self.scopes = []
        self.nsems = 0
        self.nwaits = 0
        self.ninst = 0
        self.engs = {
            "pe": Eng(self, "pe", nc.tensor),
            "act": Eng(self, "act", nc.scalar),
            "dve": Eng(self, "dve", nc.vector),
            "pool": Eng(self, "pool", nc.gpsimd),
            "sp": Eng(self, "sp", nc.sync),
        }
        for e in self.engs.values():
            e.new_sem()
        self.out_events = []
        self.scratch_events = []

    def sbuf(self, name, shape, dt, nparts=1):
        es = self.scopes[-1][0] if self.scopes else self.es
        t = es.enter_context(self.nc.sbuf_tensor(name, list(shape), dt))
        b = Buf(t, name, nparts)
        if self.scopes:
            self.scopes[-1][1].append(b)
        return b

    @contextlib.contextmanager
    def scope(self):
        es = contextlib.ExitStack()
        self.scopes.append((es, []))
        try:
            yield
        finally:
            _, bufs = self.scopes.pop()
            deps = [d for b in bufs for d in b.d]
            self.barrier(deps)
            es.close()

    def psum(self, name, shape, dt=F32, nparts=1):
        es = self.scopes[-1][0] if self.scopes else self.es
        t = es.enter_context(self.nc.psum_tensor(name, list(shape), dt))
        b = Buf(t, name, nparts)
        if self.scopes:
            self.scopes[-1][1].append(b)
        return b

    def dram(self, name, shape, dt, kind="Internal", nparts=1):
        t = self.nc.dram_tensor(name, list(shape), dt, kind=kind)
        return Buf(t.ap(), name, nparts)

    def _deps(self, E, r, w, same_raw=True):
        for d in r:
            for ev in d.lw:
                if ev[2] == E.name and not same_raw:
                    continue
                E.wait_for(ev)
        for d in w:
            for ev in d.lw:
                if ev[2] == E.name and not same_raw:
                    continue
                E.wait_for(ev)
            for ev in d.rd.values():
                if ev[2] == E.name and not same_raw:
                    continue
                E.wait_for(ev)

    def op(self, eng, fn, r=(), w=()):
        E = self.engs[eng]
        if E.cnt >= SEM_LIMIT:
            E.new_sem()
        self._deps(E, r, w, same_raw=(eng != "pe"))
        inst = fn(E.h)
        E.cnt += 1
        inst.then_inc(E.sem, 1)
        self.ninst += 1
        ev = (E.sem, E.cnt, E.name)
        for d in r:
            d.rd[E.name] = ev
        for d in w:
            d.lw = [ev]
            d.rd = {}
        return ev

    def dma(self, queue, out, in_, r=(), w=(), sem_dep=None, is_output=False, waw=True, **kw):
        E = self.engs[queue]
        if waw:
            self._deps(E, r, w, same_raw=True)
        else:
            self._deps(E, r, (), same_raw=True)
            for d in w:
                for ev in d.rd.values():
                    E.wait_for(ev)
        d0 = sem_dep or (w[0] if len(w) else r[0])
        if d0.dsem is None or d0.dcnt >= SEM_LIMIT:
            d0.dsem = self.es.enter_context(self.nc.semaphore(f"d{self.nsems}"))
            self.nsems += 1
            d0.dcnt = 0
        inst = E.h.dma_start(out=out, in_=in_, **kw)
        d0.dcnt += 16
        inst.then_inc(d0.dsem, 16)
        self.ninst += 1
        ev = (d0.dsem, d0.dcnt, "dma" + str(id(d0)))
        for d in r:
            d.rd[ev[2]] = ev
        for d in w:
            d.lw = [ev]
            d.rd = {}
        if is_output:
            self.out_events.append(ev)
        return ev

    def barrier(self, deps=()):
        evs = []
        for E in self.engs.values():
            if E.cnt > 0:
                evs.append((E.sem, E.cnt, E.name))
        for d in deps:
            evs += d.lw
            evs += list(d.rd.values())
        for E in self.engs.values():
            for ev in evs:
                if ev[2] != E.name:
                    E.wait_for(ev)

    def finish(self):
        E = self.engs["sp"]
        for ev in self.out_events:
            E.wait_for(ev)
        for name, e2 in self.engs.items():
            if name != "sp" and e2.cnt > 0:
                E.wait_for((e2.sem, e2.cnt, name))


def mm(K, ps, ps_ap, lhsT, rhs, start, stop, r=(), w=None):
    wd = w if w is not None else ps.all
    return K.op("pe", lambda e: e.matmul(ps_ap, lhsT=lhsT, rhs=rhs, start=start, stop=stop), r=r, w=wd)


def load_w_cast(K, dst_buf, dst_ap, src_ap, w=None):
    return K.dma("pool", dst_ap, src_ap, w=(w if w is not None else dst_buf.all), waw=False)


class Common:
    pass


def setup_common(K, io, need_f=True):
    nc = K.nc
    cm = Common()
    cm.ones_bf = K.sbuf("ones_bf", [128, 128], BF16)
    K.op("dve", lambda e: e.memset(cm.ones_bf[:], 1.0), w=cm.ones_bf.all)
    cT = K.sbuf("cT_sb", [128, 16], F32)
    K.dma("sp", cT[:], io["cT"], w=cT.all)
    sc = K.sbuf("sc", [128, 16], BF16)
    K.op("act", lambda e: e.activation(out=sc[:], in_=cT[:], func=AF.Silu), r=cT.all, w=sc.all)
    bada = K.sbuf("bada", [128, 96], F32)
    K.dma("sp", bada[:], io["b_adaT"], w=bada.all)
    ng = K.sbuf("ng", [128, 64], F32)
    K.dma("sp", ng[:], io["norm_gT"], w=ng.all)
    nmod = 6 if need_f else 2
    modT = K.sbuf("modT", [128, 96], F32)
    cm.A1 = K.sbuf("A1", [128, 16], F32)
    if need_f:
        cm.Cm = K.sbuf("Cm", [128, 16], F32)
        cm.A2 = K.sbuf("A2", [128, 16], F32)
        cm.Cf = K.sbuf("Cf", [128, 16], F32)
    with K.scope():
        _setup_common_body(K, io, cm, need_f, nmod, modT, sc, bada, ng)
    cm.modT = modT
    cm.B1 = modT
    return cm


def _setup_common_body(K, io, cm, need_f, nmod, modT, sc, bada, ng):
    wst = [K.sbuf(f"wada{i}", [128, 16, 512], BF16) for i in range(2)]
    psm = K.psum("ps_mod", [128, 512], F32)
    w_ada = io["w_ada"].rearrange("(kc p) n -> p kc n", p=128)
    ngrp = nmod * 4
    for g in range(ngrp):
        wt = wst[g % 2]
        load_w_cast(K, wt, wt[:], w_ada[:, :, g * 512:(g + 1) * 512])
        for j in range(4):
            col = g * 4 + j
            for kc in range(16):
                mm(K, psm, psm[:, col:col + 1], wt[:, kc, j * 128:(j + 1) * 128], sc[:, kc:kc + 1],
                   start=(kc == 0), stop=(kc == 15), r=wt.all + sc.all)
    ncol = ngrp * 4
    K.op("dve", lambda e: e.tensor_tensor(out=modT[:, 0:ncol], in0=psm[:, 0:ncol], in1=bada[:, 0:ncol], op=ALU.add),
         r=psm.all + bada.all, w=modT.all)
    K.op("dve", lambda e: e.scalar_tensor_tensor(out=cm.A1[:], in0=modT[:, 16:32], scalar=1.0, in1=ng[:, 0:16],
                                                  op0=ALU.add, op1=ALU.mult), r=modT.all + ng.all, w=cm.A1.all)
    if need_f:
        K.op("dve", lambda e: e.tensor_tensor(out=cm.Cm[:], in0=modT[:, 32:48], in1=ng[:, 16:32], op=ALU.mult),
             r=modT.all + ng.all, w=cm.Cm.all)
        K.op("dve", lambda e: e.scalar_tensor_tensor(out=cm.A2[:], in0=modT[:, 64:80], scalar=1.0, in1=ng[:, 32:48],
                                                      op0=ALU.add, op1=ALU.mult), r=modT.all + ng.all, w=cm.A2.all)
        K.op("dve", lambda e: e.tensor_tensor(out=cm.Cf[:], in0=modT[:, 80:96], in1=ng[:, 48:64], op=ALU.mult),
             r=modT.all + ng.all, w=cm.Cf.all)


def rstd_from_ss(K, ss_ps, sq_t, rstd_t, n):
    K.op("act", lambda e: e.activation(out=sq_t[:], in_=ss_ps[:], func=AF.Ln, scale=1.0 / n, bias=NORM_EPS),
         r=ss_ps.all, w=sq_t.all)
    K.op("act", lambda e: e.activation(out=rstd_t[:], in_=sq_t[:], func=AF.Exp, scale=-0.5), r=sq_t.all, w=rstd_t.all)


def phase2(K, io, cm, n_pass=2):
    nc = K.nc
    T = 1024
    ones = cm.ones_bf
    R1 = K.sbuf("R1", [128, 16, T], BF16, nparts=32)
    R2 = K.sbuf("R2", [128, 16, T], BF16, nparts=32)
    Fb = K.sbuf("Fb", [128, 16, T], F32, nparts=16)
    merged_v = Fb.t[:, 0:8, :].bitcast(BF16)
    WA = [K.sbuf(f"WA{i}", [128, 16 * 512], BF16) for i in range(2)]
    WB = [K.sbuf(f"WB{i}", [128, 16 * 256], BF16) for i in range(2)]
    xs = [K.sbuf(f"xs{i}", [128, 512], F32) for i in range(3)]
    sqb = [K.sbuf(f"sqb{i}", [128, 512], BF16) for i in range(2)]
    tt = [K.sbuf(f"tt{i}", [128, 512], F32) for i in range(2)]
    sg = [K.sbuf(f"sg{i}", [128, 512], F32) for i in range(2)]
    sqrt_t = K.sbuf("sqrt_t", [128, 512], F32)
    rstd = K.sbuf("rstd", [128, 512], F32)
    PS = [K.psum(f"ps{i}", [128, 512], F32) for i in range(8)]

    def merged_ap(m, half):
        off = (m % 2) * T + half * 512
        return merged_v[:, m // 2, off:off + 512]

    def merged_dep(m):
        return [Fb.d[m // 2]]

    def mix_ap(m):
        off = (m % 2) * 512
        return Fb.t[:, 8 + m // 2, off:off + 512]

    def mix_dep(m):
        return [Fb.d[8 + m // 2]]

    wg = io["w_gate"].rearrange("(kc p) n -> p kc n", p=128)
    wua = io["w_up_rwkv"].rearrange("(kc p) n -> p kc n", p=128)
    wub = io["w_up_sb"].rearrange("(kc p) n -> p kc n", p=128)
    wo = io["w_out"].rearrange("(kc p) n -> p kc n", p=128)
    w1 = io["w_mlp_in"].rearrange("(kc p) n -> p kc n", p=128)
    w2 = io["w_mlp_out"].rearrange("(kc p) n -> p kc n", p=128)
    xT = io["x2T"].rearrange("(c p) t -> p c t", p=128)
    yT = io["yT"].rearrange("(c p) t -> p c t", p=128)
    outT = io["outT"].rearrange("(c p) t -> p c t", p=128)
    x1s = io["x1s"].rearrange("(c p) t -> p c t", p=128)
    x1dep = [Dep(f"x1s{i}") for i in range(4)]

    cnt = {"xs": 0, "sq": 0, "tt": 0, "sg": 0, "wa": 0, "wb": 0, "ps": 0}

    def nxt(key, lst):
        b = lst[cnt[key] % len(lst)]
        cnt[key] += 1
        return b

    def stats_norm(src_ap_fn, src_dep_fn, ss):
        for m in range(16):
            sq = nxt("sq", sqb)
            K.op("act", lambda e: e.activation(out=sq[:], in_=src_ap_fn(m), func=AF.Square),
                 r=src_dep_fn(m), w=sq.all)
            mm(K, ss, ss[:], ones[:], sq[:], start=(m == 0), stop=(m == 15), r=sq.all + ones.all)
        rstd_from_ss(K, ss, sqrt_t, rstd, float(D))

    for p in range(n_pass):
        t0 = p * T
        for c in range(16):
            evl = K.dma("sp", R2[:, c, :], yT[:, c, t0:t0 + T], w=[R2.d[2 * c], R2.d[2 * c + 1]], sem_dep=R2.d[0], waw=False)
        for d_ in R2.d:
            d_.lw = [evl]
        for half in range(2):
            tk = t0 + half * 512
            ss = PS[7]
            xst = Fb.t[:, 0:8, :].rearrange("p a (b t) -> p (a b) t", t=512)
            for c in range(16):
                K.dma("sp", xst[:, c, :], xT[:, c, tk:tk + 512], w=[Fb.d[c // 2]], sem_dep=Fb.d[c // 2], waw=False)
            stats_norm(lambda m: xst[:, m, :], lambda m: [Fb.d[m // 2]], ss)
            for c in range(16):
                t_ = nxt("tt", tt)
                K.op("dve", lambda e: e.tensor_tensor(out=t_[:], in0=xst[:, c, :], in1=rstd[:], op=ALU.mult),
                     r=[Fb.d[c // 2]] + rstd.all, w=t_.all)
                K.op("act", lambda e: e.activation(out=R1[:, c, half * 512:(half + 1) * 512], in_=t_[:],
                                                    func=AF.Identity, scale=cm.A1[:, c:c + 1],
                                                    bias=cm.B1[:, c:c + 1]),
                     r=t_.all + cm.A1.all + cm.B1.all, w=[R1.d[2 * c + half]])
        for jg in range(8):
            wa = nxt("wa", WA)
            wb = nxt("wb", WB)
            wav = wa.t[:, :].rearrange("p (a n) -> p a n", n=256)
            wbv = wb.t[:, :].rearrange("p (a n) -> p a n", n=256)
            c0 = jg * 256
            load_w_cast(K, wa, wav[:, 0:16, :], wg[:, :, c0:c0 + 256])
            load_w_cast(K, wa, wav[:, 16:32, :], wg[:, :, 2048 + c0:2048 + c0 + 256])
            load_w_cast(K, wb, wbv[:, 0:8, :], wua[:, :, c0:c0 + 256])
            load_w_cast(K, wb, wbv[:, 8:16, :], wub[:, :, c0:c0 + 256])
            for jj in range(2):
                j = jg * 2 + jj
                for half in range(2):
                    hs = slice(half * 512, (half + 1) * 512)
                    base = (cnt["ps"] % 2) * 4
                    cnt["ps"] += 1
                    pga, pgb, pua, pub = PS[base], PS[base + 1], PS[base + 2], PS[base + 3]
                    for kc in range(16):
                        mm(K, pga, pga[:], wav[:, kc, jj * 128:(jj + 1) * 128], R1[:, kc, hs],
                           start=(kc == 0), stop=(kc == 15), r=wa.all + [R1.d[2 * kc + half]])
                    for kc in range(16):
                        mm(K, pgb, pgb[:], wav[:, 16 + kc, jj * 128:(jj + 1) * 128], R1[:, kc, hs],
                           start=(kc == 0), stop=(kc == 15), r=wa.all + [R1.d[2 * kc + half]])
                    for kc in range(8):
                        mm(K, pua, pua[:], wbv[:, kc, jj * 128:(jj + 1) * 128], R2[:, kc, hs],
                           start=(kc == 0), stop=(kc == 7), r=wb.all + [R2.d[2 * kc + half]])
                    for kc in range(8):
                        mm(K, pub, pub[:], wbv[:, 8 + kc, jj * 128:(jj + 1) * 128], R2[:, 8 + kc, hs],
                           start=(kc == 0), stop=(kc == 7), r=wb.all + [R2.d[2 * (8 + kc) + half]])
                    sa = nxt("sg", sg)
                    sb_ = nxt("sg", sg)
                    K.op("act", lambda e: e.activation(out=sa[:], in_=pga[:], func=AF.Sigmoid), r=pga.all, w=sa.all)
                    K.op("act", lambda e: e.activation(out=sb_[:], in_=pgb[:], func=AF.Sigmoid), r=pgb.all, w=sb_.all)
                    K.op("dve", lambda e: e.tensor_tensor(out=sa[:], in0=sa[:], in1=pua[:], op=ALU.mult),
                         r=sa.all + pua.all, w=sa.all)
                    K.op("dve", lambda e: e.tensor_tensor(out=sb_[:], in0=sb_[:], in1=pub[:], op=ALU.mult),
                         r=sb_.all + pub.all, w=sb_.all)
                    K.op("dve", lambda e: e.tensor_tensor(out=merged_ap(j, half), in0=sa[:], in1=sb_[:], op=ALU.add),
                         r=sa.all + sb_.all, w=merged_dep(j))
        for half in range(2):
            tk = t0 + half * 512
            ss = PS[7]
            for mg in range(4):
                wa = nxt("wa", WA)
                wav = wa.t[:, :].rearrange("p (a n) -> p a n", n=512)
                load_w_cast(K, wa, wav[:, :, :], wo[:, :, mg * 512:(mg + 1) * 512])
                for mj in range(4):
                    m = mg * 4 + mj
                    pm = PS[cnt["ps"] % 4]
                    cnt["ps"] += 1
                    for kc in range(16):
                        mm(K, pm, pm[:], wav[:, kc, mj * 128:(mj + 1) * 128], merged_ap(kc, half),
                           start=(kc == 0), stop=(kc == 15), r=wa.all + merged_dep(kc))
                    K.op("act", lambda e: e.activation(out=mix_ap(m), in_=pm[:], func=AF.Identity),
                         r=pm.all, w=mix_dep(m))
                    sq = nxt("sq", sqb)
                    K.op("act", lambda e: e.activation(out=sq[:], in_=pm[:], func=AF.Square), r=pm.all, w=sq.all)
                    mm(K, ss, ss[:], ones[:], sq[:], start=(m == 0), stop=(m == 15), r=sq.all + ones.all)
            rstd_from_ss(K, ss, sqrt_t, rstd, float(D))
            ss2 = PS[6]
            for m in range(16):
                xb = nxt("xs", xs)
                K.dma("sp", xb[:], xT[:, m, tk:tk + 512], w=xb.all)
                t_ = nxt("tt", tt)
                K.op("dve", lambda e: e.tensor_tensor(out=t_[:], in0=mix_ap(m), in1=rstd[:], op=ALU.mult),
                     r=mix_dep(m) + rstd.all, w=t_.all)
                K.op("dve", lambda e: e.scalar_tensor_tensor(out=mix_ap(m), in0=t_[:], scalar=cm.Cm[:, m:m + 1],
                                                              in1=xb[:], op0=ALU.mult, op1=ALU.add),
                     r=t_.all + xb.all + cm.Cm.all, w=mix_dep(m))
                ev_ = K.dma("sp", x1s[:, m, tk:tk + 512], mix_ap(m), r=mix_dep(m), sem_dep=mix_dep(m)[0])
                if m == 0:
                    x1dep[p * 2 + half].lw = []
                x1dep[p * 2 + half].lw.append(ev_)
                sq = nxt("sq", sqb)
                K.op("act", lambda e: e.activation(out=sq[:], in_=mix_ap(m), func=AF.Square), r=mix_dep(m), w=sq.all)
                mm(K, ss2, ss2[:], ones[:], sq[:], start=(m == 0), stop=(m == 15), r=sq.all + ones.all)
            rstd_from_ss(K, ss2, sqrt_t, rstd, float(D))
            for m in range(16):
                t_ = nxt("tt", tt)
                K.op("dve", lambda e: e.tensor_tensor(out=t_[:], in0=mix_ap(m), in1=rstd[:], op=ALU.mult),
                     r=mix_dep(m) + rstd.all, w=t_.all)
                K.op("act", lambda e: e.activation(out=R1[:, m, half * 512:(half + 1) * 512], in_=t_[:],
                                                    func=AF.Identity, scale=cm.A2[:, m:m + 1],
                                                    bias=cm.modT[:, 48 + m:49 + m]),
                     r=t_.all + cm.A2.all + cm.modT.all, w=[R1.d[2 * m + half]])
        for G in range(4):
            for kg in range(4):
                wa = nxt("wa", WA)
                wav = wa.t[:, :].rearrange("p (a n) -> p a n", n=512)
                f0 = G * 2048 + kg * 512
                load_w_cast(K, wa, wav[:, :, :], w1[:, :, f0:f0 + 512])
                for kj in range(4):
                    k = kg * 4 + kj
                    for half in range(2):
                        hs = slice(half * 512, (half + 1) * 512)
                        pa = PS[cnt["ps"] % 4]
                        cnt["ps"] += 1
                        for kc in range(16):
                            mm(K, pa, pa[:], wav[:, kc, kj * 128:(kj + 1) * 128], R1[:, kc, hs],
                               start=(kc == 0), stop=(kc == 15), r=wa.all + [R1.d[2 * kc + half]])
                        r_ = nxt("sg", sg)
                        K.op("act", lambda e: e.activation(out=r_[:], in_=pa[:], func=AF.Relu), r=pa.all, w=r_.all)
                        K.op("dve", lambda e: e.tensor_tensor(out=R2[:, k, hs], in0=r_[:], in1=r_[:], op=ALU.mult),
                             r=r_.all, w=[R2.d[2 * k + half]])
            for mp in range(8):
                wb = nxt("wb", WB)
                wbv = wb.t[:, :].rearrange("p (a n) -> p a n", n=256)
                load_w_cast(K, wb, wbv[:, :, :], w2[:, G * 16:(G + 1) * 16, mp * 256:(mp + 1) * 256])
                for mj in range(2):
                    m = mp * 2 + mj
                    for half in range(2):
                        hs = slice(half * 512, (half + 1) * 512)
                        pf = PS[4 + cnt["ps"] % 2]
                        cnt["ps"] += 1
                        for k in range(16):
                            mm(K, pf, pf[:], wbv[:, k, mj * 128:(mj + 1) * 128], R2[:, k, hs],
                               start=(k == 0), stop=(k == 15), r=wb.all + [R2.d[2 * k + half]])
                        if G == 0:
                            K.op("act", lambda e: e.activation(out=Fb.t[:, m, hs], in_=pf[:], func=AF.Identity),
                                 r=pf.all, w=[Fb.d[m]])
                        else:
                            K.op("dve", lambda e: e.tensor_tensor(out=Fb.t[:, m, hs], in0=Fb.t[:, m, hs], in1=pf[:],
                                                                  op=ALU.add), r=pf.all + [Fb.d[m]], w=[Fb.d[m]])
        for half in range(2):
            tk = t0 + half * 512
            hs = slice(half * 512, (half + 1) * 512)
            ss = PS[7]
            stats_norm(lambda m: Fb.t[:, m, hs], lambda m: [Fb.d[m]], ss)
            for m in range(16):
                xb = nxt("xs", xs)
                K.dma("sp", xb[:], x1s[:, m, tk:tk + 512], r=[x1dep[p * 2 + half]], w=xb.all)
                t_ = nxt("tt", tt)
                K.op("dve", lambda e: e.tensor_tensor(out=t_[:], in0=Fb.t[:, m, hs], in1=rstd[:], op=ALU.mult),
                     r=[Fb.d[m]] + rstd.all, w=t_.all)
                K.op("dve", lambda e: e.scalar_tensor_tensor(out=t_[:], in0=t_[:], scalar=cm.Cf[:, m:m + 1],
                                                              in1=xb[:], op0=ALU.mult, op1=ALU.add),
                     r=t_.all + xb.all + cm.Cf.all, w=t_.all)
                K.dma("sp", outT[:, m, tk:tk + 512], t_[:], r=t_.all, sem_dep=t_.d[0], is_output=True)


def declare(nc, name, shape, dt, kind):
    return nc.dram_tensor(name, list(shape), dt, kind=kind).ap()


def build_phase2_only():
    nc = bass.Bass("TRN2", target_bir_lowering=False)
    io = {}
    io["cT"] = declare(nc, "cT", [128, 16], F32, "ExternalInput")
    io["b_adaT"] = declare(nc, "b_adaT", [128, 96], F32, "ExternalInput")
    io["norm_gT"] = declare(nc, "norm_gT", [128, 64], F32, "ExternalInput")
    io["w_ada"] = declare(nc, "w_ada", [D, 6 * D], F32, "ExternalInput")
    io["x2T"] = declare(nc, "x2T", [D, 2048], F32, "ExternalInput")
    io["yT"] = declare(nc, "yT", [D, 2048], BF16, "ExternalInput")
    io["w_gate"] = declare(nc, "w_gate", [D, 2 * D], F32, "ExternalInput")
    io["w_up_rwkv"] = declare(nc, "w_up_rwkv", [C, D], F32, "ExternalInput")
    io["w_up_sb"] = declare(nc, "w_up_sb", [C, D], F32, "ExternalInput")
    io["w_out"] = declare(nc, "w_out", [D, D], F32, "ExternalInput")
    io["w_mlp_in"] = declare(nc, "w_mlp_in", [D, DFF], F32, "ExternalInput")
    io["w_mlp_out"] = declare(nc, "w_mlp_out", [DFF, D], F32, "ExternalInput")
    io["outT"] = declare(nc, "outT", [D, 2048], F32, "ExternalOutput")
    io["x1s"] = declare(nc, "x1s", [D, 2048], F32, "Internal")
    K = Ctx(nc)
    with K.es:
        cm = setup_common(K, io, need_f=True)
        phase2(K, io, cm)
        K.finish()
    print("phase2: ninst", K.ninst, "nwaits", K.nwaits, "nsems", K.nsems)
    return nc


def cols128(v):
    return np.ascontiguousarray(v.reshape(-1, 128).T)


def common_inputs(inp, b):
    return {
        "cT": cols128(inp["c"][b]),
        "b_adaT": cols128(inp["b_ada"][0]),
        "norm_gT": cols128(inp["norm_g"][0].reshape(-1)),
        "w_ada": inp["w_ada"][0],
    }


def host_consts():
    p = np.arange(128)[:, None]
    c128_ = np.arange(128)[None, :]
    same = (p // 64) == (c128_ // 64)
    lo = (same & ((c128_ % 64) < (p % 64))).astype(np.float32)
    ups = (same & ((p % 64) < (c128_ % 64))).astype(np.float32)
    upi = (same & ((p % 64) <= (c128_ % 64))).astype(np.float32)
    cst = {}
    cst["mask1"] = np.tile(lo, (1, 4))
    cst["mask2"] = np.tile(np.concatenate([ups, upi], axis=1), (1, 2))
    cst["mask3x2"] = np.tile(upi, (1, 4))
    cst["id128x8"] = np.tile((p == c128_).astype(np.float32), (1, 8))
    c128 = np.arange(128)[None, :]
    cst["ident"] = (p == c128).astype(np.float32)
    cst["negtri"] = -(p >= c128).astype(np.float32)
    t512 = np.arange(512)[None, :]
    cst["maskd"] = np.concatenate([((p + 128 * d) < t512).astype(np.float32) for d in range(4)], axis=1)
    bf = {k: v.astype(ml_dtypes.bfloat16) for k, v in cst.items()}
    bf["bd1"] = ((p // 64) == (c128 // 64)).astype(np.float32)
    bf["bd64"] = bf["bd1"] / 64.0
    bf["scanmask"] = np.tile((np.arange(512)[None, :] % 64 != 0).astype(np.float32), (128, 1))
    return bf


CONST_SHAPES = {"mask1": (512, BF16), "mask2": (512, BF16), "mask3x2": (512, BF16), "id128x8": (1024, BF16),
                "ident": (128, BF16), "negtri": (128, BF16), "maskd": (2048, BF16), "bd1": (128, F32),
                "bd64": (128, F32), "scanmask": (512, F32)}


def phase1(K, io, cm, n_tiles=16, do_rwkv=True, do_sb=True, groups=(None,), own_from=0, ysc=None):
    nc = K.nc
    ones = cm.ones_bf
    cst = {}
    for name, (n, dt) in CONST_SHAPES.items():
        cst[name] = K.sbuf("cs_" + name, [128, n], dt)
        K.dma("sp", cst[name][:], io["c_" + name], w=cst[name].all)
    prm = K.sbuf("prm_sb", [128, 32], F32)
    omk = K.sbuf("omk", [128, 2], F32)
    lw = K.sbuf("lw", [128, 4, 256], BF16)
    tokm = K.sbuf("tokm", [128, 512], BF16) if "tokmask" in io else None

    def load_group(g):
        sel = (lambda ap: ap) if g is None else (lambda ap: ap[g])
        K.dma("sp", prm[:], sel(io["prm"]), w=prm.all)
        K.op("dve", lambda e: e.tensor_scalar(out=omk[:], in0=prm[:, 16:18], scalar1=-1.0, scalar2=1.0,
                                               op0=ALU.mult, op1=ALU.add), r=prm.all, w=omk.all)
        K.dma("pool", lw[0:64, 0, :], sel(io["w2s"]), w=lw.all, waw=False)
        K.dma("pool", lw[0:64, 1, :], sel(io["a2s"]), w=lw.all, waw=False)
        K.dma("pool", lw[:, 2, :], sel(io["g2s"])[0:128, :], w=lw.all, waw=False)
        K.dma("pool", lw[0:32, 3, :], sel(io["g2s"])[128:160, :], w=lw.all, waw=False)

    KT = [K.sbuf(f"KT{i}", [128, S], BF16, nparts=16) for i in range(2)]
    Vtm = K.sbuf("Vtm", [128, 64, 256], BF16, nparts=16)
    hT = K.sbuf("hT", [128, 16, 512], BF16, nparts=16)
    raw = [K.sbuf(f"raw{i}", [128, 513], F32) for i in range(2)]
    lastcol = K.sbuf("lastcol", [128, 10], F32)
    wbuf = [K.sbuf(f"w1b{i}", [128, 16, 128], BF16) for i in range(2)]
    xs = [K.sbuf(f"p1xs{i}", [128, 512], F32) for i in range(2)]
    sqb = [K.sbuf(f"p1sq{i}", [128, 512], BF16) for i in range(2)]
    sqrt_t = K.sbuf("p1sqrt", [128, 512], F32) if not do_rwkv else None
    rstd = K.sbuf("p1rstd", [128, 512], F32) if not do_rwkv else None
    PSall = K.psum("p1psall", [128, 4096], F32)
    PS = [PSall.view(PSall.t[:, i * 512:(i + 1) * 512], f"p1ps{i}") for i in range(8)]
    PW = []
    for i in range(4):
        wb_ = Buf(PSall.t[:, i * 1024:(i + 1) * 1024], f"p1pw{i}", 0)
        wb_.d = [PS[2 * i].d[0], PS[2 * i + 1].d[0]]
        PW.append(wb_)
    pr1 = K.sbuf("p_r0", [128, 512], F32)
    pk1 = K.sbuf("p_k0", [128, 512], F32)
    pv1 = K.sbuf("p_v0", [128, 512], F32)
    pr_, pk_, pv_ = [pr1, pr1], [pk1, pk1], [pv1, pv1]
    dwt = K.sbuf("dwt", [64, 512], BF16)
    dat = K.sbuf("dat", [64, 512], BF16)
    dgs = K.sbuf("dgs", [128, 2, 512], BF16)
    q8 = [K.sbuf(f"q8_{i}", [128, 512], BF16) for i in range(2)]
    vtmp = K.sbuf("vtmp", [128, 512], BF16)
    tmpf = [K.sbuf(f"tmpf{i}", [128, 512], F32) for i in range(2)] if not do_rwkv else [None, None]
    cnt = {"xs": 0, "sq": 0, "w": 0, "ps": 0, "tf": 0}

    def nxt(key, lst):
        b = lst[cnt[key] % len(lst)]
        cnt[key] += 1
        return b

    xT = io["xT"].rearrange("(c p) t -> p c t", p=128)

    hcache = None
    if len(groups) > 1:
        hsc = nc.dram_tensor("hsc", [n_tiles, 128, 16, 512], BF16, kind="Internal").ap()
        hcache = {"ap": [hsc[i] for i in range(n_tiles)], "dep": [Dep(f"hsc{i}") for i in range(n_tiles)],
                  "have": [False] * n_tiles}
    rw = RwkvState(K, cst, prm, omk, lw, PS, PW) if do_rwkv else None
    if rw is not None:
        tmpf[:] = [rw.t1, rw.t2]
        sqrt_t, rstd = rw.t3, rw.W
    for g in groups:
        load_group(g)
        K.op("dve", lambda e: e.memset(lastcol[:], 0.0), w=lastcol.all)
        if rw is not None:
            rw.reset()
        w1c = io["w1c"] if g is None else io["w1c"][g]
        if ysc is None:
            ya_dst = lambda pair, qt: io["yaT"][pair * 128:(pair + 1) * 128, qt * 512:(qt + 1) * 512]
            yb_dst = lambda hd, qt: io["ybT"][hd * 64:(hd + 1) * 64, qt * 512:(qt + 1) * 512]
            ydep = None
        else:
            gg = g
            ya_dst = lambda pair, qt: ysc[gg * 256 + pair * 128:gg * 256 + (pair + 1) * 128,
                                          (qt - own_from) * 512:(qt - own_from + 1) * 512]
            yb_dst = lambda hd, qt: ysc[1024 + gg * 256 + hd * 64:1024 + gg * 256 + (hd + 1) * 64,
                                        (qt - own_from) * 512:(qt - own_from + 1) * 512]
            ydep = ysc.d[0]
        _phase1_tiles(K, io, cm, n_tiles, do_rwkv, do_sb, own_from, w1c, ya_dst, yb_dst, ydep, rw, xT, tokm,
                      cst, ones, prm, PS, KT, Vtm, hT, raw, lastcol, wbuf, xs, sqb, sqrt_t, rstd, pr_, pk_, pv_,
                      dwt, dat, dgs, q8, vtmp, tmpf, cnt, nxt, hcache=hcache)


def _phase1_tiles(K, io, cm, n_tiles, do_rwkv, do_sb, own_from, w1c, ya_dst, yb_dst, ydep, rw, xT, tokm,
                  cst, ones, prm, PS, KT, Vtm, hT, raw, lastcol, wbuf, xs, sqb, sqrt_t, rstd, pr_, pk_, pv_,
                  dwt, dat, dgs, q8, vtmp, tmpf, cnt, nxt, hcache=None):
    for qt in range(n_tiles):
        t0 = qt * 512
        hc = hcache
        if hc is not None and hc["have"][qt]:
            K.dma("sp", hT[:], hc["ap"][qt], r=[hc["dep"][qt]], w=hT.all, sem_dep=hT.d[0])
        else:
            ss = PS[7]
            for c in range(16):
                xb = nxt("xs", xs)
                K.dma("sp", xb[:], xT[:, c, t0:t0 + 512], w=xb.all)
                sq = nxt("sq", sqb)
                K.op("act", lambda e: e.activation(out=sq[:], in_=xb[:], func=AF.Square), r=xb.all, w=sq.all)
                mm(K, ss, ss[:], ones[:], sq[:], start=(c == 0), stop=(c == 15), r=sq.all + ones.all)
            rstd_from_ss(K, ss, sqrt_t, rstd, float(D))
            for c in range(16):
                xb = nxt("xs", xs)
                K.dma("sp", xb[:], xT[:, c, t0:t0 + 512], w=xb.all)
                K.op("dve", lambda e: e.tensor_tensor(out=xb[:], in0=xb[:], in1=rstd[:], op=ALU.mult),
                     r=xb.all + rstd.all, w=xb.all)
                K.op("act", lambda e: e.activation(out=hT[:, c, :], in_=xb[:], func=AF.Identity,
                                                    scale=cm.A1[:, c:c + 1], bias=cm.B1[:, c:c + 1]),
                     r=xb.all + cm.A1.all + cm.B1.all, w=[hT.d[c]])
                if tokm is not None:
                    if c == 0:
                        K.dma("sp", tokm[:], io["tokmask"][:, t0:t0 + 512], w=tokm.all)
                    K.op("dve", lambda e: e.tensor_tensor(out=hT[:, c, :], in0=hT[:, c, :], in1=tokm[:], op=ALU.mult),
                         r=[hT.d[c]] + tokm.all, w=[hT.d[c]])
            if hc is not None:
                ev_ = K.dma("sp", hc["ap"][qt], hT[:], r=hT.all, sem_dep=hT.d[0])
                hc["dep"][qt].lw = [ev_]
                hc["have"][qt] = True
        own = qt >= own_from
        def do_chunk(ch):
            if (ch < 10 and not do_rwkv) or (ch >= 10 and not do_sb):
                return
            if ch in (10, 11) and not own:
                return
            if ch in (0, 1, 8, 9) and qt < own_from - 1:
                return
            wb = nxt("w", wbuf)
            load_w_cast(K, wb, wb[:], w1c[ch])
            pp = PS[cnt["ps"] % 4]
            cnt["ps"] += 1
            M = {6: 64, 7: 64, 9: 32}.get(ch, 128)
            for kc in range(16):
                mm(K, pp, pp[0:M, :], wb[:, kc, 0:M], hT[:, kc, :], start=(kc == 0), stop=(kc == 15),
                   r=wb.all + [hT.d[kc]])
            if ch < 10:
                rb = raw[ch % 2]
                K.op("dve", lambda e: e.tensor_copy(out=rb[0:M, 0:1], in_=lastcol[0:M, ch:ch + 1]),
                     r=lastcol.all, w=rb.all)
                K.op("act", lambda e: e.activation(out=rb[0:M, 1:513], in_=pp[0:M, :], func=AF.Identity),
                     r=pp.all, w=rb.all)
                df = nxt("tf", tmpf)
                K.op("dve", lambda e: e.tensor_tensor(out=df[0:M, :], in0=rb[0:M, 0:512], in1=rb[0:M, 1:513],
                                                      op=ALU.subtract), r=rb.all, w=df.all)
                if ch < 6:
                    dst = [pr_, pk_, pv_][ch // 2][ch % 2]
                    K.op("dve", lambda e: e.scalar_tensor_tensor(out=dst[:], in0=df[:], scalar=prm[:, ch:ch + 1],
                                                                  in1=rb[:, 1:513], op0=ALU.mult, op1=ALU.add),
                         r=df.all + rb.all + prm.all, w=dst.all)
                else:
                    K.op("dve", lambda e: e.scalar_tensor_tensor(out=df[0:M, :], in0=df[0:M, :],
                                                                  scalar=prm[0:M, ch:ch + 1], in1=rb[0:M, 1:513],
                                                                  op0=ALU.mult, op1=ALU.add),
                         r=df.all + rb.all + prm.all, w=df.all)
                    if ch == 6:
                        K.op("act", lambda e: e.activation(out=dwt[:], in_=df[0:64, :], func=AF.Tanh),
                             r=df.all, w=dwt.all)
                    elif ch == 7:
                        K.op("act", lambda e: e.activation(out=dat[:], in_=df[0:64, :], func=AF.Identity),
                             r=df.all, w=dat.all)
                    elif ch == 8:
                        K.op("act", lambda e: e.activation(out=dgs[:, 0, :], in_=df[:], func=AF.Sigmoid),
                             r=df.all, w=dgs.all)
                    else:
                        K.op("act", lambda e: e.activation(out=dgs[0:32, 1, :], in_=df[0:32, :], func=AF.Sigmoid),
                             r=df.all, w=dgs.all)
                K.op("dve", lambda e: e.tensor_copy(out=lastcol[0:M, ch:ch + 1], in_=rb[0:M, 512:513]),
                     r=rb.all, w=lastcol.all)
            elif ch < 12:
                K.op("act", lambda e: e.activation(out=q8[ch - 10][:], in_=pp[:], func=AF.Identity, scale=0.125),
                     r=pp.all, w=q8[ch - 10].all)
            elif ch < 14:
                K.op("act", lambda e: e.activation(out=KT[ch - 12][:, t0:t0 + 512], in_=pp[:], func=AF.Identity),
                     r=pp.all, w=[KT[ch - 12].d[qt]])
            else:
                pair = ch - 14
                K.op("act", lambda e: e.activation(out=vtmp[:], in_=pp[:], func=AF.Identity), r=pp.all, w=vtmp.all)
                pT = PS[6]
                pTb = pT.t[:].bitcast(BF16)
                for bk in range(4):
                    K.op("pe", lambda e: e.transpose(pTb[:, bk * 128:(bk + 1) * 128], vtmp[:, bk * 128:(bk + 1) * 128],
                                                     cst["ident"][:]),
                         r=vtmp.all + cst["ident"].all, w=pT.all)
                K.op("dve", lambda e: e.tensor_copy(
                    out=Vtm[:, 4 * qt:4 * qt + 4, pair * 128:(pair + 1) * 128],
                    in_=pTb[:, 0:512].rearrange("p (b c) -> p b c", c=128)), r=pT.all, w=[Vtm.d[qt]])
        for ch in (6, 7, 8, 9):
            do_chunk(ch)
        for pair in range(2):
            for ch in (pair, 2 + pair, 4 + pair):
                do_chunk(ch)
            if do_rwkv:
                rw.tile(qt, pair, pr_[pair], pk_[pair], pv_[pair], dwt, dat, dgs, ya_dst(pair, qt) if own else None, own, ydep)
        for ch in range(10, 16):
            do_chunk(ch)
        if do_sb and own:
            K.barrier()
            sb_tile(K, cst, ones, PS, qt, q8, KT, Vtm, yb_dst, ydep, tmpf, rw=rw)
            K.barrier()


_sbst = {}


def sb_tile(K, cst, ones, PS, qt, q8, KT, Vtm, yb_dst, ydep, tmpf, rw=None):
    st = _sbst.get(id(K))
    if st is None:
        st = {}
        if rw is None:
            st["e"] = [K.sbuf(f"sb_e{i}", [128, 512], F32) for i in range(5)]
            st["w"] = [K.sbuf(f"sb_w{i}", [128, 512], F32) for i in range(2)]
            st["sp"] = [K.sbuf(f"sb_sp{i}", [128, 512], BF16) for i in range(3)]
            st["ar"] = [K.sbuf(f"sb_ar{i}", [128, 512], F32) for i in range(2)]
            st["att"] = [K.sbuf(f"sb_att{i}", [128, 512], BF16) for i in range(3)]
            st["Cc"] = K.sbuf("sb_Cc", [128, 512], F32)
        else:
            st["e"] = [rw.kk.view(rw.kk[:], "sbv_e0"), rw.cl.view(rw.cl[:], "sbv_e1"), rw.kf.view(rw.kf[:], "sbv_e2"),
                       rw.g.view(rw.g[:], "sbv_e3"), rw.W.view(rw.W[:], "sbv_e4")]
            st["w"] = [rw.ld.view(rw.ld[:], "sbv_w0"), rw.a.view(rw.a[:], "sbv_w1")]
            st["ar"] = [rw.t1.view(rw.t1[:], "sbv_ar0"), rw.t2.view(rw.t2[:], "sbv_ar1")]
            st["Cc"] = rw.t3.view(rw.t3[:], "sbv_Cc")
            st["sp"] = [rw.AR.view(rw.AR[:, 0, :], "sbv_sp0"), rw.AR.view(rw.AR[:, 1, :], "sbv_sp1"),
                        rw.BK.view(rw.BK[:, 0, :], "sbv_sp2")]
            st["att"] = [rw.BK.view(rw.BK[:, 1, :], "sbv_at0"), rw.pre.view(rw.pre[:, 0, :], "sbv_at1"),
                         rw.pre.view(rw.pre[:, 1, :], "sbv_at2")]
        st["yo"] = [K.sbuf(f"sb_yo{i}", [64, 512], BF16) for i in range(2)]
        st["n"] = 0
        _sbst[id(K)] = st
    t0 = qt * 512
    negtri = cst["negtri"]
    maskd = cst["maskd"]
    Cc = st["Cc"]
    for hd in range(4):
        pair, bp = hd // 2, 64 * (hd % 2)
        qv = q8[pair][bp:bp + 64, :]
        qd = q8[pair].all
        jmax = 4 * qt + 3
        js = list(range(jmax, -1, -1))
        ypsum = PS[6]
        info = {}

        NE = len(st["e"])

        def s1(idx):
            j = js[idx]
            d = j - 4 * qt
            kT = KT[pair][bp:bp + 64, j * 128:(j + 1) * 128]
            kd = [KT[pair].d[j // 4]]
            zp = PS[idx % 2]
            mm(K, zp, zp[:], kT, qv, True, True, r=kd + qd)
            e = st["e"][idx % NE]
            K.op("act", lambda en: en.activation(out=e[:], in_=zp[:], func=AF.Exp), r=zp.all, w=e.all)
            info[idx] = dict(j=j, d=d, e=e, sp=st["sp"][idx % 3], att=st["att"][idx % 3], wv=st["w"][idx % 2],
                             ar=st["ar"][idx % 2], ap=PS[2 + idx % 2], cp=PS[4 + idx % 2], last=(idx == len(js) - 1))

        def s2(idx):
            I = info[idx]
            sp, e, d = I["sp"], I["e"], I["d"]
            K.op("act", lambda en: en.activation(out=sp[:], in_=e[:], func=AF.Ln, bias=1.0), r=e.all, w=sp.all)
            if d >= 0:
                K.op("dve", lambda en: en.tensor_tensor(out=sp[:], in0=sp[:], in1=maskd[:, d * 512:(d + 1) * 512],
                                                        op=ALU.mult), r=sp.all + maskd.all, w=sp.all)

        def s3(idx):
            I = info[idx]
            sp, ap_, cp, ar, last = I["sp"], I["ap"], I["cp"], I["ar"], I["last"]
            mm(K, ap_, ap_[:], negtri[:], sp[:], True, True, r=sp.all + negtri.all)
            if not last:
                mm(K, cp, cp[:], ones[:], sp[:], True, True, r=sp.all + ones.all)
            if idx == 0:
                K.op("dve", lambda en: en.tensor_copy(out=ar[:], in_=ap_[:]), r=ap_.all, w=ar.all)
                if not last:
                    K.op("dve", lambda en: en.tensor_copy(out=Cc[:], in_=cp[:]), r=cp.all, w=Cc.all)
            else:
                K.op("dve", lambda en: en.tensor_tensor(out=ar[:], in0=ap_[:], in1=Cc[:], op=ALU.subtract),
                     r=ap_.all + Cc.all, w=ar.all)
                if not last:
                    K.op("dve", lambda en: en.tensor_tensor(out=Cc[:], in0=Cc[:], in1=cp[:], op=ALU.add),
                         r=cp.all + Cc.all, w=Cc.all)

        def s4(idx):
            I = info[idx]
            wv, ar = I["wv"], I["ar"]
            K.op("act", lambda en: en.activation(out=wv[:], in_=ar[:], func=AF.Exp), r=ar.all, w=wv.all)

        def s5(idx):
            I = info[idx]
            att, e, wv, d = I["att"], I["e"], I["wv"], I["d"]
            meng = "pool" if idx % 2 == 0 else "dve"
            K.op(meng, lambda en: en.tensor_tensor(out=att[:], in0=e[:], in1=wv[:], op=ALU.mult),
                 r=e.all + wv.all, w=att.all)
            if d >= 0:
                K.op(meng, lambda en: en.tensor_tensor(out=att[:], in0=att[:], in1=maskd[:, d * 512:(d + 1) * 512],
                                                       op=ALU.mult), r=att.all + maskd.all, w=att.all)

        def s6(idx):
            I = info[idx]
            mm(K, ypsum, ypsum[0:64, :], Vtm[:, I["j"], hd * 64:(hd + 1) * 64], I["att"][:], idx == 0,
               idx == len(js) - 1, r=I["att"].all + [Vtm.d[I["j"] // 4]])

        n = len(js)
        stages = [s1, s2, s3, s4, s5, s6]
        for step in range(n + len(stages) - 1):
            for k_, fn_ in enumerate(stages):
                if 0 <= step - k_ < n:
                    fn_(step - k_)
        yo = st["yo"][st["n"] % 2]
        st["n"] += 1
        K.op("act", lambda en: en.activation(out=yo[:], in_=ypsum[0:64, :], func=AF.Identity), r=ypsum.all, w=yo.all)
        if ydep is None:
            K.dma("sp", yb_dst(hd, qt), yo[:], r=yo.all, sem_dep=yo.d[0], is_output=True)
        else:
            K.scratch_events.append(K.dma("sp", yb_dst(hd, qt), yo[:], r=yo.all, sem_dep=yo.d[0]))


class RwkvState:
    def __init__(self, K, cst, prm, omk, lw, PS, PW):
        self.K, self.cst, self.prm, self.omk, self.lw, self.PS, self.PW = K, cst, prm, omk, lw, PS, PW
        sb = K.sbuf
        self.St = [[sb(f"St{p}_{i}", [128, 64], BF16) for i in range(3)] for p in range(2)]
        for p in range(2):
            K.op("dve", lambda e: e.memset(self.St[p][0][:], 0.0), w=self.St[p][0].all)
        self.stn = [0, 0]
        f = lambda n: sb("rw_" + n, [128, 512], F32)
        self.ld, self.a, self.g, self.kk, self.kf, self.cl, self.W = (f(n) for n in ("ld", "a", "g", "kk", "kf", "cl", "W"))
        self.t1, self.t2, self.t3 = f("t1"), f("t2"), f("t3")
        self.AR = sb("rw_AR", [128, 2, 512], BF16)
        self.BK = sb("rw_BK", [128, 2, 512], BF16)
        self.pre = sb("rw_pre", [128, 3, 512], BF16)
        self.TT = sb("rw_TT", [128, 4, 512], BF16)
        self.X1 = sb("rw_X1", [128, 2048], BF16)
        self.X2 = sb("rw_X2", [128, 2048], BF16)
        self.X3 = sb("rw_X3", [128, 1024], BF16)
        self.A = [sb(f"rw_A{i}", [128, 1024], BF16) for i in range(2)]
        self.AT = [sb(f"rw_AT{i}", [128, 1024], BF16) for i in range(2)]
        self.Tt = [sb(f"rw_Tt{i}", [128, 1024], BF16) for i in range(2)]
        self.X1b = sb("rw_X1b", [128, 1024], BF16)
        self.X2b = sb("rw_X2b", [128, 1024], BF16)
        self.Rbd = sb("rw_Rbd", [128, 8, 2, 64], BF16)
        self.Pbd = sb("rw_Pbd", [128, 8, 128], BF16)
        self.Ghb = sb("rw_Ghb", [128, 8, 128], BF16)
        self.Gbd = sb("rw_Gbd", [128, 8, 128], BF16)
        self.BhB = sb("rw_BhB", [128, 8, 128], BF16)
        self.KhB = sb("rw_KhB", [128, 8, 128], BF16)
        for b_ in (self.Rbd, self.Pbd, self.BhB, self.KhB):
            K.op("dve", lambda e: e.memset(b_[:], 0.0), w=b_.all)
        self.yo = [sb(f"rw_yo{i}", [128, 512], BF16) for i in range(2)]
        self.nyo = 0

    def reset(self):
        for p in range(2):
            cur = self.St[p][self.stn[p] % 3]
            self.K.op("dve", lambda e: e.memset(cur[:], 0.0), w=cur.all)

    def tile(self, qt, pair, p_r, p_k, p_v, dwt, dat, dgs, ya_ap, emit_y=True, ydep=None):
        K, cst, prm, omk, lw, PS = self.K, self.cst, self.prm, self.omk, self.lw, self.PS
        t0 = qt * 512
        ps_ = slice(pair * 128, (pair + 1) * 128)
        col = lambda i: prm[:, i + pair:i + pair + 1]
        ld, a, g, kk, kf, cl, W = self.ld, self.a, self.g, self.kk, self.kf, self.cl, self.W
        t1, t2, t3 = self.t1, self.t2, self.t3
        AR, BK, pre, TT = self.AR, self.BK, self.pre, self.TT
        DV = lambda fn, r, w: K.op("dve", fn, r=r, w=w)
        AC = lambda fn, r, w: K.op("act", fn, r=r, w=w)
        mm(K, PS[0], PS[0][:], lw[0:64, 0, ps_], dwt[0:64, :], True, True, r=lw.all + dwt.all)
        mm(K, PS[1], PS[1][:], lw[0:64, 1, ps_], dat[0:64, :], True, True, r=lw.all + dat.all)
        if emit_y:
            mm(K, PS[2], PS[2][:], lw[:, 2, ps_], dgs[:, 0, :], True, False, r=lw.all + dgs.all)
            mm(K, PS[2], PS[2][:], lw[0:32, 3, ps_], dgs[0:32, 1, :], False, True, r=lw.all + dgs.all)
        AC(lambda e: e.activation(out=t1[:], in_=PS[0][:], func=AF.Sigmoid, bias=col(10)), PS[0].all + prm.all, t1.all)
        DV(lambda e: e.tensor_scalar(out=ld[:], in0=t1[:], scalar1=-0.6065306597126334, scalar2=None, op0=ALU.mult),
           t1.all, ld.all)
        AC(lambda e: e.activation(out=a[:], in_=PS[1][:], func=AF.Sigmoid, bias=col(12)), PS[1].all + prm.all, a.all)
        if emit_y:
            AC(lambda e: e.activation(out=g[:], in_=PS[2][:], func=AF.Identity), PS[2].all, g.all)
        DV(lambda e: e.tensor_scalar(out=t1[:], in0=p_k[:], scalar1=col(14), scalar2=None, op0=ALU.mult),
           p_k.all + prm.all, t1.all)
        AC(lambda e: e.activation(out=t2[:], in_=t1[:], func=AF.Square), t1.all, t2.all)
        mm(K, PS[3], PS[3][:], cst["bd1"][:], t2[:], True, True, r=cst["bd1"].all + t2.all)
        DV(lambda e: e.tensor_scalar(out=t2[:], in0=PS[3][:], scalar1=1e-24, scalar2=None, op0=ALU.max),
           PS[3].all, t2.all)
        AC(lambda e: e.activation(out=t3[:], in_=t2[:], func=AF.Ln, scale=float(2.0 ** 40)), t2.all, t3.all)
        AC(lambda e: e.activation(out=t2[:], in_=t3[:], func=AF.Exp, scale=-0.5, bias=20.0 * 0.6931471805599453),
           t3.all, t2.all)
        DV(lambda e: e.tensor_tensor(out=kk[:], in0=t1[:], in1=t2[:], op=ALU.mult), t1.all + t2.all, kk.all)
        DV(lambda e: e.tensor_scalar(out=t1[:], in0=a[:], scalar1=col(16), scalar2=omk[:, pair:pair + 1],
                                     op0=ALU.mult, op1=ALU.add), a.all + prm.all + omk.all, t1.all)
        DV(lambda e: e.tensor_tensor(out=kf[:], in0=p_k[:], in1=t1[:], op=ALU.mult), p_k.all + t1.all, kf.all)
        DV(lambda e: e.tensor_tensor_scan(out=cl[:], data0=cst["scanmask"][:], data1=ld[:], initial=0.0,
                                          op0=ALU.mult, op1=ALU.add), cst["scanmask"].all + ld.all, cl.all)
        AC(lambda e: e.activation(out=W[:], in_=cl[:], func=AF.Exp), cl.all, W.all)
        DV(lambda e: e.tensor_tensor(out=t3[:], in0=kk[:], in1=a[:], op=ALU.mult), kk.all + a.all, t3.all)
        AC(lambda e: e.activation(out=t1[:], in_=cl[:], func=AF.Exp, scale=-1.0), cl.all, t1.all)
        DV(lambda e: e.tensor_tensor(out=BK[:, 0, :], in0=t3[:], in1=t1[:], op=ALU.mult), t3.all + t1.all, BK.all)
        DV(lambda e: e.tensor_tensor(out=BK[:, 1, :], in0=kf[:], in1=t1[:], op=ALU.mult), kf.all + t1.all, BK.all)
        DV(lambda e: e.tensor_tensor(out=t2[:], in0=cl[:], in1=ld[:], op=ALU.subtract), cl.all + ld.all, t2.all)
        AC(lambda e: e.activation(out=t2[:], in_=t2[:], func=AF.Exp), t2.all, t2.all)
        DV(lambda e: e.scalar_tensor_tensor(out=AR[:, 0, :], in0=kk[:], scalar=-1.0, in1=t2[:], op0=ALU.mult,
                                            op1=ALU.mult), kk.all + t2.all, AR.all)
        if emit_y:
            DV(lambda e: e.tensor_tensor(out=AR[:, 1, :], in0=p_r[:], in1=W[:], op=ALU.mult), p_r.all + W.all, AR.all)
        cl3 = cl[:].rearrange("p (c s) -> p c s", s=64)
        t13 = t1[:].rearrange("p (c s) -> p c s", s=64)
        DV(lambda e: e.tensor_tensor(out=t13, in0=cl3[:, :, 63:64].to_broadcast([128, 8, 64]), in1=cl3,
                                     op=ALU.subtract), cl.all, t1.all)
        AC(lambda e: e.activation(out=t1[:], in_=t1[:], func=AF.Exp), t1.all, t1.all)
        DV(lambda e: e.tensor_tensor(out=pre[:, 0, :], in0=t3[:], in1=t1[:], op=ALU.mult), t3.all + t1.all, pre.all)
        DV(lambda e: e.tensor_tensor(out=pre[:, 1, :], in0=kf[:], in1=t1[:], op=ALU.mult), kf.all + t1.all, pre.all)
        AC(lambda e: e.activation(out=pre[:, 2, :], in_=p_v[:], func=AF.Identity), p_v.all, pre.all)
        if STOP_AT == 1:
            return
        ident = cst["ident"]
        srcs = [AR[:, 0, :], pre[:, 0, :], pre[:, 1, :], pre[:, 2, :]]
        BhB, KhB = self.BhB, self.KhB
        for qi in range(4):
            pT = PS[6 + qi % 2]
            pTb = pT.t[:].bitcast(BF16)
            for bk in range(4):
                K.op("pe", lambda e: e.transpose(pTb[:, bk * 128:(bk + 1) * 128], srcs[qi][:, bk * 128:(bk + 1) * 128],
                                                 ident[:]), r=AR.all + pre.all + ident.all, w=pT.all)
            AC(lambda e: e.activation(out=TT[:, qi, :], in_=pTb[:, 0:512], func=AF.Identity), pT.all, TT.all)
            if qi in (1, 2):
                BD = BhB if qi == 1 else KhB
                for h2 in range(2):
                    hs_ = slice(h2 * 64, h2 * 64 + 64)
                    AC(lambda e: e.activation(out=BD[hs_, :, h2 * 64:h2 * 64 + 64],
                                              in_=pTb[hs_, 0:512].rearrange("p (b j) -> p b j", j=64),
                                              func=AF.Identity), pT.all, BD.all)
        if STOP_AT == 2:
            return
        X1, X2, X3, X1b, X2b = self.X1, self.X2, self.X3, self.X1b, self.X2b
        Rbd, Pbd, Ghb, Gbd = self.Rbd, self.Pbd, self.Ghb, self.Gbd
        X1s = X1[:].rearrange("p (s w x) -> p s w x", w=2, x=128)
        X2s = X2[:].rearrange("p (s w x) -> p s w x", w=2, x=128)
        X3s = X3[:].rearrange("p (s x) -> p s x", x=128)
        PW = self.PW

        def sl_iter():
            for blk in range(4):
                for hd in range(2):
                    yield blk, hd, blk * 2 + hd, slice(hd * 64, hd * 64 + 64), slice(blk * 128, blk * 128 + 128)
        for blk, hd, sl, hp, pc in sl_iter():
            b1 = PS[hd * 2 + blk // 2]
            b2 = PS[4 + hd * 2 + blk // 2]
            co = (blk % 2) * 256
            mm(K, b1, b1[:, co:co + 256], AR[hp, 0, pc], BK[hp, :, pc], True, True, r=AR.all + BK.all)
            if emit_y:
                mm(K, b2, b2[:, co:co + 256], BK[hp, 0, pc], AR[hp, :, pc], True, True, r=AR.all + BK.all)
            else:
                mm(K, b2, b2[:, co:co + 128], BK[hp, 0, pc], AR[hp, 0, pc], True, True, r=AR.all + BK.all)
        m1v = cst["mask1"][:].rearrange("p (c w x) -> p c w x", w=2, x=128)
        m2v = cst["mask2"][:].rearrange("p (c w x) -> p c w x", w=2, x=128)
        X1v = X1[:].rearrange("p (b h w x) -> p b h w x", h=2, w=2, x=128)
        X2v = X2[:].rearrange("p (b h w x) -> p b h w x", h=2, w=2, x=128)
        X3v = X3[:].rearrange("p (b h x) -> p b h x", h=2, x=128)
        for hd in range(2):
            for bb in range(2):
                b1 = PS[hd * 2 + bb]
                b2 = PS[4 + hd * 2 + bb]
                DV(lambda e: e.tensor_tensor(out=X1v[:, bb * 2:bb * 2 + 2, hd],
                                             in0=b1[:].rearrange("p (c w x) -> p c w x", w=2, x=128), in1=m1v,
                                             op=ALU.mult), b1.all + cst["mask1"].all, X1.all)
                if emit_y:
                    DV(lambda e: e.tensor_tensor(out=X2v[:, bb * 2:bb * 2 + 2, hd],
                                                 in0=b2[:].rearrange("p (c w x) -> p c w x", w=2, x=128), in1=m2v,
                                                 op=ALU.mult), b2.all + cst["mask2"].all, X2.all)
                else:
                    DV(lambda e: e.tensor_tensor(out=X2v[:, bb * 2:bb * 2 + 2, hd, 0, :],
                                                 in0=b2[:].rearrange("p (c w x) -> p c w x", w=2, x=128)[:, :, 0, :],
                                                 in1=m2v[:, :, 0, :], op=ALU.mult), b2.all + cst["mask2"].all, X2.all)
        if emit_y:
            for blk, hd, sl, hp, pc in sl_iter():
                b3 = PS[hd * 2]
                mm(K, b3, b3[:, blk * 128:(blk + 1) * 128], BK[hp, 1, pc], AR[hp, 1, pc], True, True, r=AR.all + BK.all)
            for hd in range(2):
                b3 = PS[hd * 2]
                DV(lambda e: e.tensor_tensor(out=X3v[:, :, hd], in0=b3[:].rearrange("p (c x) -> p c x", x=128),
                                             in1=cst["mask3x2"][:].rearrange("p (c x) -> p c x", x=128),
                                             op=ALU.mult), b3.all + cst["mask3x2"].all, X3.all)
        Tt = self.Tt
        Tcur = Tt[0]
        DV(lambda e: e.tensor_tensor(out=Tcur[:].rearrange("p (s x) -> p s x", x=128), in0=X1s[:, :, 0, :],
                                     in1=cst["id128x8"][:].rearrange("p (s x) -> p s x", x=128), op=ALU.add),
           X1.all + cst["id128x8"].all, Tcur.all)
        Acur = (lambda sl: X2s[:, sl, 0, :], X2.all)
        ATcur = (lambda sl: X1s[:, sl, 0, :], X1.all)
        ti = 0
        WA_, WAT_, WT_ = PW[0], PW[1], PW[2]
        for k in range(6):
            do_sq = k <= 4
            do_sqT = k <= 3
            do_T = k >= 1
            for sl in range(8):
                so = slice(sl * 128, sl * 128 + 128)
                if do_sq:
                    mm(K, WA_, WA_[:, so], ATcur[0](sl), Acur[0](sl), True, True, r=Acur[1] + ATcur[1])
                if do_sqT:
                    mm(K, WAT_, WAT_[:, so], Acur[0](sl), ATcur[0](sl), True, True, r=Acur[1] + ATcur[1])
                if do_T:
                    mm(K, WT_, WT_[:, so], Acur[0](sl), Tcur[:, so], True, True, r=Acur[1] + Tcur.all)
            if do_T:
                Tn = Tt[(ti + 1) % 2]
                DV(lambda e: e.tensor_tensor(out=Tn[:], in0=WT_[:], in1=Tcur[:], op=ALU.add),
                   WT_.all + Tcur.all, Tn.all)
                Tcur = Tn
                ti += 1
            if do_sq:
                An = self.A[k % 2]
                AC(lambda e: e.activation(out=An[:], in_=WA_[:], func=AF.Identity), WA_.all, An.all)
                if do_sqT:
                    ATn = self.AT[k % 2]
                    AC(lambda e: e.activation(out=ATn[:], in_=WAT_[:], func=AF.Identity), WAT_.all, ATn.all)
                    ATcur = ((lambda b: (lambda sl: b[:, sl * 128:sl * 128 + 128]))(ATn), ATn.all)
                Acur = ((lambda b: (lambda sl: b[:, sl * 128:sl * 128 + 128]))(An), An.all)
        WX1, WX2 = PW[3], PW[0]
        for blk, hd, sl, hp, pc in sl_iter():
            so = slice(sl * 128, sl * 128 + 128)
            if emit_y:
                mm(K, WX1, WX1[:, so], Tcur[:, so], X2s[:, sl, 1, :], True, True, r=Tcur.all + X2.all)
            mm(K, WX2, WX2[:, so], Tcur[:, so], BhB[:, sl, :], True, True, r=Tcur.all + BhB.all)
        if emit_y:
            AC(lambda e: e.activation(out=X1b[:], in_=WX1[:], func=AF.Identity), WX1.all, X1b.all)
        AC(lambda e: e.activation(out=X2b[:], in_=WX2[:], func=AF.Identity), WX2.all, X2b.all)
        WP, WGh, WG = PW[2], PW[3], PW[0]
        p3v = WP[:].rearrange("p (c x) -> p c x", x=128)
        for blk, hd, sl, hp, pc in sl_iter():
            so = slice(sl * 128, sl * 128 + 128)
            atT = TT[:, 0, blk * 128 + hd * 64:blk * 128 + hd * 64 + 64]
            if emit_y:
                mm(K, PS[2], PS[2][hp, blk * 128:(blk + 1) * 128], atT, X1b[:, so], True, True, r=TT.all + X1b.all)
            for e2 in range(2):
                mm(K, WP, p3v[hp, blk * 2 + e2, hd * 64:hd * 64 + 64], atT,
                   X2b[:, sl * 128 + e2 * 64:sl * 128 + e2 * 64 + 64], True, True, r=TT.all + X2b.all)
            if emit_y:
                mm(K, WGh, WGh[:, so], X1s[:, sl, 1, :], X1b[:, so], True, True, r=X1.all + X1b.all)
            mm(K, WG, WG[:, so], X1s[:, sl, 1, :], X2b[:, so], True, True, r=X1.all + X2b.all)
        for hd in range(2):
            hp = slice(hd * 64, hd * 64 + 64)
            if emit_y:
                DV(lambda e: e.tensor_tensor(out=Rbd[hp, :, hd, :],
                                             in0=PS[2][hp, :].rearrange("p (c t) -> p c t", t=64),
                                             in1=AR[hp, 1, :].rearrange("p (c t) -> p c t", t=64),
                                             op=ALU.add), PS[2].all + AR.all, Rbd.all)
            AC(lambda e: e.activation(out=Pbd[hp, :, hd * 64:hd * 64 + 64],
                                      in_=p3v[hp, :, hd * 64:hd * 64 + 64], func=AF.Identity), WP.all, Pbd.all)
        if emit_y:
            DV(lambda e: e.tensor_tensor(out=Ghb[:], in0=WGh[:].rearrange("p (s x) -> p s x", x=128), in1=X3s,
                                         op=ALU.add), WGh.all + X3.all, Ghb.all)
        DV(lambda e: e.tensor_tensor(out=Gbd[:], in0=WG[:].rearrange("p (s x) -> p s x", x=128), in1=KhB[:], op=ALU.add),
           WG.all + KhB.all, Gbd.all)
        PY, PSt = PS[6], PS[7]
        for c in range(8):
            blk, e2 = c // 2, c % 2
            Sc = self.St[pair][self.stn[pair] % 3]
            Sn = self.St[pair][(self.stn[pair] + 1) % 3]
            self.stn[pair] += 1
            cc = slice(c * 64, c * 64 + 64)
            mm(K, PSt, PSt[:, cc], Pbd[:, c, :], Sc[:], True, False, r=Pbd.all + Sc.all)
            for hd in range(2):
                hp = slice(hd * 64, hd * 64 + 64)
                vT = TT[:, 3, blk * 128 + hd * 64:blk * 128 + hd * 64 + 64]
                es_ = slice(e2 * 64, e2 * 64 + 64)
                if emit_y:
                    mm(K, PY, PY[hp, cc], Sc[:], Rbd[:, c, hd, :], True, False, r=Sc.all + Rbd.all)
                    mm(K, PY, PY[hp, cc], vT, Ghb[:, blk * 2 + hd, es_], False, True, r=TT.all + Ghb.all)
                mm(K, PSt, PSt[hp, cc], Gbd[:, blk * 2 + hd, es_], vT, False, True, r=TT.all + Gbd.all)
            DV(lambda e: e.scalar_tensor_tensor(out=Sn[:], in0=Sc[:], scalar=W[:, c * 64 + 63:c * 64 + 64],
                                                in1=PSt[:, cc], op0=ALU.mult, op1=ALU.add),
               Sc.all + W.all + PSt.all, Sn.all)
        if not emit_y:
            return
        bd64, bd1 = cst["bd64"], cst["bd1"]
        AC(lambda e: e.activation(out=t1[:], in_=PY[:], func=AF.Identity), PY.all, t1.all)
        AC(lambda e: e.activation(out=t2[:], in_=PY[:], func=AF.Square), PY.all, t2.all)
        mm(K, PS[0], PS[0][:], bd64[:], t1[:], True, True, r=bd64.all + t1.all)
        mm(K, PS[1], PS[1][:], bd64[:], t2[:], True, True, r=bd64.all + t2.all)
        AC(lambda e: e.activation(out=t2[:], in_=PS[0][:], func=AF.Square), PS[0].all, t2.all)
        DV(lambda e: e.tensor_tensor(out=t2[:], in0=PS[1][:], in1=t2[:], op=ALU.subtract), PS[1].all + t2.all, t2.all)
        DV(lambda e: e.tensor_scalar(out=t2[:], in0=t2[:], scalar1=0.0, scalar2=None, op0=ALU.max), t2.all, t2.all)
        AC(lambda e: e.activation(out=t2[:], in_=t2[:], func=AF.Sqrt, bias=GN_EPS), t2.all, t2.all)
        DV(lambda e: e.reciprocal(out=t2[:], in_=t2[:]), t2.all, t2.all)
        DV(lambda e: e.tensor_tensor(out=t1[:], in0=t1[:], in1=PS[0][:], op=ALU.subtract), t1.all + PS[0].all, t1.all)
        DV(lambda e: e.tensor_tensor(out=t1[:], in0=t1[:], in1=t2[:], op=ALU.mult), t1.all + t2.all, t1.all)
        AC(lambda e: e.activation(out=t1[:], in_=t1[:], func=AF.Identity, scale=col(20), bias=col(22)),
           t1.all + prm.all, t1.all)
        DV(lambda e: e.scalar_tensor_tensor(out=t3[:], in0=p_r[:], scalar=col(18), in1=kf[:], op0=ALU.mult,
                                            op1=ALU.mult), p_r.all + kf.all + prm.all, t3.all)
        mm(K, PS[2], PS[2][:], bd1[:], t3[:], True, True, r=bd1.all + t3.all)
        DV(lambda e: e.tensor_tensor(out=t3[:], in0=PS[2][:], in1=p_v[:], op=ALU.mult), PS[2].all + p_v.all, t3.all)
        DV(lambda e: e.tensor_tensor(out=t1[:], in0=t1[:], in1=t3[:], op=ALU.add), t1.all + t3.all, t1.all)
        yo = self.yo[self.nyo % 2]
        self.nyo += 1
        DV(lambda e: e.tensor_tensor(out=yo[:], in0=t1[:], in1=g[:], op=ALU.mult), t1.all + g.all, yo.all)
        if ydep is None:
            K.dma("sp", ya_ap, yo[:], r=yo.all, sem_dep=yo.d[0], is_output=True)
        else:
            K.scratch_events.append(K.dma("sp", ya_ap, yo[:], r=yo.all, sem_dep=yo.d[0]))


def declare_phase1_io(nc, io):
    for name, (n, dt) in CONST_SHAPES.items():
        io["c_" + name] = declare(nc, "c_" + name, [128, n], dt, "ExternalInput")
    io["prm"] = declare(nc, "prm", [128, 32], F32, "ExternalInput")
    io["w2s"] = declare(nc, "w2s", [64, 256], F32, "ExternalInput")
    io["a2s"] = declare(nc, "a2s", [64, 256], F32, "ExternalInput")
    io["g2s"] = declare(nc, "g2s", [160, 256], F32, "ExternalInput")
    io["xT"] = declare(nc, "xT", [D, S], F32, "ExternalInput")
    io["w1c"] = declare(nc, "w1c", [16, 128, 16, 128], F32, "ExternalInput")


def build_phase1_only(n_tiles=16, do_rwkv=True, do_sb=True):
    nc = bass.Bass("TRN2", target_bir_lowering=False)
    io = {}
    io["cT"] = declare(nc, "cT", [128, 16], F32, "ExternalInput")
    io["b_adaT"] = declare(nc, "b_adaT", [128, 96], F32, "ExternalInput")
    io["norm_gT"] = declare(nc, "norm_gT", [128, 64], F32, "ExternalInput")
    io["w_ada"] = declare(nc, "w_ada", [D, 6 * D], F32, "ExternalInput")
    declare_phase1_io(nc, io)
    io["yaT"] = declare(nc, "yaT", [256, S], BF16, "ExternalOutput")
    io["ybT"] = declare(nc, "ybT", [256, S], BF16, "ExternalOutput")
    K = Ctx(nc)
    with K.es:
        cm = setup_common(K, io, need_f=False)
        phase1(K, io, cm, n_tiles=n_tiles, do_rwkv=do_rwkv, do_sb=do_sb)
        K.finish()
    print("phase1: ninst", K.ninst, "nwaits", K.nwaits, "nsems", K.nsems)
    return nc


def phase1_inputs(inp, b, g, consts, light=False):
    cs = slice(256 * g, 256 * g + 256)
    W = inp["w_in"][0]
    def colsel(c0):
        return W[:, c0:c0 + 1024][:, cs]
    blocks = []
    for base in (0, 1024, 2048):
        w = colsel(base)
        blocks += [w[:, 0:128], w[:, 128:256]]
    def pad(w):
        out = np.zeros((D, 128), np.float32)
        out[:, :w.shape[1]] = w
        return out
    blocks += [pad(W[:, 3072:3136]), pad(W[:, 3136:3200]), W[:, 3200:3328], pad(W[:, 3328:3360])]
    for base in (3360, 4384, 5408):
        w = colsel(base)
        blocks += [w[:, 0:128], w[:, 128:256]]
    w1c = np.stack([np.ascontiguousarray(blk.reshape(16, 128, 128).transpose(1, 0, 2)) for blk in blocks])
    mu = inp["mu_shift"][0]
    prm = np.zeros((128, 32), np.float32)
    def padv(v):
        out = np.zeros(128, np.float32)
        out[:v.shape[0]] = v
        return out
    mus = [mu[0:1024][cs][0:128], mu[0:1024][cs][128:256], mu[1024:2048][cs][0:128], mu[1024:2048][cs][128:256],
           mu[2048:3072][cs][0:128], mu[2048:3072][cs][128:256], padv(mu[3072:3136]), padv(mu[3136:3200]),
           mu[3200:3328], padv(mu[3328:3360])]
    for i, v in enumerate(mus):
        prm[:, i] = v
    for j, key in enumerate(["w0", "a0", "k_k", "k_a"]):
        v = inp[key][0][cs]
        prm[:, 10 + 2 * j] = v[0:128]
        prm[:, 11 + 2 * j] = v[128:256]
    rk = inp["r_k"][0].reshape(-1)[cs]
    prm[:, 18], prm[:, 19] = rk[0:128], rk[128:256]
    for j, key in enumerate(["ln_x_w", "ln_x_b"]):
        v = inp[key][0][cs]
        prm[:, 20 + 2 * j] = v[0:128]
        prm[:, 21 + 2 * j] = v[128:256]
    m = {} if light else common_inputs(inp, b)
    if not light:
        m.update({"c_" + k: v for k, v in consts.items()})
        m["xT"] = np.ascontiguousarray(inp["x"][b].T)
    m["prm"] = prm
    m["w2s"] = np.ascontiguousarray(inp["w2"][0][:, cs])
    m["a2s"] = np.ascontiguousarray(inp["a2"][0][:, cs])
    m["g2s"] = np.ascontiguousarray(inp["g2"][0][:, cs])
    m["w1c"] = w1c
    return m


def build_fused(n_tiles=16, own_from=12, n_pass=2):
    nc = bass.Bass("TRN2", target_bir_lowering=False)
    io = {}
    io["cT"] = declare(nc, "cT", [128, 16], F32, "ExternalInput")
    io["b_adaT"] = declare(nc, "b_adaT", [128, 96], F32, "ExternalInput")
    io["norm_gT"] = declare(nc, "norm_gT", [128, 64], F32, "ExternalInput")
    io["w_ada"] = declare(nc, "w_ada", [D, 6 * D], F32, "ExternalInput")
    for name, (n, dt) in CONST_SHAPES.items():
        io["c_" + name] = declare(nc, "c_" + name, [128, n], dt, "ExternalInput")
    io["prm"] = declare(nc, "prm", [4, 128, 32], F32, "ExternalInput")
    io["w2s"] = declare(nc, "w2s", [4, 64, 256], F32, "ExternalInput")
    io["a2s"] = declare(nc, "a2s", [4, 64, 256], F32, "ExternalInput")
    io["g2s"] = declare(nc, "g2s", [4, 160, 256], F32, "ExternalInput")
    io["xT"] = declare(nc, "xT", [D, S], F32, "ExternalInput")
    io["tokmask"] = declare(nc, "tokmask", [128, S], BF16, "ExternalInput")
    io["w1c"] = declare(nc, "w1c", [4, 16, 128, 16, 128], F32, "ExternalInput")
    io["x2T"] = declare(nc, "x2T", [D, 2048], F32, "ExternalInput")
    io["w_gate"] = declare(nc, "w_gate", [D, 2 * D], F32, "ExternalInput")
    io["w_up_rwkv"] = declare(nc, "w_up_rwkv", [C, D], F32, "ExternalInput")
    io["w_up_sb"] = declare(nc, "w_up_sb", [C, D], F32, "ExternalInput")
    io["w_out"] = declare(nc, "w_out", [D, D], F32, "ExternalInput")
    io["w_mlp_in"] = declare(nc, "w_mlp_in", [D, DFF], F32, "ExternalInput")
    io["w_mlp_out"] = declare(nc, "w_mlp_out", [DFF, D], F32, "ExternalInput")
    io["outT"] = declare(nc, "outT", [D, 2048], F32, "ExternalOutput")
    io["x1s"] = declare(nc, "x1s", [D, 2048], F32, "Internal")
    K = Ctx(nc)
    with K.es:
        cm = setup_common(K, io, need_f=True)
        ysc = K.dram("ysc", [2048, 2048], BF16)
        with K.scope():
            phase1(K, io, cm, n_tiles=n_tiles, groups=(0, 1, 2, 3), own_from=own_from, ysc=ysc)
        for E_ in K.engs.values():
            for ev_ in K.scratch_events:
                E_.wait_for(ev_)
        io["yT"] = ysc.t
        phase2(K, io, cm, n_pass=n_pass)
        K.finish()
    print("fused: ninst", K.ninst, "nwaits", K.nwaits, "nsems", K.nsems)
    return nc


def fused_inputs(inp, b, q, consts, n_tiles=16, own_from=12):
    own = (n_tiles - own_from) * 512
    n_real = (q + 1) * own
    n_seq = n_tiles * 512
    xT = np.zeros((D, S), np.float32)
    xT[:, n_seq - n_real:n_seq] = inp["x"][b, 0:n_real].T
    tokmask = np.zeros((128, S), ml_dtypes.bfloat16)
    tokmask[:, n_seq - n_real:n_seq] = 1.0
    m = common_inputs(inp, b)
    m.update({"c_" + k: v for k, v in consts.items()})
    per_g = [phase1_inputs(inp, b, g, consts, light=True) for g in range(4)]
    for key in ("prm", "w2s", "a2s", "g2s", "w1c"):
        m[key] = np.stack([pg[key] for pg in per_g])
    m["xT"] = xT
    m["tokmask"] = tokmask
    x2T = np.zeros((D, 2048), np.float32)
    x2T[:, 0:own] = inp["x"][b, q * own:(q + 1) * own].T
    m["x2T"] = x2T
    m["w_gate"] = np.ascontiguousarray(inp["w_in"][0][:, 6432:])
    for k in ["w_up_rwkv", "w_up_sb", "w_out", "w_mlp_in", "w_mlp_out"]:
        m[k] = inp[k][0]
    return m


def kernel(**inputs):
    inp = {k: np.asarray(v) for k, v in inputs.items()}
    consts = host_consts()
    nc = build_fused()
    in_maps = [fused_inputs(inp, c // 4, c % 4, consts) for c in range(NCORES)]
    res = run_bass_kernel_spmd(nc, in_maps, core_ids=list(range(NCORES)))
    out = np.zeros((NB, S, D), np.float32)
    for c in range(NCORES):
        b, q = c // 4, c % 4
        out[b, q * 2048:(q + 1) * 2048] = res.results[c]["outT"].T
    return out
```
